# Optimizing a Trainium2 kernel written in Bass

```python
import numpy as np
import jax, jax.numpy as jnp
from jax import lax

D_MODEL = 2048
BATCH = 2
SEQ = 8192
DEPTH = 1

NSA_HEADS = 8
NSA_KV_GROUPS = 2
NSA_HG = NSA_HEADS // NSA_KV_GROUPS
NSA_DK = 128
NSA_WIDTH = NSA_HEADS * NSA_DK
KV_W = NSA_KV_GROUPS * NSA_DK
CMP_BLOCK = 32
CMP_STRIDE = 16
SEL_BLOCK = 64
SEL_TOPN = 16
WINDOW = 512
Q_BLOCK = 128
FORCE = 1e4
NEG = -1e30
RET_HEADS = 4
RET_DK = 256
RET_DV = 256
RET_WIDTH = RET_HEADS * RET_DV
RET_CHUNK = 128
ROPE_BASE = 10000.0
GN_EPS = 1e-6
MIX_WIDTH = NSA_WIDTH + RET_WIDTH
T5_BUCKETS = 32
T5_MAX_DIST = 128
D_FF = 4 * D_MODEL
NORM_EPS = 1e-6
IN_SIZES = [NSA_WIDTH, 6 * KV_W, 3 * NSA_HEADS, RET_HEADS * RET_DK, RET_HEADS * RET_DK, RET_WIDTH, RET_WIDTH]
D_IN = sum(IN_SIZES)
IN_OFFSETS = np.cumsum(IN_SIZES)[:-1].tolist()

kernel_name = "hymba_nsa_retention_sandwich_block"


def rms_norm(x, g):
    xf = x.astype(jnp.float32)
    y = xf * lax.rsqrt(jnp.mean(xf * xf, axis=-1, keepdims=True) + NORM_EPS)
    return (y * g.astype(jnp.float32)).astype(x.dtype)


def t5_bucket(rel):
    n = jnp.maximum(rel, 0)
    max_exact = T5_BUCKETS // 2
    nf = jnp.maximum(n, 1).astype(jnp.float32)
    large = max_exact + (jnp.log(nf / max_exact) / np.log(T5_MAX_DIST / max_exact)
                         * (T5_BUCKETS - max_exact)).astype(jnp.int32)
    large = jnp.minimum(large, T5_BUCKETS - 1)
    return jnp.where(n < max_exact, n, large)


def rope(x, pos):
    d = x.shape[-1]
    inv_freq = ROPE_BASE ** (-jnp.arange(0, d, 2, dtype=jnp.float32) / d)
    ang = pos.astype(jnp.float32)[:, None] * inv_freq[None, :]
    cos = jnp.cos(ang)[None, :, None, :]
    sin = jnp.sin(ang)[None, :, None, :]
    xf = x.astype(jnp.float32)
    x1, x2 = xf[..., : d // 2], xf[..., d // 2:]
    return jnp.concatenate([x1 * cos - x2 * sin, x2 * cos + x1 * sin], axis=-1).astype(x.dtype)


def compress_kv(kv, pe, w1, b1, w2):
    B, S, G, dk = kv.shape
    n_cmp = (S - CMP_BLOCK) // CMP_STRIDE + 1
    idx = jnp.arange(n_cmp)[:, None] * CMP_STRIDE + jnp.arange(CMP_BLOCK)[None, :]
    blocks = kv[:, idx] + pe[None, None, :, None, :]
    flat = blocks.transpose(0, 1, 3, 2, 4).reshape(B, n_cmp, G, CMP_BLOCK * dk)
    hid = jax.nn.gelu(flat @ w1 + b1)
    return hid @ w2


def nsa_attention(q, k_cmp, v_cmp, k_slc, v_slc, k_win, v_win, gates, t5_table):
    B, S, H, dk = q.shape
    G, Hg = NSA_KV_GROUPS, NSA_HG
    n_cmp = k_cmp.shape[1]
    n_sel = S // SEL_BLOCK
    top_n = min(SEL_TOPN, n_sel)
    scale = dk ** -0.5
    cmp_end = jnp.arange(n_cmp) * CMP_STRIDE + CMP_BLOCK - 1
    cs = np.arange(n_cmp) * CMP_STRIDE
    ss = np.arange(n_sel) * SEL_BLOCK
    overlap = (cs[:, None] <= ss[None, :] + SEL_BLOCK - 1) & (cs[:, None] + CMP_BLOCK - 1 >= ss[None, :])
    m_sel = jnp.asarray(overlap, jnp.float32)
    tbl = t5_table.astype(jnp.float32).reshape(T5_BUCKETS, G, Hg)
    ks_blocks = k_slc.reshape(B, n_sel, SEL_BLOCK, G, dk).transpose(0, 3, 1, 2, 4)
    vs_blocks = v_slc.reshape(B, n_sel, SEL_BLOCK, G, dk).transpose(0, 3, 1, 2, 4)
    kw_pad = jnp.pad(k_win, ((0, 0), (WINDOW, 0), (0, 0), (0, 0)))
    vw_pad = jnp.pad(v_win, ((0, 0), (WINDOW, 0), (0, 0), (0, 0)))
    qg = q.reshape(B, S, G, Hg, dk)
    gg = gates.reshape(B, S, G, Hg, 3)
    b_idx = jnp.arange(B)[:, None, None, None]
    g_idx = jnp.arange(G)[None, :, None, None]
    g_idx5 = jnp.arange(G)[None, :, None, None, None]
    jblk = jnp.arange(n_sel)

    def block(c):
        s0 = c * Q_BLOCK
        qc = lax.dynamic_slice_in_dim(qg, s0, Q_BLOCK, axis=1)
        tq = s0 + jnp.arange(Q_BLOCK)
        rel = tq[:, None] - cmp_end[None, :]
        s = jnp.einsum('bqghd,bngd->bghqn', qc, k_cmp).astype(jnp.float32) * scale
        s = s + tbl[t5_bucket(rel)].transpose(2, 3, 0, 1)[None]
        valid = rel >= 0
        p = jax.nn.softmax(jnp.where(valid, s, NEG), axis=-1)
        p = jnp.where(valid, p, 0.0)
        o_cmp = jnp.einsum('bghqn,bngd->bqghd', p.astype(v_cmp.dtype), v_cmp)
        imp = jnp.einsum('bghqn,nj->bgqj', p, m_sel)
        cur = tq // SEL_BLOCK
        blk_valid = jblk[None, :] * SEL_BLOCK <= tq[:, None]
        forced = (jblk[None, :] == 0) | (jblk[None, :] == cur[:, None]) | (jblk[None, :] == cur[:, None] - 1)
        imp = jnp.where(blk_valid, jnp.where(forced, FORCE, imp), -1.0)
        top_val, top_idx = lax.top_k(imp, top_n)
        ksel = ks_blocks[b_idx, g_idx, top_idx]
        vsel = vs_blocks[b_idx, g_idx, top_idx]
        kpos = top_idx[..., None] * SEL_BLOCK + jnp.arange(SEL_BLOCK)
        rel = tq[None, None, :, None, None] - kpos
        s = jnp.einsum('bqghd,bgqnld->bghqnl', qc, ksel).astype(jnp.float32) * scale
        s = s + tbl[t5_bucket(rel), g_idx5].transpose(0, 1, 5, 2, 3, 4)
        valid = ((rel >= 0) & (top_val >= 0)[..., None])[:, :, None]
        s = jnp.where(valid, s, NEG).reshape(B, G, Hg, Q_BLOCK, top_n * SEL_BLOCK)
        p = jax.nn.softmax(s, axis=-1).reshape(B, G, Hg, Q_BLOCK, top_n, SEL_BLOCK)
        o_slc = jnp.einsum('bghqnl,bgqnld->bqghd', p.astype(vsel.dtype), vsel)
        kw = lax.dynamic_slice_in_dim(kw_pad, s0, WINDOW + Q_BLOCK, axis=1)
        vw = lax.dynamic_slice_in_dim(vw_pad, s0, WINDOW + Q_BLOCK, axis=1)
        kpos = s0 - WINDOW + jnp.arange(WINDOW + Q_BLOCK)
        rel = tq[:, None] - kpos[None, :]
        valid = (rel >= 0) & (rel < WINDOW) & (kpos[None, :] >= 0)
        s = jnp.einsum('bqghd,bkgd->bghqk', qc, kw).astype(jnp.float32) * scale
        s = s + tbl[t5_bucket(rel)].transpose(2, 3, 0, 1)[None]
        p = jax.nn.softmax(jnp.where(valid, s, NEG), axis=-1)
        o_win = jnp.einsum('bghqk,bkgd->bqghd', p.astype(vw.dtype), vw)
        gc = lax.dynamic_slice_in_dim(gg, s0, Q_BLOCK, axis=1)
        o = gc[..., 0:1] * o_cmp + gc[..., 1:2] * o_slc + gc[..., 2:3] * o_win
        return o.reshape(B, Q_BLOCK, H * dk)

    out = lax.map(block, jnp.arange(S // Q_BLOCK))
    return out.transpose(1, 0, 2, 3).reshape(B, S, H * dk)


def retention(q, k, v, log_gamma):
    B, S, H, dk = q.shape
    dv = v.shape[-1]
    C = RET_CHUNK
    N = S // C
    k = k * (dk ** -0.5)
    qc = q.reshape(B, N, C, H, dk).transpose(1, 0, 3, 2, 4)
    kc = k.reshape(B, N, C, H, dk).transpose(1, 0, 3, 2, 4)
    vc = v.reshape(B, N, C, H, dv).transpose(1, 0, 3, 2, 4)
    idx = jnp.arange(C, dtype=jnp.float32)
    diff = idx[:, None] - idx[None, :]
    decay = jnp.where(diff[None] >= 0, jnp.exp(jnp.maximum(diff, 0.0)[None] * log_gamma[:, None, None]), 0.0)
    xi = jnp.exp((idx + 1.0)[None, :] * log_gamma[:, None])
    zeta = jnp.exp((C - 1.0 - idx)[None, :] * log_gamma[:, None])
    g_c = jnp.exp(C * log_gamma)
    inner = jnp.einsum('nbhcd,nbhmd->nbhcm', qc, kc).astype(jnp.float32) * decay
    inner = jnp.einsum('nbhcm,nbhme->nbhce', inner, vc.astype(jnp.float32))

    def step(R, xs):
        q_n, k_n, v_n = xs
        cross = jnp.einsum('bhcd,bhde->bhce', q_n.astype(jnp.float32), R) * xi[None, :, :, None]
        R = R * g_c[None, :, None, None] + jnp.einsum(
            'bhcd,bhce->bhde', k_n.astype(jnp.float32) * zeta[None, :, :, None], v_n.astype(jnp.float32))
        return R, cross

    R0 = jnp.zeros((B, H, dk, dv), jnp.float32)
    _, cross = lax.scan(step, R0, (qc, kc, vc))
    o = (inner + cross).transpose(1, 0, 3, 2, 4).reshape(B, S, H, dv)
    mu = jnp.mean(o, axis=-1, keepdims=True)
    var = jnp.mean(jnp.square(o - mu), axis=-1, keepdims=True)
    return (o - mu) * lax.rsqrt(var + GN_EPS)


def setup_inputs(seed: int = 0) -> dict:
    key = jax.random.key(seed)
    ks = jax.random.split(key, 20)
    f32 = jnp.float32

    def nrm(k, shape, scale):
        return jax.random.normal(k, shape, f32) * scale

    def gain(k):
        return 1.0 + nrm(k, (DEPTH, D_MODEL), 0.05)

    LDK = CMP_BLOCK * NSA_DK
    return {
        "x": nrm(ks[0], (BATCH, SEQ, D_MODEL), 1.0),
        "norm_mix_pre": gain(ks[1]),
        "w_in": nrm(ks[2], (DEPTH, D_MODEL, D_IN), D_MODEL ** -0.5),
        "cmp_pe_k": nrm(ks[3], (DEPTH, CMP_BLOCK, NSA_DK), 0.1),
        "cmp_w1_k": nrm(ks[4], (DEPTH, LDK, NSA_DK), LDK ** -0.5),
        "cmp_b1_k": nrm(ks[5], (DEPTH, NSA_DK), 0.02),
        "cmp_w2_k": nrm(ks[6], (DEPTH, NSA_DK, NSA_DK), NSA_DK ** -0.5),
        "cmp_pe_v": nrm(ks[7], (DEPTH, CMP_BLOCK, NSA_DK), 0.1),
        "cmp_w1_v": nrm(ks[8], (DEPTH, LDK, NSA_DK), LDK ** -0.5),
        "cmp_b1_v": nrm(ks[9], (DEPTH, NSA_DK), 0.02),
        "cmp_w2_v": nrm(ks[10], (DEPTH, NSA_DK, NSA_DK), NSA_DK ** -0.5),
        "t5_bias": nrm(ks[11], (T5_BUCKETS, NSA_HEADS), 0.2),
        "w_out": nrm(ks[12], (DEPTH, MIX_WIDTH, D_MODEL), MIX_WIDTH ** -0.5),
        "norm_mix_post": gain(ks[13]),
        "norm_mlp_pre": gain(ks[14]),
        "w_up": nrm(ks[15], (DEPTH, D_MODEL, D_FF), D_MODEL ** -0.5),
        "w_down": nrm(ks[16], (DEPTH, D_FF, D_MODEL), D_FF ** -0.5),
        "norm_mlp_post": gain(ks[17]),
    }


def reference(x, norm_mix_pre, w_in, cmp_pe_k, cmp_w1_k, cmp_b1_k, cmp_w2_k, cmp_pe_v, cmp_w1_v,
              cmp_b1_v, cmp_w2_v, t5_bias, w_out, norm_mix_post, norm_mlp_pre, w_up, w_down,
              norm_mlp_post):
    B, S, _ = x.shape
    pos = jnp.arange(S)
    log_gamma = jnp.log(1.0 - jnp.exp2(-5.0 - jnp.arange(RET_HEADS, dtype=jnp.float32)))
    for l in range(DEPTH):
        h = rms_norm(x, norm_mix_pre[l])
        proj = h @ w_in[l]
        q_a, kv_a, gate_a, q_r, k_r, v_r, g_r = jnp.split(proj, IN_OFFSETS, axis=-1)
        q_a = q_a.reshape(B, S, NSA_HEADS, NSA_DK)
        kv6 = kv_a.reshape(B, S, 6, NSA_KV_GROUPS, NSA_DK)
        k_cmp = compress_kv(kv6[:, :, 0], cmp_pe_k[l], cmp_w1_k[l], cmp_b1_k[l], cmp_w2_k[l])
        v_cmp = compress_kv(kv6[:, :, 1], cmp_pe_v[l], cmp_w1_v[l], cmp_b1_v[l], cmp_w2_v[l])
        gates = jax.nn.sigmoid(gate_a.reshape(B, S, NSA_HEADS, 3))
        o_a = nsa_attention(q_a, k_cmp, v_cmp, kv6[:, :, 2], kv6[:, :, 3], kv6[:, :, 4], kv6[:, :, 5],
                            gates, t5_bias)
        q_r = rope(q_r.reshape(B, S, RET_HEADS, RET_DK), pos)
        k_r = rope(k_r.reshape(B, S, RET_HEADS, RET_DK), pos)
        o_r = retention(q_r, k_r, v_r.reshape(B, S, RET_HEADS, RET_DV), log_gamma)
        o_r = (jax.nn.silu(g_r.astype(jnp.float32)) * o_r.reshape(B, S, RET_WIDTH)).astype(x.dtype)
        mix = jnp.concatenate([o_a.astype(x.dtype), o_r], axis=-1) @ w_out[l]
        x = x + rms_norm(mix, norm_mix_post[l])
        h = rms_norm(x, norm_mlp_pre[l])
        u = jnp.square(jax.nn.relu(h @ w_up[l]))
        x = x + rms_norm(u @ w_down[l], norm_mlp_post[l])
    return x
```

```python
import numpy as np
from contextlib import ExitStack
import concourse.bass as bass
import concourse.mybir as mybir
from concourse.bass_utils import run_bass_kernel_spmd

F32 = mybir.dt.float32
BF16 = mybir.dt.bfloat16
AF = mybir.ActivationFunctionType
ALU = mybir.AluOpType
AX = mybir.AxisListType

NEGM = -30000.0
S = 8192
DM = 2048
NSTEP = 16
DEBUG = False
STOP_AT = None


class _Stop(Exception):
    pass


def ck(k):
    if STOP_AT == k:
        FW.enabled = False
N_RUN_STEPS = 16
RUN_B = True


class Buf:
    __slots__ = ("name", "w", "r", "ds", "ss", "psum")

    def __init__(self, name=""):
        self.name = name
        self.w = None
        self.r = []
        self.ds = None
        self.ss = None
        self.psum = name.startswith("bank")


def _compact(lst):
    best = {}
    for k, v in lst:
        if best.get(k, 0) < v:
            best[k] = v
    return list(best.items())


class FW:
    def __init__(self, nc, es):
        self.nc = nc
        self.es = es
        self.eng = {"pe": nc.tensor, "act": nc.scalar, "dve": nc.vector, "pool": nc.gpsimd, "sp": nc.sync}
        self.sems = {}
        self.cnt = {}
        self.waited = {k: {} for k in self.eng}
        for k in self.eng:
            self.sems[k] = es.enter_context(nc.semaphore("s_" + k))
            self.cnt[k] = 0
        self.n_inst = 0
        self.store_sems = []

    def dsem(self, name):
        key = "d_" + name
        self.sems[key] = self.es.enter_context(self.nc.semaphore(key))
        self.cnt[key] = 0
        return key

    def _wait(self, e, deps):
        need = {}
        for d in deps:
            if d is None:
                continue
            k, v = d
            if k == e and e == "pe":
                continue
            if need.get(k, 0) < v:
                need[k] = v
        for k, v in need.items():
            if self.waited[e].get(k, 0) >= v:
                continue
            self.eng[e].wait_ge(self.sems[k], v)
            self.waited[e][k] = v

    def _deps(self, reads, writes, e=None):
        deps = []
        for b in reads:
            deps.append(b.w)
            if b.psum and e in ("act", "dve"):
                other = "dve" if e == "act" else "act"
                deps.extend(t for t in b.r if t[0] == other)
        for b in writes:
            deps.append(b.w)
            deps.extend(b.r)
        return deps

    def _upd(self, tok, reads, writes):
        for b in writes:
            b.w = tok
            b.r = []
        for b in reads:
            b.r.append(tok)
            if len(b.r) > 32:
                b.r = _compact(b.r)

    enabled = True

    def op(self, e, fn, reads=(), writes=()):
        if not FW.enabled:
            return
        self._wait(e, self._deps(reads, writes, e))
        ins = fn(self.eng[e])
        self.cnt[e] += 1
        ins.then_inc(self.sems[e], 1)
        self._upd((e, self.cnt[e]), reads, writes)
        self.n_inst += 1

    def barrier(self):
        if not FW.enabled:
            return
        deps = [(k, v) for k, v in self.cnt.items() if v > 0]
        for e in self.eng:
            need = {}
            for k, v in deps:
                need[k] = v
            for k, v in need.items():
                if self.waited[e].get(k, 0) >= v:
                    continue
                self.eng[e].wait_ge(self.sems[k], v)
                self.waited[e][k] = v

    def dma(self, q, sem, out, in_, reads=(), writes=(), **kw):
        if not FW.enabled:
            return
        if writes:
            assert len(writes) == 1
            b = writes[0]
            if b.ds is None:
                b.ds = self.dsem("b%d" % len(self.sems))
            sem = b.ds
        else:
            b = reads[0]
            if b.ss is None:
                b.ss = self.dsem("s%d" % len(self.sems))
                self.store_sems.append(b.ss)
            sem = b.ss
        self._wait(q, self._deps(reads, writes))
        ins = self.eng[q].dma_start(out=out, in_=in_, **kw)
        self.cnt[sem] += 16
        ins.then_inc(self.sems[sem], 16)
        self._upd((sem, self.cnt[sem]), reads, writes)
        self.n_inst += 1


def _t5_bucket(rel):
    n = np.maximum(rel, 0)
    nf = np.maximum(n, 1).astype(np.float32)
    large = 16 + (np.log(nf / np.float32(16)) / np.float32(np.log(128 / 16)) * np.float32(16)).astype(np.int32)
    large = np.minimum(large, 31)
    return np.where(n < 16, n, large)


def _host_consts(j, t5):
    qi = np.arange(128)[None, :]
    ki = np.arange(128)[:, None]
    c = {}
    raw = np.zeros((12, 128, 512), np.float32)
    r31 = np.zeros((12, 128, 512), np.float32)
    wm = np.zeros((128, 8, 128), np.float32)
    sm = np.zeros((128, 5, 128), np.float32)
    for r in range(8):
        rel = 128 * (4 + j - r) + qi - ki
        wm[:, r, :] = np.where((rel >= 0) & (rel < 512), 0.0, NEGM)
        if r >= 3:
            sm[:, r - 3, :] = np.where(rel >= 0, 0.0, NEGM)
            bk = _t5_bucket(rel)
            for g in range(2):
                for h in range(4):
                    raw[6 * g + r - 3, :, 128 * h:128 * h + 128] = t5[bk, 4 * g + h]
                    r31[6 * g + r - 3, :, 128 * h:128 * h + 128] = t5[31, 4 * g + h]
    rr = np.arange(64)[:, None]
    relc = 512 + 128 * j + qi - 16 * rr - 15
    bkc = _t5_bucket(relc)
    bcm = np.zeros((128, 2, 512), np.float32)
    for g in range(2):
        for h in range(4):
            raw[6 * g + 5, :64, 128 * h:128 * h + 128] = t5[bkc, 4 * g + h]
            r31[6 * g + 5, :64, 128 * h:128 * h + 128] = t5[31, 4 * g + h]
    for h in range(4):
        m = np.where(relc >= 0, 0.0, NEGM)
        bcm[:64, 0, 128 * h:128 * h + 128] = m
        m0 = m.copy()
        m0[32, :] = NEGM
        bcm[:64, 1, 128 * h:128 * h + 128] = m0
    c["t5raw"] = raw
    c["t531"] = r31
    c["wm"] = wm
    c["sm"] = sm
    c["bcm"] = bcm
    dum = np.zeros((128, 512), np.float32)
    dum[0, :] = NEGM
    c["dum"] = dum
    e32 = np.zeros((128, 2, 16, 128), np.float32)
    for p in range(128):
        jl = p % 64
        a, rem = divmod(jl, 32)
        m, par = rem // 2, rem % 2
        e32[p, a, m, 64 * par:64 * par + 64] = 1.0
    c["e32"] = e32.reshape(128, 4096)
    sh = np.zeros((128, 288), np.float32)
    for r in range(64):
        sh[r, r + 128] = 1.0
    c["sh"] = sh
    c["ident"] = np.eye(128, dtype=np.float32)
    msel = np.zeros((128, 4, 128), np.float32)
    for slot in range(1, 512):
        n = slot - 1
        cs = 16 * n
        for jb in range(128):
            ss = 64 * jb
            if cs <= ss + 63 and cs + 31 >= ss:
                msel[slot % 128, slot // 128, jb] = 1.0
    c["msel"] = msel
    a16 = np.ones((128, 16), np.float32)
    b16 = np.zeros((128, 16), np.float32)
    for q in range(128):
        ucur = 8 + 2 * j + (1 if q >= 64 else 0)
        for u in range(16):
            if u > ucur:
                a16[q, u] = 0.0
                b16[q, u] = -1.0
            elif u == ucur or u == ucur - 1:
                a16[q, u] = 0.0
                b16[q, u] = 1e4
    c["a16"] = a16
    c["b16"] = b16
    gam = 1.0 - 2.0 ** (-5.0 - np.arange(4, dtype=np.float64))
    dec = np.zeros((128, 4, 4, 128), np.float64)
    xi = np.zeros((128, 4, 128), np.float64)
    zeta = np.zeros((128, 4, 4), np.float64)
    for h in range(4):
        for r in range(4):
            d = 128 * (j - r) + qi - ki
            dec[:, h, r, :] = np.where(d >= 0, gam[h] ** np.maximum(d, 0), 0.0)
            zeta[:, h, r] = gam[h] ** (511 - (128 * r + np.arange(128)))
        xi[:, h, :] = gam[h] ** (128 * j + np.arange(128) + 1)[None, :]
    c["dec"] = dec.astype(np.float32)
    c["xi"] = xi.astype(np.float32)
    c["zeta"] = zeta.astype(np.float32).reshape(128, 16)
    return c


def _rope_tables():
    inv = (10000.0 ** (-np.arange(0, 256, 2, dtype=np.float32) / np.float32(256))).astype(np.float32)
    ang = np.arange(S, dtype=np.float32)[None, :] * inv[:, None]
    return np.cos(ang).astype(np.float32), np.sin(ang).astype(np.float32)


G512 = [float((1.0 - 2.0 ** (-5.0 - h)) ** 512) for h in range(4)]

WIN_SLABS = ([("QA%d" % s, "K", 256 * s) for s in range(4)] +
             [("KC", "K", 1024), ("VC", "K", 1280), ("KS", "K", 1536), ("VS", "V", 1792),
              ("KW", "K", 2048), ("VW", "V", 2304)] +
             [("QR%d" % h, "K", 2584 + 256 * h) for h in range(4)] +
             [("KR%d" % h, "K", 3608 + 256 * h) for h in range(4)] +
             [("VR%d" % h, "V", 4632 + 256 * h) for h in range(4)] +
             [("GR%d" % h, "V", 5656 + 256 * h) for h in range(4)])
WIN_IDX = {n: k for k, (n, _, _) in enumerate(WIN_SLABS)}
N_WIN = len(WIN_SLABS)
OFF_WOUT = N_WIN
OFF_WUP = OFF_WOUT + 8
OFF_WDN = OFF_WUP + 32
OFF_W1 = OFF_WDN + 32
N_SLAB = OFF_W1 + 4


def build():
    FW.enabled = True
    nc = bass.Bass("TRN2", target_bir_lowering=False)
    D = {}

    def din(name, shape, dt=F32):
        D[name] = nc.dram_tensor(name, list(shape), dt, kind="ExternalInput").ap()
        return D[name]

    xT = din("xT", [DM, S])
    xTo = din("xTo", [DM, 2048])
    xo = din("xo", [2048, DM])
    w_in = din("w_in", [DM, 6680])
    w_out = din("w_out", [DM, DM])
    w_up = din("w_up", [DM, 8192])
    w_down = din("w_down", [8192, DM])
    din("gpre", [128, 16]); din("gmlp", [128, 16])
    din("gpostB", [128, DM]); din("gpost2B", [128, DM])
    din("w1k", [4096, 128]); din("w1v", [4096, 128])
    din("peTk", [128, 32]); din("peTv", [128, 32])
    din("b1k", [128, 1]); din("b1v", [128, 1])
    din("w2k", [128, 128]); din("w2v", [128, 128])
    din("cosT", [128, S]); din("sinT", [128, S])
    din("cosTo", [128, 2048]); din("sinTo", [128, 2048])
    din("t5raw", [12, 128, 512]); din("t531", [12, 128, 512])
    din("wm", [128, 8, 128]); din("sm", [128, 5, 128]); din("bcm", [128, 2, 512]); din("dum", [128, 512])
    din("e32", [128, 4096]); din("sh", [128, 288]); din("ident", [128, 128])
    din("msel", [128, 4, 128]); din("a16", [128, 16]); din("b16", [128, 16])
    din("dec", [128, 4, 4, 128]); din("xi", [128, 4, 128]); din("zeta", [128, 16])
    out = nc.dram_tensor("out", [2048, DM], F32, kind="ExternalOutput").ap()
    if DEBUG:
        dbg_o = nc.dram_tensor("dbg_o", [2048, DM], F32, kind="ExternalOutput").ap()
        dbg_x1 = nc.dram_tensor("dbg_x1", [2048, DM], F32, kind="ExternalOutput").ap()

    wscr = nc.dram_tensor("wscr", [N_SLAB, 128, 4096], BF16, kind="Internal").ap()
    ks_scr = nc.dram_tensor("ks_scr", [2, NSTEP, 128, 512], BF16, kind="Internal").ap()
    vs_scr = nc.dram_tensor("vs_scr", [2, NSTEP, 128, 4 * 130], BF16, kind="Internal").ap()
    oT_scr = nc.dram_tensor("oT_scr", [NSTEP, 128, 16 * 128], BF16, kind="Internal").ap()

    es = ExitStack()
    with es:
        fw = FW(nc, es)
        try:

            def sb(name, shape, dt, stack=es):
                return stack.enter_context(nc.sbuf_tensor("sb_" + name, list(shape), dt))

            banks = [es.enter_context(nc.psum_tensor("bank%d" % k, [128, 512], F32)) for k in range(8)]
            bbank = [Buf("bank%d" % k) for k in range(8)]
            rot = [4]

            def nb():
                k = rot[0]
                rot[0] = 4 + (rot[0] - 4 + 1) % 4
                return banks[k], bbank[k]

            def mm(o, lhsT, rhs, start, stop, reads, writes):
                fw.op("pe", lambda e: e.matmul(o, lhsT=lhsT, rhs=rhs, start=start, stop=stop), reads, writes)

            def tr(o, in_, reads, writes):
                fw.op("pe", lambda e: e.transpose(o, in_, ident[:]), reads + [bconst], writes)

            d_const = (lambda *a: None)("const")
            d_out = (lambda *a: None)("out")
            d_wst = (lambda *a: None)("wstore")
            bconst = Buf("const")
            bwscr = [Buf("wscrA"), Buf("wscrB")]
            boTs = Buf("oTs")

            ident = sb("ident", [128, 128], BF16)
            ones = sb("ones", [128, 128], BF16)
            gpre = sb("gpre", [128, 16], F32)
            gmlp = sb("gmlp", [128, 16], F32)
            epsT = sb("epsT", [128, 1], F32)
            ca = ExitStack()
            e32 = sb("e32", [128, 2, 2048], BF16, ca)
            shb = sb("shb", [128, 288], BF16, ca)
            td = sb("td", [128, 2, 5, 512], BF16, ca)
            bc = sb("bc", [128, 2, 2, 512], BF16, ca)
            dumb = sb("dumb", [128, 512], BF16, ca)
            wmb = sb("wmb", [128, 8, 128], BF16, ca)
            smb = sb("smb", [128, 5, 128], BF16, ca)
            a16 = sb("a16", [128, 16], F32, ca)
            b16 = sb("b16", [128, 16], F32, ca)
            dec = sb("dec", [128, 4, 4, 128], F32, ca)
            xi = sb("xi", [128, 4, 128], F32, ca)
            zeta = sb("zeta", [128, 16], F32, ca)
            w2k = sb("w2k", [128, 128], BF16, ca)
            w2v = sb("w2v", [128, 128], BF16, ca)
            biask = sb("biask", [128, 1], F32, ca)
            biasv = sb("biasv", [128, 1], F32, ca)
            wg = sb("wg", [128, 16, 24], BF16, ca)
            vcm = sb("vcm", [128, 2, 4, 256], BF16, ca)
            kcT = sb("kcT", [128, 2, 512], BF16, ca)

            with ExitStack() as ps_:
                stA = sb("stA", [128, 16, 256], F32, ps_)
                stB = sb("stB", [128, 16, 256], F32, ps_)
                cvA = sb("cvA", [128, 4096], BF16, ps_)
                cvB = sb("cvB", [128, 4096], BF16, ps_)
                sts = [(stA, Buf("stA"), cvA, Buf("cvA"), None), (stB, Buf("stB"), cvB, Buf("cvB"), None)]
                sm_f = sb("sm_f", [128, 2048], F32, ps_)
                bsm = Buf("sm_f")
                d_sm = (lambda *a: None)("sm")

                def load_cast(dst_ap, src_ap, n, eng="dve"):
                    view = sm_f[:dst_ap.shape[0], 0:n]
                    if len(dst_ap.shape) > 2:
                        pat = {3: "p (a b) -> p a b", 4: "p (a b c) -> p a b c"}[len(dst_ap.shape)]
                        kw = dict(zip("abc", dst_ap.shape[1:]))
                        kw.pop("a")
                        view = view.rearrange(pat, **kw)
                    fw.dma("sp", d_sm, view, src_ap, writes=[bsm])
                    fw.op(eng, lambda e: e.tensor_copy(out=dst_ap, in_=view), reads=[bsm], writes=[bconst])

                load_cast(ident[:], D["ident"], 128)
                load_cast(e32[:, 0, :], D["e32"][:, 0:2048], 2048)
                load_cast(e32[:, 1, :], D["e32"][:, 2048:4096], 2048)
                load_cast(shb[:], D["sh"], 288)
                load_cast(wmb[:], D["wm"], 1024)
                load_cast(dumb[:], D["dum"], 512)
                load_cast(smb[:], D["sm"], 640)
                load_cast(a16[:], D["a16"], 16)
                load_cast(b16[:], D["b16"], 16)
                load_cast(dec[:], D["dec"], 2048)
                load_cast(xi[:], D["xi"], 512)
                load_cast(zeta[:], D["zeta"], 16)
                load_cast(gpre[:], D["gpre"], 16)
                load_cast(gmlp[:], D["gmlp"], 16)
                load_cast(w2k[:], D["w2k"], 128)
                load_cast(w2v[:], D["w2v"], 128)
                bkc = [Buf(), Buf()]
                bvcm = [Buf(), Buf()]
                for g in range(2):
                    load_cast(vcm[:, g, :, 128:256], D["msel"], 512)
                fw.op("dve", lambda e: e.memset(ones[:], 1.0), writes=[bconst])
                fw.op("dve", lambda e: e.memset(epsT[:], 1e-6), writes=[bconst])
                fw.op("dve", lambda e: e.memset(vcm[:, :, :, 0:128], 0.0), reads=[bconst], writes=bvcm)
                fw.op("dve", lambda e: e.memset(kcT[:], 0.0), writes=bkc)
                ck(1)
                t5a = sb("t5a", [128, 512], F32, ps_)
                t5b = sb("t5b", [128, 512], F32, ps_)
                bt5a, bt5b = Buf(), Buf()
                d_t5 = (lambda *a: None)("t5")
                bcm = sb("bcm", [128, 2, 512], F32, ps_)
                bbcm = Buf()
                fw.dma("sp", d_t5, bcm[:], D["bcm"], writes=[bbcm])
                for t in range(12):
                    g, k = divmod(t, 6)
                    fw.dma("sp", d_t5, t5a[:], D["t5raw"][t], writes=[bt5a])
                    fw.dma("sp", d_t5, t5b[:], D["t531"][t], writes=[bt5b])
                    if k < 5:
                        fw.op("dve", lambda e: e.tensor_tensor(out=td[:, g, k, :], in0=t5a[:], in1=t5b[:], op=ALU.subtract),
                              reads=[bt5a, bt5b], writes=[bconst])
                    else:
                        fw.op("dve", lambda e: e.tensor_tensor(out=t5a[:], in0=t5a[:], in1=t5b[:], op=ALU.subtract),
                              reads=[bt5a, bt5b], writes=[bt5a])
                        for v in range(2):
                            fw.op("dve", lambda e: e.tensor_tensor(out=bc[:, g, v, :], in0=t5a[:], in1=bcm[:, v, :], op=ALU.add),
                                  reads=[bt5a, bbcm], writes=[bconst])
                gst = sb("gst", [128, 16, 24], F32, ps_)
                bgst = Buf()
                with nc.allow_non_contiguous_dma(reason="small gate weight slice"):
                    fw.dma("sp", d_t5, gst[:], w_in.rearrange("(c p) n -> p c n", p=128)[:, :, 2560:2584], writes=[bgst])
                fw.op("dve", lambda e: e.tensor_tensor(out=wg[:], in0=gst[:], in1=gpre[:, :, None].broadcast_to([128, 16, 24]), op=ALU.mult),
                      reads=[bgst, bconst], writes=[bconst])

                ck(2)
                slab_jobs = []
                for k, (nm, kind, c0) in enumerate(WIN_SLABS):
                    slab_jobs.append((k, w_in.rearrange("(c p) n -> p c n", p=128)[:, :, c0:c0 + 256], kind, gpre))
                for cg in range(8):
                    slab_jobs.append((OFF_WOUT + cg, w_out.rearrange("(c p) n -> p c n", p=128)[:, :, 256 * cg:256 * cg + 256], "V", None))
                for s in range(32):
                    slab_jobs.append((OFF_WUP + s, w_up.rearrange("(c p) n -> p c n", p=128)[:, :, 256 * s:256 * s + 256], "K", gmlp))
                for cg in range(8):
                    for hq in range(4):
                        slab_jobs.append((OFF_WDN + cg * 4 + hq,
                                          w_down.rearrange("(c p) n -> p c n", p=128)[:, 16 * hq:16 * hq + 16, 256 * cg:256 * cg + 256], "V", None))
                for t, nm in enumerate(["w1k", "w1v"]):
                    for hf in range(2):
                        slab_jobs.append((OFF_W1 + 2 * t + hf, D[nm].rearrange("(l d) h -> d l h", d=128)[:, 16 * hf:16 * hf + 16, :], "W1", None))
                for n, (sid, src, kind, gain) in enumerate(slab_jobs):
                    st, bst, cv, bcv, dsm = sts[n % 2]
                    ce = "dve" if n % 2 == 0 else "pool"
                    if kind == "W1":
                        fw.dma("sp", dsm, st[:, :, 0:128], src, writes=[bst])
                        fw.op(ce, lambda e: e.tensor_copy(out=cv[:, 0:2048].rearrange("p (l h) -> p l h", h=128), in_=st[:, :, 0:128]),
                              reads=[bst], writes=[bcv])
                        fw.dma("act", d_wst, wscr[sid, :, 0:2048], cv[:, 0:2048], reads=[bcv], writes=[bwscr[n % 2]])
                        continue
                    fw.dma("sp", dsm, st[:], src, writes=[bst])
                    if kind == "K":
                        ov = cv[:].rearrange("p (t c n) -> p t c n", t=2, c=16)
                        iv = st[:].rearrange("p c (t n) -> p t c n", t=2)
                        gv = None if gain is None else gain[:, None, :, None].broadcast_to([128, 2, 16, 128])
                    else:
                        ov = cv[:].rearrange("p (c n) -> p c n", c=16)
                        iv = st[:]
                        gv = None if gain is None else gain[:, :, None].broadcast_to([128, 16, 256])
                    if gv is None:
                        fw.op(ce, lambda e: e.tensor_copy(out=ov, in_=iv), reads=[bst], writes=[bcv])
                    else:
                        fw.op(ce, lambda e: e.tensor_tensor(out=ov, in0=iv, in1=gv, op=ALU.mult), reads=[bst, bconst], writes=[bcv])
                    fw.dma("act", d_wst, wscr[sid], cv[:], reads=[bcv], writes=[bwscr[n % 2]])
                ck(3)
                w1st = sb("w1st", [128, 32, 128], BF16, ps_)
                pest = sb("pest", [128, 32], BF16, ps_)
                b1st = sb("b1st", [128, 1], F32, ps_)
                bw1, bpe, bb1 = Buf(), Buf(), Buf()
                d_w1 = (lambda *a: None)("w1p")
                for t, (nm, pnm, bnm, dst) in enumerate([("w1k", "peTk", "b1k", biask), ("w1v", "peTv", "b1v", biasv)]):
                    for hf in range(2):
                        fw.dma("sp", d_w1, w1st[:, 16 * hf:16 * hf + 16, :], wscr[OFF_W1 + 2 * t + hf, :, 0:2048].rearrange("p (l h) -> p l h", h=128),
                               reads=bwscr, writes=[bw1])
                    load_cast(pest[:], D[pnm], 32)
                    fw.dma("sp", d_w1, b1st[:], D[bnm], writes=[bb1])
                    pb, bpb = nb()
                    for l in range(32):
                        mm(pb[:, 0:1], w1st[:, l, :], pest[:, l:l + 1], l == 0, l == 31, [bw1, bconst], [bpb])
                    fw.op("dve", lambda e: e.tensor_tensor(out=dst[:], in0=pb[:, 0:1], in1=b1st[:], op=ALU.add), reads=[bpb, bb1], writes=[bconst])

            ck(4)
            fw.barrier()
            with ExitStack() as pa:
                slabs = [(sb("slab%d" % k, [128, 4096], BF16, pa), Buf(), (lambda *a: None)("slab%d" % k)) for k in range(3)]
                slab_n = [0]

                def load_slab(sid, half=False):
                    t, b, dsm = slabs[slab_n[0] % 3]
                    slab_n[0] += 1
                    if half:
                        fw.dma("sp", dsm, t[:, 0:2048], wscr[sid, :, 0:2048], reads=bwscr, writes=[b])
                    else:
                        fw.dma("sp", dsm, t[:], wscr[sid], reads=bwscr, writes=[b])
                    return t, b

                xst = [(sb("xst%d" % k, [128, 4, 512], F32, pa), Buf(), (lambda *a: None)("xst%d" % k)) for k in range(2)]
                xst_n = [0]
                xb = sb("xb", [128, 16, 512], BF16, pa); bxb = Buf()
                xsq = sb("xsq", [128, 4, 512], BF16, pa); bxsq = Buf()
                xob = sb("xob", [128, 16, 128], BF16, pa); bxob = Buf()
                rstdB = sb("rstdB", [128, 512], F32, pa); brB = Buf()
                rstdT = sb("rstdT", [128, 4], F32, pa); brT = Buf()
                rstdBo = sb("rstdBo", [128, 128], F32, pa); brBo = Buf()
                rstdTo = sb("rstdTo", [128, 1], F32, pa); brTo = Buf()
                cs = sb("cs", [128, 2, 512], F32, pa); bcs = Buf(); d_cs = (lambda *a: None)("cs")
                cso = sb("cso", [128, 2, 128], F32, pa); bcso = Buf()
                kcs = [sb("kcs%d" % t, [128, 2, 528], BF16, pa) for t in range(2)]
                bkcs = [Buf(), Buf()]
                ksf = sb("ksf", [128, 2, 512], BF16, pa); bksf = [Buf(), Buf()]
                vsf = sb("vsf", [128, 2, 4, 130], BF16, pa); bvsf = Buf()
                kwT = sb("kwT", [128, 2, 2, 512], BF16, pa); bkw = [[Buf(), Buf()], [Buf(), Buf()]]
                vw = sb("vw", [128, 2, 2, 4, 130], BF16, pa); bvw = [Buf(), Buf()]
                ksl = [(sb("ksl%d" % k, [128, 512], BF16, pa), sb("vsl%d" % k, [128, 4, 130], BF16, pa), Buf(), (lambda *a: None)("ksl%d" % k)) for k in range(3)]
                ksl_n = [0]
                d_kst = [[(lambda *a: None)("kst%d%d" % (g_, p_)) for p_ in range(2)] for g_ in range(2)]
                bksc = [Buf(), Buf()]
                bvsc = [Buf(), Buf()]
                qaT = sb("qaT", [128, 8, 128], BF16, pa); bqa = Buf()
                gat = sb("gat", [128, 24], F32, pa); bgat = Buf()
                hidg = sb("hidg", [128, 2, 2, 128], BF16, pa); bhid = Buf()
                gtmp = sb("gtmp", [128, 3, 32], F32, pa); bgt = Buf()
                put = [(sb("put%d" % k, [128, 512], BF16, pa), Buf()) for k in range(3)]
                put_n = [0]
                Of = sb("Of", [128, 1024], F32, pa); bOf = Buf()
                Ob = sb("Ob", [128, 2048], BF16, pa); bOb = Buf()
                oT = sb("oT", [128, 16, 128], BF16, pa); boT = Buf()
                d_oT = (lambda *a: None)("oT")
                impf = sb("impf", [128, 128], F32, pa); bimp = Buf()
                imp1 = sb("imp1", [128, 128], F32, pa); bimp1 = Buf()
                imp2 = sb("imp2", [128, 128], F32, pa); bimp2 = Buf()
                top8 = sb("top8", [128, 16], F32, pa); btop = Buf()
                nsel = sb("nsel", [128, 128], BF16, pa); bnsel = Buf()
                NS = sb("NS", [128, 2, 512], BF16, pa); bNS = [Buf(), Buf()]
                small = sb("small", [128, 64], F32, pa); bsmall = Buf()
                R = sb("R", [128, 4, 2, 256], F32, pa); bR = [Buf() for _ in range(4)]
                Rb = sb("Rb", [128, 4, 2, 256], BF16, pa); bRb = [Buf() for _ in range(4)]
                krT = sb("krT", [128, 2, 512], BF16, pa); bkr = Buf()
                rtmp = sb("rtmp", [128, 4, 512], F32, pa); brt = Buf()
                kz = sb("kz", [128, 4, 2, 128], BF16, pa); bkz = Buf()
                vr = sb("vr", [128, 4, 256], BF16, pa); bvr = Buf()
                qrT = sb("qrT", [128, 2, 128], BF16, pa); bqr = Buf()
                qrX = sb("qrX", [128, 2, 128], BF16, pa); bqrx = Buf()
                sdT = sb("sdT", [128, 4, 128], BF16, pa); bsd = Buf()
                sg = sb("sg", [128, 256], F32, pa); bsg = Buf()
                gn = sb("gn", [128, 256], F32, pa); bgn = Buf()
                dbgt = sb("dbgt", [128, 2048], F32, pa) if DEBUG else None
                bdbg = Buf()

                fw.op("dve", lambda e: e.memset(R[:], 0.0), writes=bR)
                fw.op("dve", lambda e: e.memset(Rb[:], 0.0), writes=bRb)
                fw.op("dve", lambda e: e.memset(vsf[:, :, :, 128:130], 1.0), writes=[bvsf])
                fw.op("dve", lambda e: e.memset(vw[:, :, :, :, 128:130], 1.0), writes=bvw)
                for t in range(2):
                    fw.op("dve", lambda e: e.memset(kcs[t][:], 0.0), writes=[bkcs[t]])
                fw.op("dve", lambda e: e.memset(hidg[:], 0.0), writes=[bhid])

                ck(5)
                xT_v = xT.rearrange("(c p) t -> p c t", p=128)
                xTo_v = xTo.rearrange("(c p) t -> p c t", p=128)

                def rstd_from(ps_ap, dst_ap, breads, bw):
                    fw.op("act", lambda e: e.activation(out=dst_ap, in_=ps_ap, func=AF.Sqrt, bias=epsT[:dst_ap.shape[0], :], scale=1.0 / DM),
                          reads=breads + [bconst], writes=[bw])
                    fw.op("dve", lambda e: e.reciprocal(out=dst_ap, in_=dst_ap), reads=[bw], writes=[bw])

                def proj_K(slab, ct, rhs_fn, n, breads):
                    t, b = slab
                    pb, bpb = nb()
                    tv = t[:].rearrange("p (t c n) -> p t c n", t=2, c=16)
                    for c in range(16):
                        mm(pb[:, 0:n], tv[:, ct, c, :], rhs_fn(c), c == 0, c == 15, [b] + breads, [bpb])
                    return pb, bpb

                def proj_V(slab, lhs_fn, breads, ncols=256):
                    t, b = slab
                    pb, bpb = nb()
                    tv = t[:].rearrange("p (c n) -> p c n", c=16)
                    for c in range(16):
                        mm(pb[:, 0:ncols], lhs_fn(c), tv[:, c, 0:ncols], c == 0, c == 15, [b] + breads, [bpb])
                    return pb, bpb

                def rope(pb0, bp0, pb1, bp1, rB, brBx, tab, btab, n, dst, bdst, pre):
                    a1, a2, t1, t2 = rtmp[:, 0, 0:n], rtmp[:, 1, 0:n], rtmp[:, 2, 0:n], rtmp[:, 3, 0:n]
                    fw.op("dve", lambda e: e.scalar_tensor_tensor(out=a1, in0=pb0[:, 0:n], scalar=pre, in1=rB, op0=ALU.mult, op1=ALU.mult),
                          reads=[bp0, brBx], writes=[brt])
                    fw.op("dve", lambda e: e.scalar_tensor_tensor(out=a2, in0=pb1[:, 0:n], scalar=pre, in1=rB, op0=ALU.mult, op1=ALU.mult),
                          reads=[bp1, brBx], writes=[brt])
                    fw.op("dve", lambda e: e.tensor_tensor(out=t1, in0=a1, in1=tab[:, 0, 0:n], op=ALU.mult), reads=[brt, btab], writes=[brt])
                    fw.op("pool", lambda e: e.tensor_tensor(out=t2, in0=a2, in1=tab[:, 1, 0:n], op=ALU.mult), reads=[brt, btab], writes=[brt])
                    fw.op("dve", lambda e: e.tensor_tensor(out=dst[:, 0, 0:n], in0=t1, in1=t2, op=ALU.subtract), reads=[brt], writes=[bdst])
                    fw.op("dve", lambda e: e.tensor_tensor(out=t1, in0=a2, in1=tab[:, 0, 0:n], op=ALU.mult), reads=[brt, btab], writes=[brt])
                    fw.op("pool", lambda e: e.tensor_tensor(out=t2, in0=a1, in1=tab[:, 1, 0:n], op=ALU.mult), reads=[brt, btab], writes=[brt])
                    fw.op("dve", lambda e: e.tensor_tensor(out=dst[:, 1, 0:n], in0=t1, in1=t2, op=ALU.add), reads=[brt], writes=[bdst])

                for i in range(N_RUN_STEPS):
                    slot = i % 2
                    psB, bpsB = nb()
                    psT, bpsT = nb()
                    for qtr in range(4):
                        st, bst, dsm = xst[xst_n[0] % 2]; xst_n[0] += 1
                        fw.dma("sp", dsm, st[:], xT_v[:, 4 * qtr:4 * qtr + 4, 512 * i:512 * i + 512], writes=[bst])
                        fw.op("dve", lambda e: e.tensor_copy(out=xb[:, 4 * qtr:4 * qtr + 4, :], in_=st[:]), reads=[bst], writes=[bxb])
                        fw.op("act", lambda e: e.activation(out=xsq[:], in_=st[:], func=AF.Square), reads=[bst], writes=[bxsq])
                        for c in range(4):
                            first, last = (qtr == 0 and c == 0), (qtr == 3 and c == 3)
                            mm(psB[:, :], ones[:], xsq[:, c, :], first, last, [bconst, bxsq], [bpsB])
                        for r in range(4):
                            for c in range(4):
                                first, last = (qtr == 0 and c == 0 and r == 0), (qtr == 3 and c == 3 and r == 3)
                                mm(psT[:, r:r + 1], xsq[:, c, 128 * r:128 * r + 128], ones[:, 0:1], first, last, [bconst, bxsq], [bpsT])
                    rstd_from(psB[:, :], rstdB[:], [bpsB], brB)
                    rstd_from(psT[:, 0:4], rstdT[:], [bpsT], brT)
                    psB, bpsB = nb()
                    for qtr in range(4):
                        st, bst, dsm = xst[xst_n[0] % 2]; xst_n[0] += 1
                        stv = st[:].rearrange("p a (b t) -> p (a b) t", t=128)[:, 0:4, :]
                        fw.dma("sp", dsm, stv, xTo_v[:, 4 * qtr:4 * qtr + 4, 128 * i:128 * i + 128], writes=[bst])
                        fw.op("dve", lambda e: e.tensor_copy(out=xob[:, 4 * qtr:4 * qtr + 4, :], in_=stv), reads=[bst], writes=[bxob])
                        fw.op("act", lambda e: e.activation(out=xsq[:, 0, :].rearrange("p (a t) -> p a t", t=128), in_=stv, func=AF.Square),
                              reads=[bst], writes=[bxsq])
                        for c in range(4):
                            first, last = (qtr == 0 and c == 0), (qtr == 3 and c == 3)
                            mm(psB[:, 0:128], ones[:], xsq[:, 0, 128 * c:128 * c + 128], first, False, [bconst, bxsq], [bpsB])
                        for c in range(4):
                            mm(psB[:, 256:257], xsq[:, 0, 128 * c:128 * c + 128], ones[:, 0:1], False, (qtr == 3 and c == 3), [bconst, bxsq], [bpsB])
                    rstd_from(psB[:, 0:128], rstdBo[:], [bpsB], brBo)
                    rstd_from(psB[:, 256:257], rstdTo[:], [bpsB], brTo)
                    ck(6)
                    fw.dma("sp", d_cs, cs[:, 0, :], D["cosT"][:, 512 * i:512 * i + 512], writes=[bcs])
                    fw.dma("sp", d_cs, cs[:, 1, :], D["sinT"][:, 512 * i:512 * i + 512], writes=[bcs])
                    fw.dma("sp", d_cs, cso[:, 0, :], D["cosTo"][:, 128 * i:128 * i + 128], writes=[bcso])
                    fw.dma("sp", d_cs, cso[:, 1, :], D["sinTo"][:, 128 * i:128 * i + 128], writes=[bcso])

                    xrhs = lambda c: xb[:, c, :]
                    xorhs = lambda c: xob[:, c, :]

                    for nm, t in (("KC", 0), ("VC", 1)):
                        sl = load_slab(WIN_IDX[nm])
                        for g in range(2):
                            pb, bpb = proj_K(sl, g, xrhs, 512, [bxb])
                            fw.op("dve", lambda e: e.tensor_tensor(out=kcs[t][:, g, 16:528], in0=pb[:, :], in1=rstdB[:], op=ALU.mult),
                                  reads=[bpb, brB], writes=[bkcs[t]])
                    sl = load_slab(WIN_IDX["KS"])
                    for g in range(2):
                        pb, bpb = proj_K(sl, g, xrhs, 512, [bxb])
                        fw.op("dve", lambda e: e.tensor_tensor(out=ksf[:, g, :], in0=pb[:, :], in1=rstdB[:], op=ALU.mult),
                              reads=[bpb, brB], writes=[bksf[g]])
                        fw.dma("pool", None, ks_scr[g, i], ksf[:, g, :], reads=[bksf[g]], writes=[bksc[g]])
                    sl = load_slab(WIN_IDX["KW"])
                    for g in range(2):
                        pb, bpb = proj_K(sl, g, xrhs, 512, [bxb])
                        fw.op("dve", lambda e: e.tensor_tensor(out=kwT[:, g, slot, :], in0=pb[:, :], in1=rstdB[:], op=ALU.mult),
                              reads=[bpb, brB], writes=[bkw[g][slot]])
                    sl = load_slab(WIN_IDX["VS"])
                    for r in range(4):
                        pb, bpb = proj_V(sl, lambda c: xb[:, c, 128 * r:128 * r + 128], [bxb])
                        fw.op("act", lambda e: e.activation(out=vsf[:, :, r, 0:128], in_=pb[:, 0:256].rearrange("p (g d) -> p g d", g=2),
                                                            func=AF.Copy, scale=rstdT[:, r:r + 1]), reads=[bpb, brT], writes=[bvsf])
                    for g in range(2):
                        fw.dma("pool", None, vs_scr[g, i], vsf[:, g, :, :].rearrange("p r d -> p (r d)"), reads=[bvsf], writes=[bvsc[g]])
                    sl = load_slab(WIN_IDX["VW"])
                    for r in range(4):
                        pb, bpb = proj_V(sl, lambda c: xb[:, c, 128 * r:128 * r + 128], [bxb])
                        fw.op("act", lambda e: e.activation(out=vw[:, slot, :, r, 0:128], in_=pb[:, 0:256].rearrange("p (g d) -> p g d", g=2),
                                                            func=AF.Copy, scale=rstdT[:, r:r + 1]), reads=[bpb, brT], writes=[bvw[slot]])

                    ck(7)
                    qd = 32 * (i % 4)
                    chn = i // 4
                    for t in range(2):
                        w1lo = load_slab(OFF_W1 + 2 * t, half=True)
                        w1hi = load_slab(OFF_W1 + 2 * t + 1, half=True)
                        for g in range(2):
                            pb, bpb = nb()
                            for l in range(32):
                                wt, bwt = (w1lo if l < 16 else w1hi)
                                lv = wt[:, 0:2048].rearrange("p (l h) -> p l h", h=128)[:, l % 16, :]
                                mm(pb[:, 0:32], lv, kcs[t][:, g, l:l + 497:16], l == 0, l == 31, [bwt, bkcs[t]], [bpb])
                            bia = biask if t == 0 else biasv
                            xg, x2, x3 = gtmp[:, 0, :], gtmp[:, 1, :], gtmp[:, 2, :]
                            fw.op("act", lambda e: e.activation(out=xg, in_=pb[:, 0:32], func=AF.Identity, bias=bia[:], scale=1.0),
                                  reads=[bpb, bconst], writes=[bgt])
                            fw.op("dve", lambda e: e.tensor_tensor(out=x2, in0=xg, in1=xg, op=ALU.mult), reads=[bgt], writes=[bgt])
                            fw.op("dve", lambda e: e.tensor_scalar(out=x2, in0=x2, scalar1=0.044715, scalar2=1.0, op0=ALU.mult, op1=ALU.add),
                                  reads=[bgt], writes=[bgt])
                            fw.op("dve", lambda e: e.tensor_tensor(out=x3, in0=x2, in1=xg, op=ALU.mult), reads=[bgt], writes=[bgt])
                            fw.op("act", lambda e: e.activation(out=x3, in_=x3, func=AF.Sigmoid, scale=1.5957691216057308), reads=[bgt], writes=[bgt])
                            fw.op("dve", lambda e: e.tensor_tensor(out=hidg[:, t, g, qd:qd + 32], in0=x3, in1=xg, op=ALU.mult), reads=[bgt], writes=[bhid])
                            pb, bpb = nb()
                            if t == 0:
                                mm(pb[:, 0:32], w2k[:], hidg[:, 0, g, qd:qd + 32], True, True, [bconst, bhid], [bpb])
                                fw.op("dve", lambda e: e.tensor_copy(out=kcT[:, g, 32 * i:32 * i + 32], in_=pb[:, 0:32]), reads=[bpb], writes=[bkc[g]])
                            else:
                                mm(pb[:, 0:128], hidg[:, 1, g, :], w2v[:], True, True, [bconst, bhid], [bpb])
                                fw.op("dve", lambda e: e.tensor_copy(out=vcm[qd:qd + 32, g, chn, 0:128], in_=pb[qd:qd + 32, 0:128]), reads=[bpb], writes=[bvcm[g]])
                        fw.op("dve", lambda e: e.tensor_copy(out=kcs[t][:, :, 0:16], in_=kcs[t][:, :, 512:528]), reads=[bkcs[t]], writes=[bkcs[t]])

                    ck(8)
                    for s in range(4):
                        sl = load_slab(WIN_IDX["QA%d" % s])
                        for ct in range(2):
                            pb, bpb = proj_K(sl, ct, xorhs, 128, [bxob])
                            fw.op("dve", lambda e: e.scalar_tensor_tensor(out=qaT[:, 2 * s + ct, :], in0=pb[:, 0:128], scalar=float(128 ** -0.5),
                                                                          in1=rstdBo[:], op0=ALU.mult, op1=ALU.mult), reads=[bpb, brBo], writes=[bqa])
                    pb, bpb = nb()
                    for c in range(16):
                        mm(pb[:, 0:24], xob[:, c, :], wg[:, c, :], c == 0, c == 15, [bxob, bconst], [bpb])
                    fw.op("act", lambda e: e.activation(out=gat[:], in_=pb[:, 0:24], func=AF.Sigmoid, scale=rstdTo[:, 0:1]), reads=[bpb, brTo], writes=[bgat])

                    ck(9)
                    for g in range(2):
                        qg = qaT[:, 4 * g:4 * g + 4, :].rearrange("p h q -> p (h q)")
                        accs = [(banks[0], bbank[0]), (banks[1], bbank[1])]
                        lb, blb = banks[2], bbank[2]
                        nch = i // 4 + 1
                        for m in range(nch):
                            npop = 128 if m < nch - 1 else 32 * (i % 4 + 1)
                            pb, bpb = nb()
                            biases = []
                            if i == 0:
                                biases.append((shb[32:64, 160:288], bc[32:64, g, 1, :]))
                            else:
                                cha, chb = (i - 1) // 4, i // 4
                                s0 = 32 * ((i - 1) % 4)
                                if cha == chb:
                                    if m == cha:
                                        biases.append((shb[0:64, 128 - s0:256 - s0], bc[0:64, g, 0, :]))
                                else:
                                    if m == cha:
                                        biases.append((shb[0:32, 32:160], bc[0:32, g, 0, :]))
                                    if m == chb:
                                        biases.append((shb[32:64, 160:288], bc[32:64, g, 0, :]))
                            if i >= 1 and m == 0:
                                biases.append((shb[0:32, 128:256], dumb[0:32, :]))
                            mm(pb[:, :], kcT[:, g, 128 * m:128 * m + 128], qg, True, len(biases) == 0, [bkc[g], bqa], [bpb])
                            for bi, (l_, r_) in enumerate(biases):
                                mm(pb[:, :], l_, r_, False, bi == len(biases) - 1, [bconst], [bpb])
                            pt, bpt = put[put_n[0] % 3]; put_n[0] += 1
                            fw.op("act", lambda e: e.activation(out=pt[0:npop, :], in_=pb[0:npop, :], func=AF.Exp), reads=[bpb], writes=[bpt])
                            for h in range(4):
                                ab, bab = accs[h // 2]
                                mm(ab[:, 256 * (h % 2):256 * (h % 2) + 256], pt[0:npop, 128 * h:128 * h + 128], vcm[0:npop, g, m, :],
                                   m == 0 and h % 2 == 0, m == nch - 1 and h % 2 == 1, [bpt, bvcm[g]], [bab])
                                mm(lb[:, h:h + 1], pt[0:npop, 128 * h:128 * h + 128], ones[0:npop, 0:1], m == 0 and h == 0, m == nch - 1 and h == 3, [bpt, bconst], [blb])
                        ck(91)
                        rl = small[:, 0:4]
                        fw.op("dve", lambda e: e.tensor_scalar(out=rl, in0=lb[:, 0:4], scalar1=1e-30, scalar2=None, op0=ALU.max), reads=[blb], writes=[bsmall])
                        fw.op("dve", lambda e: e.reciprocal(out=rl, in_=rl), reads=[bsmall], writes=[bsmall])
                        cf = small[:, 4:8]
                        fw.op("dve", lambda e: e.tensor_tensor(out=cf, in0=rl, in1=gat[:, 12 * g:12 * g + 12:3], op=ALU.mult), reads=[bsmall, bgat], writes=[bsmall])
                        for h in range(4):
                            ab, bab = accs[h // 2]
                            oc = 256 * (h % 2)
                            fw.op("act", lambda e: e.activation(out=Of[:, 512 * g + 128 * h:512 * g + 128 * h + 128], in_=ab[:, oc:oc + 128],
                                                                func=AF.Copy, scale=small[:, 4 + h:5 + h]), reads=[bab, bsmall], writes=[bOf])
                            if h == 0:
                                fw.op("dve", lambda e: e.tensor_scalar(out=impf[:], in0=ab[:, oc + 128:oc + 256], scalar1=small[:, h:h + 1], scalar2=None,
                                                                       op0=ALU.mult), reads=[bab, bsmall], writes=[bimp])
                            else:
                                fw.op("dve", lambda e: e.scalar_tensor_tensor(out=impf[:], in0=ab[:, oc + 128:oc + 256], scalar=small[:, h:h + 1],
                                                                              in1=impf[:], op0=ALU.mult, op1=ALU.add), reads=[bab, bsmall, bimp], writes=[bimp])
                        ck(92)
                        w0 = max(0, 8 * i - 8)
                        w1 = 8 * i + 8
                        u0 = w0 - (8 * i - 8)
                        if w1 < 128:
                            fw.op("pool", lambda e: e.memset(imp1[:, w1:128], -1.0), writes=[bimp1])
                        if w0 > 0:
                            fw.op("dve", lambda e: e.tensor_copy(out=imp1[:, 0:w0], in_=impf[:, 0:w0]), reads=[bimp], writes=[bimp1])
                        fw.op("dve", lambda e: e.tensor_tensor(out=imp1[:, w0:w1], in0=impf[:, w0:w1], in1=a16[:, u0:16], op=ALU.mult),
                              reads=[bimp, bconst], writes=[bimp1])
                        fw.op("dve", lambda e: e.tensor_tensor(out=imp1[:, w0:w1], in0=imp1[:, w0:w1], in1=b16[:, u0:16], op=ALU.add),
                              reads=[bconst, bimp1], writes=[bimp1])
                        fw.op("dve", lambda e: e.memset(imp1[:, 0:1], 1e4), writes=[bimp1])
                        fw.op("dve", lambda e: e.max(out=top8[:, 0:8], in_=imp1[:]), reads=[bimp1], writes=[btop])
                        fw.op("dve", lambda e: e.match_replace(out=imp2[:], in_to_replace=top8[:, 0:8], in_values=imp1[:], imm_value=-2.0),
                              reads=[bimp1, btop], writes=[bimp2])
                        fw.op("dve", lambda e: e.max(out=top8[:, 8:16], in_=imp2[:]), reads=[bimp2], writes=[btop])
                        fw.op("dve", lambda e: e.tensor_scalar(out=small[:, 8:9], in0=top8[:, 15:16], scalar1=0.0, scalar2=None, op0=ALU.max),
                              reads=[btop], writes=[bsmall])
                        fw.op("dve", lambda e: e.tensor_scalar(out=nsel[:], in0=imp1[:], scalar1=small[:, 8:9], scalar2=NEGM, op0=ALU.is_lt, op1=ALU.mult),
                              reads=[bimp1, bsmall], writes=[bnsel])
                        ck(93)
                        pb, bpb = nb()
                        pbv = pb[:, 0:64].bitcast(BF16)
                        tr(pbv, nsel[:], [bnsel], [bpb])
                        fw.op("dve", lambda e: e.tensor_copy(out=NS[:, g, :].rearrange("p (h q) -> p h q", h=4),
                                                             in_=pbv[:, None, :].broadcast_to([128, 4, 128])), reads=[bpb], writes=[bNS[g]])
                        ck(94)
                        for br in ("slc", "win"):
                            accs = [(banks[0], bbank[0]), (banks[1], bbank[1])] if br == "slc" else [(banks[2], bbank[2]), (banks[3], bbank[3])]
                            if br == "slc":
                                tiles = list(range(4 * i + 4))
                            else:
                                tiles = [r for r in range(8) if (i > 0 or r >= 4)]
                            cur_chunk = [None, None, None]
                            for ti, kt in enumerate(tiles):
                                if br == "slc":
                                    r = kt - 4 * (i - 1)
                                    ch, rl_ = divmod(kt, 4)
                                    if ch == i:
                                        kT_ap, v_ap, bkv = ksf[:, g, 128 * rl_:128 * rl_ + 128], vsf[:, g, rl_, 0:129], [bksf[g], bvsf]
                                    else:
                                        if cur_chunk[0] != ch:
                                            kt_, vt_, bk_, dsm = ksl[ksl_n[0] % 3]; ksl_n[0] += 1
                                            fw.dma("sp", dsm, kt_[:], ks_scr[g, ch], reads=[bksc[g]], writes=[bk_])
                                            fw.dma("sp", dsm, vt_[:].rearrange("p r d -> p (r d)"), vs_scr[g, ch], reads=[bvsc[g]], writes=[bk_])
                                            cur_chunk = [ch, (kt_, vt_), bk_]
                                        kt_, vt_ = cur_chunk[1]
                                        kT_ap, v_ap, bkv = kt_[:, 128 * rl_:128 * rl_ + 128], vt_[:, rl_, 0:129], [cur_chunk[2]]
                                else:
                                    r = kt
                                    sl_ = (i - 1) % 2 if r < 4 else i % 2
                                    rl_ = r % 4
                                    kT_ap, v_ap = kwT[:, g, sl_, 128 * rl_:128 * rl_ + 128], vw[:, sl_, g, rl_, 0:129]
                                    bkv = [bkw[g][sl_], bvw[sl_]]
                                pb, bpb = nb()
                                extra = []
                                if br == "slc":
                                    a_, m_ = divmod(kt, 16)
                                    hb_, ap_ = divmod(a_, 2)
                                    extra.append((e32[64 * hb_:64 * hb_ + 64, ap_, 128 * m_:128 * m_ + 128], NS[64 * hb_:64 * hb_ + 64, g, :], [bNS[g]]))
                                    if 3 <= r <= 7:
                                        extra.append((ident[:], smb[:, r - 3, None, :].broadcast_to([128, 4, 128]), []))
                                else:
                                    extra.append((ident[:], wmb[:, r, None, :].broadcast_to([128, 4, 128]), []))
                                if 3 <= r <= 7:
                                    extra.append((ident[:], td[:, g, r - 3, :], []))
                                mm(pb[:, :], kT_ap, qg, True, False, bkv + [bqa], [bpb])
                                for bi, (l_, r_, br_) in enumerate(extra):
                                    ro = pb[:, :] if len(r_.shape) == 2 else pb[:, :].rearrange("p (h q) -> p h q", h=4)
                                    mm(ro, l_, r_, False, bi == len(extra) - 1, [bconst] + br_, [bpb])
                                pt, bpt = put[put_n[0] % 3]; put_n[0] += 1
                                fw.op("act", lambda e: e.activation(out=pt[:], in_=pb[:, :], func=AF.Exp), reads=[bpb], writes=[bpt])
                                for h in range(4):
                                    ab, bab = accs[h // 2]
                                    oc = 129 * (h % 2)
                                    mm(ab[:, oc:oc + 129], pt[:, 128 * h:128 * h + 128], v_ap, ti == 0 and h % 2 == 0, ti == len(tiles) - 1 and h % 2 == 1, [bpt] + bkv, [bab])
                            gi = 1 if br == "slc" else 2
                            lall = small[:, 16:20]
                            for h in range(4):
                                ab, bab = accs[h // 2]
                                oc = 129 * (h % 2)
                                fw.op("dve", lambda e: e.reciprocal(out=small[:, 16 + h:17 + h], in_=ab[:, oc + 128:oc + 129]), reads=[bab], writes=[bsmall])
                            fw.op("dve", lambda e: e.tensor_tensor(out=small[:, 20:24], in0=lall, in1=gat[:, 12 * g + gi:12 * g + 12:3], op=ALU.mult),
                                  reads=[bsmall, bgat], writes=[bsmall])
                            for h in range(4):
                                ab, bab = accs[h // 2]
                                oc = 129 * (h % 2)
                                dst = Of[:, 512 * g + 128 * h:512 * g + 128 * h + 128]
                                fw.op("dve", lambda e: e.scalar_tensor_tensor(out=dst, in0=ab[:, oc:oc + 128], scalar=small[:, 20 + h:21 + h], in1=dst,
                                                                              op0=ALU.mult, op1=ALU.add), reads=[bab, bsmall, bOf], writes=[bOf])
                    fw.op("act", lambda e: e.copy(out=Ob[:, 0:1024], in_=Of[:]), reads=[bOf], writes=[bOb])

                    ck(10)
                    for h in range(4):
                        slk = load_slab(WIN_IDX["KR%d" % h])
                        p0, b0 = proj_K(slk, 0, xrhs, 512, [bxb])
                        p1, b1 = proj_K(slk, 1, xrhs, 512, [bxb])
                        rope(p0, b0, p1, b1, rstdB[:], brB, cs, bcs, 512, krT, bkr, 1.0 / 16.0)
                        for r in range(4):
                            pb, bpb = nb()
                            pbv = pb[:, 0:128].bitcast(BF16)
                            for e_ in range(2):
                                tr(pbv[:, 128 * e_:128 * e_ + 128], krT[:, e_, 128 * r:128 * r + 128], [bkr], [bpb])
                            fw.op("act", lambda e: e.activation(out=kz[:, r, :, :], in_=pbv.rearrange("p (e d) -> p e d", e=2), func=AF.Copy,
                                                                scale=zeta[:, 4 * h + r:4 * h + r + 1]), reads=[bpb, bconst], writes=[bkz])
                        slv = load_slab(WIN_IDX["VR%d" % h])
                        for r in range(4):
                            pb, bpb = proj_V(slv, lambda c: xb[:, c, 128 * r:128 * r + 128], [bxb])
                            fw.op("act", lambda e: e.activation(out=vr[:, r, :], in_=pb[:, 0:256], func=AF.Copy, scale=rstdT[:, r:r + 1]),
                                  reads=[bpb, brT], writes=[bvr])
                        slq = load_slab(WIN_IDX["QR%d" % h])
                        p0, b0 = proj_K(slq, 0, xorhs, 128, [bxob])
                        p1, b1 = proj_K(slq, 1, xorhs, 128, [bxob])
                        rope(p0, b0, p1, b1, rstdBo[:], brBo, cso, bcso, 128, qrT, bqr, 1.0)
                        fw.op("dve", lambda e: e.tensor_tensor(out=qrX[:], in0=qrT[:], in1=xi[:, h, None, :].broadcast_to([128, 2, 128]), op=ALU.mult),
                              reads=[bqr, bconst], writes=[bqrx])
                        slg = load_slab(WIN_IDX["GR%d" % h])
                        pb, bpb = proj_V(slg, lambda c: xob[:, c, :], [bxob])
                        fw.op("act", lambda e: e.activation(out=sg[:], in_=pb[:, 0:256], func=AF.Silu, scale=rstdTo[:, 0:1]), reads=[bpb, brTo], writes=[bsg])
                        pb, bpb = nb()
                        for r in range(4):
                            for e_ in range(2):
                                mm(pb[:, 128 * r:128 * r + 128], krT[:, e_, 128 * r:128 * r + 128], qrT[:, e_, :], e_ == 0, e_ == 1, [bkr, bqr], [bpb])
                        fw.op("dve", lambda e: e.tensor_tensor(out=sdT[:], in0=pb[:, :].rearrange("p (r q) -> p r q", r=4), in1=dec[:, h, :, :], op=ALU.mult),
                              reads=[bpb, bconst], writes=[bsd])
                        ab, bab = banks[h % 2], bbank[h % 2]
                        for r in range(4):
                            mm(ab[:, 0:256], sdT[:, r, :], vr[:, r, :], r == 0, False, [bsd, bvr], [bab])
                        for e_ in range(2):
                            mm(ab[:, 0:256], qrX[:, e_, :], Rb[:, h, e_, :], False, e_ == 1, [bqrx, bRb[h]], [bab])
                        fw.op("dve", lambda e: e.tensor_reduce(out=small[:, 32:33], in_=ab[:, 0:256], axis=AX.X, op=ALU.add), reads=[bab], writes=[bsmall])
                        fw.op("act", lambda e: e.activation(out=gn[:], in_=ab[:, 0:256], func=AF.Square, accum_out=small[:, 33:34]), reads=[bab], writes=[bgn, bsmall])
                        fw.op("dve", lambda e: e.tensor_scalar(out=small[:, 34:36], in0=small[:, 32:34], scalar1=1.0 / 256, scalar2=None, op0=ALU.mult),
                              reads=[bsmall], writes=[bsmall])
                        fw.op("dve", lambda e: e.tensor_tensor(out=small[:, 36:37], in0=small[:, 34:35], in1=small[:, 34:35], op=ALU.mult), reads=[bsmall], writes=[bsmall])
                        fw.op("dve", lambda e: e.tensor_tensor(out=small[:, 37:38], in0=small[:, 35:36], in1=small[:, 36:37], op=ALU.subtract), reads=[bsmall], writes=[bsmall])
                        fw.op("act", lambda e: e.activation(out=small[:, 38:39], in_=small[:, 37:38], func=AF.Sqrt, bias=epsT[:], scale=1.0), reads=[bsmall, bconst], writes=[bsmall])
                        fw.op("dve", lambda e: e.reciprocal(out=small[:, 39:40], in_=small[:, 38:39]), reads=[bsmall], writes=[bsmall])
                        fw.op("dve", lambda e: e.tensor_scalar(out=gn[:], in0=ab[:, 0:256], scalar1=small[:, 34:35], scalar2=small[:, 39:40], op0=ALU.subtract, op1=ALU.mult),
                              reads=[bab, bsmall], writes=[bgn])
                        fw.op("dve", lambda e: e.tensor_tensor(out=Ob[:, 1024 + 256 * h:1280 + 256 * h], in0=gn[:], in1=sg[:], op=ALU.mult), reads=[bgn, bsg], writes=[bOb])
                        for e_ in range(2):
                            pb, bpb = nb()
                            for r in range(4):
                                mm(pb[:, 0:256], kz[:, r, e_, :], vr[:, r, :], r == 0, r == 3, [bkz, bvr], [bpb])
                            fw.op("dve", lambda e: e.scalar_tensor_tensor(out=R[:, h, e_, :], in0=R[:, h, e_, :], scalar=G512[h], in1=pb[:, 0:256],
                                                                          op0=ALU.mult, op1=ALU.add), reads=[bpb], writes=[bR[h]])
                        fw.op("act", lambda e: e.copy(out=Rb[:, h, :, :], in_=R[:, h, :, :]), reads=[bR[h]], writes=[bRb[h]])

                    ck(11)
                    if DEBUG:
                        fw.op("act", lambda e: e.copy(out=dbgt[:], in_=Ob[:]), reads=[bOb], writes=[bdbg])
                        fw.dma("pool", d_out, dbg_o[128 * i:128 * i + 128, :], dbgt[:], reads=[bdbg])
                    for cgp in range(4):
                        pb, bpb = nb()
                        pbv = pb[:, 0:256].bitcast(BF16)
                        for c in range(4):
                            tr(pbv[:, 128 * c:128 * c + 128], Ob[:, 128 * (4 * cgp + c):128 * (4 * cgp + c) + 128], [bOb], [bpb])
                        fw.op("dve", lambda e: e.tensor_copy(out=oT[:, 4 * cgp:4 * cgp + 4, :], in_=pbv.rearrange("p (c t) -> p c t", c=4)), reads=[bpb], writes=[boT])
                    fw.dma("pool", d_oT, oT_scr[i], oT[:].rearrange("p c t -> p (c t)"), reads=[boT], writes=[boTs])
            ca.close()
            fw.barrier()

            with ExitStack() as pbk:
                slabs = [(sb("bslab%d" % k, [128, 4096], BF16, pbk), Buf(), (lambda *a: None)("bslab%d" % k)) for k in range(2)]
                slab_n = [0]

                def load_slab2(sid):
                    t, b, dsm = slabs[slab_n[0] % 2]
                    slab_n[0] += 1
                    fw.dma("sp", dsm, t[:], wscr[sid], reads=bwscr, writes=[b])
                    return t, b

                x1 = sb("x1", [128, 4, DM], F32, pbk); bx1 = [Buf() for _ in range(4)]
                yb = sb("yb", [128, 4, DM], F32, pbk); byb = [Buf() for _ in range(4)]
                hT = sb("hT", [128, 16, 512], BF16, pbk); bhT = Buf()
                uT = sb("uT", [128, 64, 512], BF16, pbk); buT = Buf()
                gB = sb("gB", [128, 2, DM], F32, pbk); bgB = Buf()
                hb = sb("hb", [128, DM], BF16, pbk); bhb = Buf()
                rel_ = sb("rel_", [128, 512], F32, pbk); brel = Buf()
                st = sb("bst", [128, 64], F32, pbk); bst_ = Buf()
                d_x = (lambda *a: None)("bx"); d_g = (lambda *a: None)("bg"); d_o = (lambda *a: None)("bo")
                fw.dma("sp", d_g, gB[:, 0, :], D["gpostB"], writes=[bgB])
                fw.dma("sp", d_g, gB[:, 1, :], D["gpost2B"], writes=[bgB])

                def row_rstd(src_fn, bsrc, col):
                    for q4 in range(4):
                        fw.op("act", lambda e: e.activation(out=rel_[:], in_=src_fn(q4), func=AF.Square, accum_out=st[:, 32 + q4:33 + q4]),
                              reads=[bsrc], writes=[brel, bst_])
                    fw.op("dve", lambda e: e.tensor_reduce(out=st[:, 36:37], in_=st[:, 32:36], axis=AX.X, op=ALU.add), reads=[bst_], writes=[bst_])
                    fw.op("act", lambda e: e.activation(out=st[:, 37:38], in_=st[:, 36:37], func=AF.Sqrt, bias=epsT[:], scale=1.0 / DM), reads=[bst_, bconst], writes=[bst_])
                    fw.op("dve", lambda e: e.reciprocal(out=st[:, col:col + 1], in_=st[:, 37:38]), reads=[bst_], writes=[bst_])

                for cb in range(4 if RUN_B else 0):
                    for s in range(4):
                        fw.dma("sp", d_x, hT[:, :, 128 * s:128 * s + 128], oT_scr[4 * cb + s].rearrange("p (c t) -> p c t", c=16), reads=[boTs], writes=[bhT])
                    for tt in range(4):
                        fw.dma("sp", d_x, x1[:, tt, :], xo[512 * cb + 128 * tt:512 * cb + 128 * tt + 128, :], writes=[bx1[tt]])
                    for cg in range(8):
                        sl, bsl = load_slab2(OFF_WOUT + cg)
                        slv = sl[:].rearrange("p (c n) -> p c n", c=16)
                        for tt in range(4):
                            pb, bpb = nb()
                            for c in range(16):
                                mm(pb[:, 0:256], hT[:, c, 128 * tt:128 * tt + 128], slv[:, c, :], c == 0, c == 15, [bhT, bsl], [bpb])
                            fw.op("act" if tt % 2 else "dve", (lambda e: e.copy(out=yb[:, tt, 256 * cg:256 * cg + 256], in_=pb[:, 0:256])) if tt % 2 else
                                  (lambda e: e.tensor_copy(out=yb[:, tt, 256 * cg:256 * cg + 256], in_=pb[:, 0:256])), reads=[bpb], writes=[byb[tt]])
                    for tt in range(4):
                        row_rstd(lambda q4: yb[:, tt, 512 * q4:512 * q4 + 512], byb[tt], tt)
                        fw.op("dve", lambda e: e.scalar_tensor_tensor(out=yb[:, tt, :], in0=yb[:, tt, :], scalar=st[:, tt:tt + 1], in1=gB[:, 0, :],
                                                                      op0=ALU.mult, op1=ALU.mult), reads=[byb[tt], bst_, bgB], writes=[byb[tt]])
                        fw.op("pool", lambda e: e.tensor_tensor(out=x1[:, tt, :], in0=x1[:, tt, :], in1=yb[:, tt, :], op=ALU.add), reads=[byb[tt], bx1[tt]], writes=[bx1[tt]])
                        if DEBUG:
                            r0 = 512 * cb + 128 * tt
                            fw.dma("pool", d_out, dbg_x1[r0:r0 + 128, :], x1[:, tt, :], reads=[bx1[tt]])
                        row_rstd(lambda q4: x1[:, tt, 512 * q4:512 * q4 + 512], bx1[tt], 4 + tt)
                        fw.op("act", lambda e: e.activation(out=hb[:], in_=x1[:, tt, :], func=AF.Copy, scale=st[:, 4 + tt:5 + tt]), reads=[bx1[tt], bst_], writes=[bhb])
                        for cgp in range(4):
                            pb, bpb = nb()
                            pbv = pb[:, 0:256].bitcast(BF16)
                            for c in range(4):
                                tr(pbv[:, 128 * c:128 * c + 128], hb[:, 128 * (4 * cgp + c):128 * (4 * cgp + c) + 128], [bhb], [bpb])
                            fw.op("dve", lambda e: e.tensor_copy(out=hT[:, 4 * cgp:4 * cgp + 4, 128 * tt:128 * tt + 128], in_=pbv.rearrange("p (c t) -> p c t", c=4)),
                                  reads=[bpb], writes=[bhT])
                    for s in range(32):
                        sl, bsl = load_slab2(OFF_WUP + s)
                        slv = sl[:].rearrange("p (t c n) -> p t c n", t=2, c=16)
                        for ct in range(2):
                            pb, bpb = nb()
                            for c in range(16):
                                mm(pb[:, :], slv[:, ct, c, :], hT[:, c, :], c == 0, c == 15, [bhT, bsl], [bpb])
                            fw.op("act", lambda e: e.activation(out=rel_[:], in_=pb[:, :], func=AF.Relu), reads=[bpb], writes=[brel])
                            fw.op("dve" if ct == 0 else "pool", lambda e: e.tensor_tensor(out=uT[:, 2 * s + ct, :], in0=rel_[:], in1=rel_[:], op=ALU.mult), reads=[brel], writes=[buT])
                    for cg in range(8):
                        accb = [(banks[k], bbank[k]) for k in range(4)]
                        for hq in range(4):
                            sl, bsl = load_slab2(OFF_WDN + cg * 4 + hq)
                            slv = sl[:].rearrange("p (c n) -> p c n", c=16)
                            for tt in range(4):
                                ab, bab = accb[tt]
                                for c in range(16):
                                    mm(ab[:, 0:256], uT[:, 16 * hq + c, 128 * tt:128 * tt + 128], slv[:, c, :], hq == 0 and c == 0, hq == 3 and c == 15, [buT, bsl], [bab])
                        for tt in range(4):
                            ab, bab = accb[tt]
                            fw.op("act" if tt % 2 else "dve", (lambda e: e.copy(out=yb[:, tt, 256 * cg:256 * cg + 256], in_=ab[:, 0:256])) if tt % 2 else
                                  (lambda e: e.tensor_copy(out=yb[:, tt, 256 * cg:256 * cg + 256], in_=ab[:, 0:256])), reads=[bab], writes=[byb[tt]])
                    for tt in range(4):
                        row_rstd(lambda q4: yb[:, tt, 512 * q4:512 * q4 + 512], byb[tt], 8 + tt)
                        fw.op("dve", lambda e: e.scalar_tensor_tensor(out=yb[:, tt, :], in0=yb[:, tt, :], scalar=st[:, 8 + tt:9 + tt], in1=gB[:, 1, :],
                                                                      op0=ALU.mult, op1=ALU.mult), reads=[byb[tt], bst_, bgB], writes=[byb[tt]])
                        fw.op("pool", lambda e: e.tensor_tensor(out=yb[:, tt, :], in0=yb[:, tt, :], in1=x1[:, tt, :], op=ALU.add), reads=[byb[tt], bx1[tt]], writes=[byb[tt]])
                        r0 = 512 * cb + 128 * tt
                        fw.dma("pool", d_out, out[r0:r0 + 128, :], yb[:, tt, :], reads=[byb[tt]])
        except _Stop:
            ca.close()
        for ssem in fw.store_sems:
            fw.eng["sp"].wait_ge(fw.sems[ssem], fw.cnt[ssem])
        print("bass instructions:", fw.n_inst, {k: v for k, v in fw.cnt.items() if not k.startswith("d_")})
    return nc


_NC_CACHE = {}


def kernel(**inputs):
    x = np.asarray(inputs["x"], np.float32)
    f = lambda k: np.ascontiguousarray(np.asarray(inputs[k], np.float32)[0])
    cosT, sinT = _rope_tables()
    t5 = np.asarray(inputs["t5_bias"], np.float32)
    shared = {
        "w_in": f("w_in"), "w_out": f("w_out"), "w_up": f("w_up"), "w_down": f("w_down"),
        "gpre": np.ascontiguousarray(f("norm_mix_pre").reshape(16, 128).T),
        "gmlp": np.ascontiguousarray(f("norm_mlp_pre").reshape(16, 128).T),
        "gpostB": np.ascontiguousarray(np.broadcast_to(f("norm_mix_post")[None, :], (128, DM))),
        "gpost2B": np.ascontiguousarray(np.broadcast_to(f("norm_mlp_post")[None, :], (128, DM))),
        "w1k": f("cmp_w1_k"), "w1v": f("cmp_w1_v"),
        "peTk": np.ascontiguousarray(f("cmp_pe_k").T), "peTv": np.ascontiguousarray(f("cmp_pe_v").T),
        "b1k": f("cmp_b1_k").reshape(128, 1).copy(), "b1v": f("cmp_b1_v").reshape(128, 1).copy(),
        "w2k": f("cmp_w2_k"), "w2v": f("cmp_w2_v"),
        "cosT": cosT, "sinT": sinT,
    }
    in_maps = []
    own_idx = []
    for core in range(8):
        b, j = divmod(core, 4)
        idx = (np.arange(16)[:, None] * 512 + 128 * j + np.arange(128)[None, :]).reshape(-1)
        own_idx.append((b, idx))
        m = dict(shared)
        m["xT"] = np.ascontiguousarray(x[b].T)
        m["xo"] = np.ascontiguousarray(x[b][idx])
        m["xTo"] = np.ascontiguousarray(m["xo"].T)
        m["cosTo"] = np.ascontiguousarray(cosT[:, idx])
        m["sinTo"] = np.ascontiguousarray(sinT[:, idx])
        m.update(_host_consts(j, t5))
        in_maps.append(m)
    if "nc" not in _NC_CACHE:
        _NC_CACHE["nc"] = build()
    res = run_bass_kernel_spmd(_NC_CACHE["nc"], in_maps, core_ids=list(range(8)))
    outp = np.zeros((2, S, DM), np.float32)
    for core in range(8):
        b, idx = own_idx[core]
        outp[b, idx] = res.results[core]["out"]
    if DEBUG:
        kernel.dbg = [(own_idx[c], res.results[c]["dbg_o"], res.results[c]["dbg_x1"]) for c in range(8)]
    return outp
```

```python
import numpy as np
from contextlib import ExitStack
import concourse.bass as bass
import concourse.mybir as mybir
from concourse.bass_utils import run_bass_kernel_spmd

F32 = mybir.dt.float32
BF16 = mybir.dt.bfloat16
AF = mybir.ActivationFunctionType
ALU = mybir.AluOpType
AX = mybir.AxisListType

NEGM = -30000.0
S = 8192
DM = 2048
NSTEP = 16
DEBUG = False
STOP_AT = None


class _Stop(Exception):
    pass


def ck(k):
    if STOP_AT == k:
        FW.enabled = False
N_RUN_STEPS = 16
RUN_B = True


class Buf:
    __slots__ = ("name", "w", "r", "ds", "ss", "psum")

    def __init__(self, name=""):
        self.name = name
        self.w = None
        self.r = []
        self.ds = None
        self.ss = None
        self.psum = name.startswith("bank")


def _compact(lst):
    best = {}
    for k, v in lst:
        if best.get(k, 0) < v:
            best[k] = v
    return list(best.items())


class FW:
    def __init__(self, nc, es):
        self.nc = nc
        self.es = es
        self.eng = {"pe": nc.tensor, "act": nc.scalar, "dve": nc.vector, "pool": nc.gpsimd, "sp": nc.sync}
        self.sems = {}
        self.cnt = {}
        self.waited = {k: {} for k in self.eng}
        for k in self.eng:
            self.sems[k] = es.enter_context(nc.semaphore("s_" + k))
            self.cnt[k] = 0
        self.n_inst = 0
        self.store_sems = []

    def dsem(self, name):
        key = "d_" + name
        self.sems[key] = self.es.enter_context(self.nc.semaphore(key))
        self.cnt[key] = 0
        return key

    def _wait(self, e, deps):
        need = {}
        for d in deps:
            if d is None:
                continue
            k, v = d
            if k == e and e == "pe":
                continue
            if need.get(k, 0) < v:
                need[k] = v
        for k, v in need.items():
            if self.waited[e].get(k, 0) >= v:
                continue
            self.eng[e].wait_ge(self.sems[k], v)
            self.waited[e][k] = v

    def _deps(self, reads, writes, e=None):
        deps = []
        for b in reads:
            deps.append(b.w)
            if b.psum and e in ("act", "dve"):
                other = "dve" if e == "act" else "act"
                deps.extend(t for t in b.r if t[0] == other)
        for b in writes:
            deps.append(b.w)
            deps.extend(b.r)
        return deps

    def _upd(self, tok, reads, writes):
        for b in writes:
            b.w = tok
            b.r = []
        for b in reads:
            b.r.append(tok)
            if len(b.r) > 32:
                b.r = _compact(b.r)

    enabled = True

    def op(self, e, fn, reads=(), writes=()):
        if not FW.enabled:
            return
        self._wait(e, self._deps(reads, writes, e))
        ins = fn(self.eng[e])
        self.cnt[e] += 1
        ins.then_inc(self.sems[e], 1)
        self._upd((e, self.cnt[e]), reads, writes)
        self.n_inst += 1

    def barrier(self):
        if not FW.enabled:
            return
        deps = [(k, v) for k, v in self.cnt.items() if v > 0]
        for e in self.eng:
            need = {}
            for k, v in deps:
                need[k] = v
            for k, v in need.items():
                if self.waited[e].get(k, 0) >= v:
                    continue
                self.eng[e].wait_ge(self.sems[k], v)
                self.waited[e][k] = v

    def dma(self, q, sem, out, in_, reads=(), writes=(), **kw):
        if not FW.enabled:
            return
        if writes:
            assert len(writes) == 1
            b = writes[0]
            if b.ds is None:
                b.ds = self.dsem("b%d" % len(self.sems))
            sem = b.ds
        else:
            b = reads[0]
            if b.ss is None:
                b.ss = self.dsem("s%d" % len(self.sems))
                self.store_sems.append(b.ss)
            sem = b.ss
        self._wait(q, self._deps(reads, writes))
        ins = self.eng[q].dma_start(out=out, in_=in_, **kw)
        self.cnt[sem] += 16
        ins.then_inc(self.sems[sem], 16)
        self._upd((sem, self.cnt[sem]), reads, writes)
        self.n_inst += 1


def _t5_bucket(rel):
    n = np.maximum(rel, 0)
    nf = np.maximum(n, 1).astype(np.float32)
    large = 16 + (np.log(nf / np.float32(16)) / np.float32(np.log(128 / 16)) * np.float32(16)).astype(np.int32)
    large = np.minimum(large, 31)
    return np.where(n < 16, n, large)


def _host_consts(j, t5):
    qi = np.arange(128)[None, :]
    ki = np.arange(128)[:, None]
    c = {}
    raw = np.zeros((12, 128, 512), np.float32)
    r31 = np.zeros((12, 128, 512), np.float32)
    wm = np.zeros((128, 8, 128), np.float32)
    sm = np.zeros((128, 5, 128), np.float32)
    for r in range(8):
        rel = 128 * (4 + j - r) + qi - ki
        wm[:, r, :] = np.where((rel >= 0) & (rel < 512), 0.0, NEGM)
        if r >= 3:
            sm[:, r - 3, :] = np.where(rel >= 0, 0.0, NEGM)
            bk = _t5_bucket(rel)
            for g in range(2):
                for h in range(4):
                    raw[6 * g + r - 3, :, 128 * h:128 * h + 128] = t5[bk, 4 * g + h]
                    r31[6 * g + r - 3, :, 128 * h:128 * h + 128] = t5[31, 4 * g + h]
    rr = np.arange(64)[:, None]
    relc = 512 + 128 * j + qi - 16 * rr - 15
    bkc = _t5_bucket(relc)
    bcm = np.zeros((128, 2, 512), np.float32)
    for g in range(2):
        for h in range(4):
            raw[6 * g + 5, :64, 128 * h:128 * h + 128] = t5[bkc, 4 * g + h]
            r31[6 * g + 5, :64, 128 * h:128 * h + 128] = t5[31, 4 * g + h]
    for h in range(4):
        m = np.where(relc >= 0, 0.0, NEGM)
        bcm[:64, 0, 128 * h:128 * h + 128] = m
        m0 = m.copy()
        m0[32, :] = NEGM
        bcm[:64, 1, 128 * h:128 * h + 128] = m0
    c["t5raw"] = raw
    c["t531"] = r31
    c["wm"] = wm
    c["sm"] = sm
    c["bcm"] = bcm
    dum = np.zeros((128, 512), np.float32)
    dum[0, :] = NEGM
    c["dum"] = dum
    e32 = np.zeros((128, 2, 16, 128), np.float32)
    for p in range(128):
        jl = p % 64
        a, rem = divmod(jl, 32)
        m, par = rem // 2, rem % 2
        e32[p, a, m, 64 * par:64 * par + 64] = 1.0
    c["e32"] = e32.reshape(128, 4096)
    sh = np.zeros((128, 288), np.float32)
    for r in range(64):
        sh[r, r + 128] = 1.0
    c["sh"] = sh
    c["ident"] = np.eye(128, dtype=np.float32)
    msel = np.zeros((128, 4, 128), np.float32)
    for slot in range(1, 512):
        n = slot - 1
        cs = 16 * n
        for jb in range(128):
            ss = 64 * jb
            if cs <= ss + 63 and cs + 31 >= ss:
                msel[slot % 128, slot // 128, jb] = 1.0
    c["msel"] = msel
    a16 = np.ones((128, 16), np.float32)
    b16 = np.zeros((128, 16), np.float32)
    for q in range(128):
        ucur = 8 + 2 * j + (1 if q >= 64 else 0)
        for u in range(16):
            if u > ucur:
                a16[q, u] = 0.0
                b16[q, u] = -1.0
            elif u == ucur or u == ucur - 1:
                a16[q, u] = 0.0
                b16[q, u] = 1e4
    c["a16"] = a16
    c["b16"] = b16
    gam = 1.0 - 2.0 ** (-5.0 - np.arange(4, dtype=np.float64))
    dec = np.zeros((128, 4, 4, 128), np.float64)
    xi = np.zeros((128, 4, 128), np.float64)
    zeta = np.zeros((128, 4, 4), np.float64)
    for h in range(4):
        for r in range(4):
            d = 128 * (j - r) + qi - ki
            dec[:, h, r, :] = np.where(d >= 0, gam[h] ** np.maximum(d, 0), 0.0)
            zeta[:, h, r] = gam[h] ** (511 - (128 * r + np.arange(128)))
        xi[:, h, :] = gam[h] ** (128 * j + np.arange(128) + 1)[None, :]
    c["dec"] = dec.astype(np.float32)
    c["xi"] = xi.astype(np.float32)
    c["zeta"] = zeta.astype(np.float32).reshape(128, 16)
    return c


def _rope_tables():
    inv = (10000.0 ** (-np.arange(0, 256, 2, dtype=np.float32) / np.float32(256))).astype(np.float32)
    ang = np.arange(S, dtype=np.float32)[None, :] * inv[:, None]
    return np.cos(ang).astype(np.float32), np.sin(ang).astype(np.float32)


G512 = [float((1.0 - 2.0 ** (-5.0 - h)) ** 512) for h in range(4)]

WIN_SLABS = ([("QA%d" % s, "K", 256 * s) for s in range(4)] +
             [("KC", "K", 1024), ("VC", "K", 1280), ("KS", "K", 1536), ("VS", "V", 1792),
              ("KW", "K", 2048), ("VW", "V", 2304)] +
             [("QR%d" % h, "K", 2584 + 256 * h) for h in range(4)] +
             [("KR%d" % h, "K", 3608 + 256 * h) for h in range(4)] +
             [("VR%d" % h, "V", 4632 + 256 * h) for h in range(4)] +
             [("GR%d" % h, "V", 5656 + 256 * h) for h in range(4)])
WIN_IDX = {n: k for k, (n, _, _) in enumerate(WIN_SLABS)}
N_WIN = len(WIN_SLABS)
OFF_WOUT = N_WIN
OFF_WUP = OFF_WOUT + 8
OFF_WDN = OFF_WUP + 32
OFF_W1 = OFF_WDN + 32
N_SLAB = OFF_W1 + 4


def build():
    FW.enabled = True
    nc = bass.Bass("TRN2", target_bir_lowering=False)
    D = {}

    def din(name, shape, dt=F32):
        D[name] = nc.dram_tensor(name, list(shape), dt, kind="ExternalInput").ap()
        return D[name]

    xT = din("xT", [DM, S])
    xTo = din("xTo", [DM, 2048])
    xo = din("xo", [2048, DM])
    w_in = din("w_in", [DM, 6680])
    w_out = din("w_out", [DM, DM])
    w_up = din("w_up", [DM, 8192])
    w_down = din("w_down", [8192, DM])
    din("gpre", [128, 16]); din("gmlp", [128, 16])
    din("gpostB", [128, DM]); din("gpost2B", [128, DM])
    din("w1k", [4096, 128]); din("w1v", [4096, 128])
    din("peTk", [128, 32]); din("peTv", [128, 32])
    din("b1k", [128, 1]); din("b1v", [128, 1])
    din("w2k", [128, 128]); din("w2v", [128, 128])
    din("cosT", [128, S]); din("sinT", [128, S])
    din("cosTo", [128, 2048]); din("sinTo", [128, 2048])
    din("t5raw", [12, 128, 512]); din("t531", [12, 128, 512])
    din("wm", [128, 8, 128]); din("sm", [128, 5, 128]); din("bcm", [128, 2, 512]); din("dum", [128, 512])
    din("e32", [128, 4096]); din("sh", [128, 288]); din("ident", [128, 128])
    din("msel", [128, 4, 128]); din("a16", [128, 16]); din("b16", [128, 16])
    din("dec", [128, 4, 4, 128]); din("xi", [128, 4, 128]); din("zeta", [128, 16])
    out = nc.dram_tensor("out", [2048, DM], F32, kind="ExternalOutput").ap()
    if DEBUG:
        dbg_o = nc.dram_tensor("dbg_o", [2048, DM], F32, kind="ExternalOutput").ap()
        dbg_x1 = nc.dram_tensor("dbg_x1", [2048, DM], F32, kind="ExternalOutput").ap()

    wscr = nc.dram_tensor("wscr", [N_SLAB, 128, 4096], BF16, kind="Internal").ap()
    ks_scr = nc.dram_tensor("ks_scr", [2, NSTEP, 128, 512], BF16, kind="Internal").ap()
    vs_scr = nc.dram_tensor("vs_scr", [2, NSTEP, 128, 4 * 130], BF16, kind="Internal").ap()
    oT_scr = nc.dram_tensor("oT_scr", [NSTEP, 128, 16 * 128], BF16, kind="Internal").ap()

    es = ExitStack()
    with es:
        fw = FW(nc, es)
        try:

            def sb(name, shape, dt, stack=es):
                return stack.enter_context(nc.sbuf_tensor("sb_" + name, list(shape), dt))

            banks = [es.enter_context(nc.psum_tensor("bank%d" % k, [128, 512], F32)) for k in range(8)]
            bbank = [Buf("bank%d" % k) for k in range(8)]
            rot = [0]
            rotset = [[4, 5, 6, 7]]

            def set_rot(lst):
                rotset[0] = list(lst)

            def nb():
                k = rotset[0][rot[0] % len(rotset[0])]
                rot[0] += 1
                return banks[k], bbank[k]

            def mm(o, lhsT, rhs, start, stop, reads, writes):
                fw.op("pe", lambda e: e.matmul(o, lhsT=lhsT, rhs=rhs, start=start, stop=stop), reads, writes)

            def tr(o, in_, reads, writes):
                fw.op("pe", lambda e: e.transpose(o, in_, ident[:]), reads + [bconst], writes)

            d_const = (lambda *a: None)("const")
            d_out = (lambda *a: None)("out")
            d_wst = (lambda *a: None)("wstore")
            bconst = Buf("const")
            bwscr = [Buf("wscrA"), Buf("wscrB")]
            boTs = Buf("oTs")

            ident = sb("ident", [128, 128], BF16)
            ones = sb("ones", [128, 128], BF16)
            gpre = sb("gpre", [128, 16], F32)
            gmlp = sb("gmlp", [128, 16], F32)
            epsT = sb("epsT", [128, 1], F32)
            ca = ExitStack()
            e32 = sb("e32", [128, 2, 2048], BF16, ca)
            shb = sb("shb", [128, 288], BF16, ca)
            td = sb("td", [128, 2, 5, 512], BF16, ca)
            bc = sb("bc", [128, 2, 2, 512], BF16, ca)
            dumb = sb("dumb", [128, 512], BF16, ca)
            wmb = sb("wmb", [128, 8, 128], BF16, ca)
            smb = sb("smb", [128, 5, 128], BF16, ca)
            a16 = sb("a16", [128, 16], F32, ca)
            b16 = sb("b16", [128, 16], F32, ca)
            dec = sb("dec", [128, 4, 4, 128], F32, ca)
            xi = sb("xi", [128, 4, 128], F32, ca)
            zeta = sb("zeta", [128, 16], F32, ca)
            w2k = sb("w2k", [128, 128], BF16, ca)
            w2v = sb("w2v", [128, 128], BF16, ca)
            biask = sb("biask", [128, 1], F32, ca)
            biasv = sb("biasv", [128, 1], F32, ca)
            wg = sb("wg", [128, 16, 24], BF16, ca)
            vcm = sb("vcm", [128, 2, 4, 256], BF16, ca)
            kcT = sb("kcT", [128, 2, 512], BF16, ca)

            with ExitStack() as ps_:
                stA = sb("stA", [128, 16, 256], F32, ps_)
                stB = sb("stB", [128, 16, 256], F32, ps_)
                cvA = sb("cvA", [128, 4096], BF16, ps_)
                cvB = sb("cvB", [128, 4096], BF16, ps_)
                sts = [(stA, Buf("stA"), cvA, Buf("cvA"), None), (stB, Buf("stB"), cvB, Buf("cvB"), None)]
                sm_f = sb("sm_f", [128, 2048], F32, ps_)
                bsm = Buf("sm_f")
                d_sm = (lambda *a: None)("sm")

                def load_cast(dst_ap, src_ap, n, eng="dve"):
                    view = sm_f[:dst_ap.shape[0], 0:n]
                    if len(dst_ap.shape) > 2:
                        pat = {3: "p (a b) -> p a b", 4: "p (a b c) -> p a b c"}[len(dst_ap.shape)]
                        kw = dict(zip("abc", dst_ap.shape[1:]))
                        kw.pop("a")
                        view = view.rearrange(pat, **kw)
                    fw.dma("sp", d_sm, view, src_ap, writes=[bsm])
                    fw.op(eng, lambda e: e.tensor_copy(out=dst_ap, in_=view), reads=[bsm], writes=[bconst])

                load_cast(ident[:], D["ident"], 128)
                load_cast(e32[:, 0, :], D["e32"][:, 0:2048], 2048)
                load_cast(e32[:, 1, :], D["e32"][:, 2048:4096], 2048)
                load_cast(shb[:], D["sh"], 288)
                load_cast(wmb[:], D["wm"], 1024)
                load_cast(dumb[:], D["dum"], 512)
                load_cast(smb[:], D["sm"], 640)
                load_cast(a16[:], D["a16"], 16)
                load_cast(b16[:], D["b16"], 16)
                load_cast(dec[:], D["dec"], 2048)
                load_cast(xi[:], D["xi"], 512)
                load_cast(zeta[:], D["zeta"], 16)
                load_cast(gpre[:], D["gpre"], 16)
                load_cast(gmlp[:], D["gmlp"], 16)
                load_cast(w2k[:], D["w2k"], 128)
                load_cast(w2v[:], D["w2v"], 128)
                bkc = [Buf(), Buf()]
                bvcm = [Buf(), Buf()]
                for g in range(2):
                    load_cast(vcm[:, g, :, 128:256], D["msel"], 512)
                fw.op("dve", lambda e: e.memset(ones[:], 1.0), writes=[bconst])
                fw.op("dve", lambda e: e.memset(epsT[:], 1e-6), writes=[bconst])
                fw.op("dve", lambda e: e.memset(vcm[:, :, :, 0:128], 0.0), reads=[bconst], writes=bvcm)
                fw.op("dve", lambda e: e.memset(kcT[:], 0.0), writes=bkc)
                ck(1)
                t5a = sb("t5a", [128, 512], F32, ps_)
                t5b = sb("t5b", [128, 512], F32, ps_)
                bt5a, bt5b = Buf(), Buf()
                d_t5 = (lambda *a: None)("t5")
                bcm = sb("bcm", [128, 2, 512], F32, ps_)
                bbcm = Buf()
                fw.dma("sp", d_t5, bcm[:], D["bcm"], writes=[bbcm])
                for t in range(12):
                    g, k = divmod(t, 6)
                    fw.dma("sp", d_t5, t5a[:], D["t5raw"][t], writes=[bt5a])
                    fw.dma("sp", d_t5, t5b[:], D["t531"][t], writes=[bt5b])
                    if k < 5:
                        fw.op("dve", lambda e: e.tensor_tensor(out=td[:, g, k, :], in0=t5a[:], in1=t5b[:], op=ALU.subtract),
                              reads=[bt5a, bt5b], writes=[bconst])
                    else:
                        fw.op("dve", lambda e: e.tensor_tensor(out=t5a[:], in0=t5a[:], in1=t5b[:], op=ALU.subtract),
                              reads=[bt5a, bt5b], writes=[bt5a])
                        for v in range(2):
                            fw.op("dve", lambda e: e.tensor_tensor(out=bc[:, g, v, :], in0=t5a[:], in1=bcm[:, v, :], op=ALU.add),
                                  reads=[bt5a, bbcm], writes=[bconst])
                gst = sb("gst", [128, 16, 24], F32, ps_)
                bgst = Buf()
                with nc.allow_non_contiguous_dma(reason="small gate weight slice"):
                    fw.dma("sp", d_t5, gst[:], w_in.rearrange("(c p) n -> p c n", p=128)[:, :, 2560:2584], writes=[bgst])
                fw.op("dve", lambda e: e.tensor_tensor(out=wg[:], in0=gst[:], in1=gpre[:, :, None].broadcast_to([128, 16, 24]), op=ALU.mult),
                      reads=[bgst, bconst], writes=[bconst])

                ck(2)
                slab_jobs = []
                for k, (nm, kind, c0) in enumerate(WIN_SLABS):
                    slab_jobs.append((k, w_in.rearrange("(c p) n -> p c n", p=128)[:, :, c0:c0 + 256], kind, gpre))
                for cg in range(8):
                    slab_jobs.append((OFF_WOUT + cg, w_out.rearrange("(c p) n -> p c n", p=128)[:, :, 256 * cg:256 * cg + 256], "V", None))
                for s in range(32):
                    slab_jobs.append((OFF_WUP + s, w_up.rearrange("(c p) n -> p c n", p=128)[:, :, 256 * s:256 * s + 256], "K", gmlp))
                for cg in range(8):
                    for hq in range(4):
                        slab_jobs.append((OFF_WDN + cg * 4 + hq,
                                          w_down.rearrange("(c p) n -> p c n", p=128)[:, 16 * hq:16 * hq + 16, 256 * cg:256 * cg + 256], "V", None))
                for t, nm in enumerate(["w1k", "w1v"]):
                    for hf in range(2):
                        slab_jobs.append((OFF_W1 + 2 * t + hf, D[nm].rearrange("(l d) h -> d l h", d=128)[:, 16 * hf:16 * hf + 16, :], "W1", None))
                for n, (sid, src, kind, gain) in enumerate(slab_jobs):
                    st, bst, cv, bcv, dsm = sts[n % 2]
                    ce = "dve" if n % 2 == 0 else "pool"
                    if kind == "W1":
                        fw.dma("sp", dsm, st[:, :, 0:128], src, writes=[bst])
                        fw.op(ce, lambda e: e.tensor_copy(out=cv[:, 0:2048].rearrange("p (l h) -> p l h", h=128), in_=st[:, :, 0:128]),
                              reads=[bst], writes=[bcv])
                        fw.dma("act", d_wst, wscr[sid, :, 0:2048], cv[:, 0:2048], reads=[bcv], writes=[bwscr[n % 2]])
                        continue
                    fw.dma("sp", dsm, st[:], src, writes=[bst])
                    if kind == "K":
                        ov = cv[:].rearrange("p (t c n) -> p t c n", t=2, c=16)
                        iv = st[:].rearrange("p c (t n) -> p t c n", t=2)
                        gv = None if gain is None else gain[:, None, :, None].broadcast_to([128, 2, 16, 128])
                    else:
                        ov = cv[:].rearrange("p (c n) -> p c n", c=16)
                        iv = st[:]
                        gv = None if gain is None else gain[:, :, None].broadcast_to([128, 16, 256])
                    if gv is None:
                        fw.op(ce, lambda e: e.tensor_copy(out=ov, in_=iv), reads=[bst], writes=[bcv])
                    else:
                        fw.op(ce, lambda e: e.tensor_tensor(out=ov, in0=iv, in1=gv, op=ALU.mult), reads=[bst, bconst], writes=[bcv])
                    fw.dma("act", d_wst, wscr[sid], cv[:], reads=[bcv], writes=[bwscr[n % 2]])
                ck(3)
                w1st = sb("w1st", [128, 32, 128], BF16, ps_)
                pest = sb("pest", [128, 32], BF16, ps_)
                b1st = sb("b1st", [128, 1], F32, ps_)
                bw1, bpe, bb1 = Buf(), Buf(), Buf()
                d_w1 = (lambda *a: None)("w1p")
                for t, (nm, pnm, bnm, dst) in enumerate([("w1k", "peTk", "b1k", biask), ("w1v", "peTv", "b1v", biasv)]):
                    for hf in range(2):
                        fw.dma("sp", d_w1, w1st[:, 16 * hf:16 * hf + 16, :], wscr[OFF_W1 + 2 * t + hf, :, 0:2048].rearrange("p (l h) -> p l h", h=128),
                               reads=bwscr, writes=[bw1])
                    load_cast(pest[:], D[pnm], 32)
                    fw.dma("sp", d_w1, b1st[:], D[bnm], writes=[bb1])
                    pb, bpb = nb()
                    for l in range(32):
                        mm(pb[:, 0:1], w1st[:, l, :], pest[:, l:l + 1], l == 0, l == 31, [bw1, bconst], [bpb])
                    fw.op("dve", lambda e: e.tensor_tensor(out=dst[:], in0=pb[:, 0:1], in1=b1st[:], op=ALU.add), reads=[bpb, bb1], writes=[bconst])

            ck(4)
            fw.barrier()
            with ExitStack() as pa:
                slabs = [(sb("slab%d" % k, [128, 4096], BF16, pa), Buf(), (lambda *a: None)("slab%d" % k)) for k in range(3)]
                slab_n = [0]

                def load_slab(sid, half=False):
                    t, b, dsm = slabs[slab_n[0] % 3]
                    slab_n[0] += 1
                    if half:
                        fw.dma("sp", dsm, t[:, 0:2048], wscr[sid, :, 0:2048], reads=bwscr, writes=[b])
                    else:
                        fw.dma("sp", dsm, t[:], wscr[sid], reads=bwscr, writes=[b])
                    return t, b

                xst = [(sb("xst%d" % k, [128, 4, 512], F32, pa), Buf(), (lambda *a: None)("xst%d" % k)) for k in range(2)]
                xst_n = [0]
                xb = sb("xb", [128, 16, 512], BF16, pa); bxb = Buf()
                xsq = sb("xsq", [128, 4, 512], BF16, pa); bxsq = Buf()
                xob = sb("xob", [128, 16, 128], BF16, pa); bxob = Buf()
                rstdB = sb("rstdB", [128, 512], F32, pa); brB = Buf()
                rstdT = sb("rstdT", [128, 4], F32, pa); brT = Buf()
                rstdBo = sb("rstdBo", [128, 128], F32, pa); brBo = Buf()
                rstdTo = sb("rstdTo", [128, 1], F32, pa); brTo = Buf()
                cs = sb("cs", [128, 2, 512], F32, pa); bcs = Buf(); d_cs = (lambda *a: None)("cs")
                cso = sb("cso", [128, 2, 128], F32, pa); bcso = Buf()
                kcs = [sb("kcs%d" % t, [128, 2, 528], BF16, pa) for t in range(2)]
                bkcs = [Buf(), Buf()]
                ksf = sb("ksf", [128, 2, 512], BF16, pa); bksf = [Buf(), Buf()]
                vsf = sb("vsf", [128, 2, 4, 130], BF16, pa); bvsf = Buf()
                kwT = sb("kwT", [128, 2, 2, 512], BF16, pa); bkw = [[Buf(), Buf()], [Buf(), Buf()]]
                vw = sb("vw", [128, 2, 2, 4, 130], BF16, pa); bvw = [Buf(), Buf()]
                ksl = [(sb("ksl%d" % k, [128, 512], BF16, pa), sb("vsl%d" % k, [128, 4, 130], BF16, pa), Buf(), (lambda *a: None)("ksl%d" % k)) for k in range(3)]
                ksl_n = [0]
                d_kst = [[(lambda *a: None)("kst%d%d" % (g_, p_)) for p_ in range(2)] for g_ in range(2)]
                bksc = [Buf(), Buf()]
                bvsc = [Buf(), Buf()]
                qaT = sb("qaT", [128, 8, 128], BF16, pa); bqa = Buf()
                gat = sb("gat", [128, 24], F32, pa); bgat = Buf()
                hidg = sb("hidg", [128, 2, 2, 128], BF16, pa); bhid = Buf()
                gtmp = sb("gtmp", [128, 3, 32], F32, pa); bgt = Buf()
                put = [(sb("put%d" % k, [128, 512], BF16, pa), Buf()) for k in range(3)]
                put_n = [0]
                Of = sb("Of", [128, 1024], F32, pa); bOf = Buf()
                Ob = sb("Ob", [128, 2048], BF16, pa); bOb = Buf()
                oT = sb("oT", [128, 16, 128], BF16, pa); boT = Buf()
                d_oT = (lambda *a: None)("oT")
                impf = sb("impf", [128, 128], F32, pa); bimp = Buf()
                imp1 = sb("imp1", [128, 128], F32, pa); bimp1 = Buf()
                imp2 = sb("imp2", [128, 128], F32, pa); bimp2 = Buf()
                top8 = sb("top8", [128, 16], F32, pa); btop = Buf()
                nsel = sb("nsel", [128, 128], BF16, pa); bnsel = Buf()
                NS = sb("NS", [128, 2, 512], BF16, pa); bNS = [Buf(), Buf()]
                small = sb("small", [128, 64], F32, pa); bsmall = Buf()
                R = sb("R", [128, 4, 2, 256], F32, pa); bR = [Buf() for _ in range(4)]
                Rb = sb("Rb", [128, 4, 2, 256], BF16, pa); bRb = [Buf() for _ in range(4)]
                krT = sb("krT", [128, 2, 512], BF16, pa); bkr = Buf()
                rtmp = sb("rtmp", [128, 4, 512], F32, pa); brt = Buf()
                kz = sb("kz", [128, 4, 2, 128], BF16, pa); bkz = Buf()
                vr = sb("vr", [128, 4, 256], BF16, pa); bvr = Buf()
                qrT = sb("qrT", [128, 2, 128], BF16, pa); bqr = Buf()
                qrX = sb("qrX", [128, 2, 128], BF16, pa); bqrx = Buf()
                sdT = sb("sdT", [128, 4, 128], BF16, pa); bsd = Buf()
                sg = sb("sg", [128, 256], F32, pa); bsg = Buf()
                gn = sb("gn", [128, 256], F32, pa); bgn = Buf()
                dbgt = sb("dbgt", [128, 2048], F32, pa) if DEBUG else None
                bdbg = Buf()

                fw.op("dve", lambda e: e.memset(R[:], 0.0), writes=bR)
                fw.op("dve", lambda e: e.memset(Rb[:], 0.0), writes=bRb)
                fw.op("dve", lambda e: e.memset(vsf[:, :, :, 128:130], 1.0), writes=[bvsf])
                fw.op("dve", lambda e: e.memset(vw[:, :, :, :, 128:130], 1.0), writes=bvw)
                for t in range(2):
                    fw.op("dve", lambda e: e.memset(kcs[t][:], 0.0), writes=[bkcs[t]])
                fw.op("dve", lambda e: e.memset(hidg[:], 0.0), writes=[bhid])

                ck(5)
                xT_v = xT.rearrange("(c p) t -> p c t", p=128)
                xTo_v = xTo.rearrange("(c p) t -> p c t", p=128)

                def rstd_from(ps_ap, dst_ap, breads, bw):
                    fw.op("act", lambda e: e.activation(out=dst_ap, in_=ps_ap, func=AF.Sqrt, bias=epsT[:dst_ap.shape[0], :], scale=1.0 / DM),
                          reads=breads + [bconst], writes=[bw])
                    fw.op("dve", lambda e: e.reciprocal(out=dst_ap, in_=dst_ap), reads=[bw], writes=[bw])

                def proj_K(slab, ct, rhs_fn, n, breads):
                    t, b = slab
                    pb, bpb = nb()
                    tv = t[:].rearrange("p (t c n) -> p t c n", t=2, c=16)
                    for c in range(16):
                        mm(pb[:, 0:n], tv[:, ct, c, :], rhs_fn(c), c == 0, c == 15, [b] + breads, [bpb])
                    return pb, bpb

                def proj_V(slab, lhs_fn, breads, ncols=256):
                    t, b = slab
                    pb, bpb = nb()
                    tv = t[:].rearrange("p (c n) -> p c n", c=16)
                    for c in range(16):
                        mm(pb[:, 0:ncols], lhs_fn(c), tv[:, c, 0:ncols], c == 0, c == 15, [b] + breads, [bpb])
                    return pb, bpb

                def rope(pb0, bp0, pb1, bp1, rB, brBx, tab, btab, n, dst, bdst, pre):
                    a1, a2, t1, t2 = rtmp[:, 0, 0:n], rtmp[:, 1, 0:n], rtmp[:, 2, 0:n], rtmp[:, 3, 0:n]
                    fw.op("dve", lambda e: e.scalar_tensor_tensor(out=a1, in0=pb0[:, 0:n], scalar=pre, in1=rB, op0=ALU.mult, op1=ALU.mult),
                          reads=[bp0, brBx], writes=[brt])
                    fw.op("dve", lambda e: e.scalar_tensor_tensor(out=a2, in0=pb1[:, 0:n], scalar=pre, in1=rB, op0=ALU.mult, op1=ALU.mult),
                          reads=[bp1, brBx], writes=[brt])
                    fw.op("dve", lambda e: e.tensor_tensor(out=t1, in0=a1, in1=tab[:, 0, 0:n], op=ALU.mult), reads=[brt, btab], writes=[brt])
                    fw.op("dve", lambda e: e.tensor_tensor(out=t2, in0=a2, in1=tab[:, 1, 0:n], op=ALU.mult), reads=[brt, btab], writes=[brt])
                    fw.op("dve", lambda e: e.tensor_tensor(out=dst[:, 0, 0:n], in0=t1, in1=t2, op=ALU.subtract), reads=[brt], writes=[bdst])
                    fw.op("dve", lambda e: e.tensor_tensor(out=t1, in0=a2, in1=tab[:, 0, 0:n], op=ALU.mult), reads=[brt, btab], writes=[brt])
                    fw.op("dve", lambda e: e.tensor_tensor(out=t2, in0=a1, in1=tab[:, 1, 0:n], op=ALU.mult), reads=[brt, btab], writes=[brt])
                    fw.op("dve", lambda e: e.tensor_tensor(out=dst[:, 1, 0:n], in0=t1, in1=t2, op=ALU.add), reads=[brt], writes=[bdst])

                for i in range(N_RUN_STEPS):
                    slot = i % 2
                    set_rot(range(8))
                    psB, bpsB = nb()
                    psT, bpsT = nb()
                    for qtr in range(4):
                        st, bst, dsm = xst[xst_n[0] % 2]; xst_n[0] += 1
                        fw.dma("sp", dsm, st[:], xT_v[:, 4 * qtr:4 * qtr + 4, 512 * i:512 * i + 512], writes=[bst])
                        fw.op("dve", lambda e: e.tensor_copy(out=xb[:, 4 * qtr:4 * qtr + 4, :], in_=st[:]), reads=[bst], writes=[bxb])
                        fw.op("act", lambda e: e.activation(out=xsq[:], in_=st[:], func=AF.Square), reads=[bst], writes=[bxsq])
                        for c in range(4):
                            first, last = (qtr == 0 and c == 0), (qtr == 3 and c == 3)
                            mm(psB[:, :], ones[:], xsq[:, c, :], first, last, [bconst, bxsq], [bpsB])
                        for r in range(4):
                            for c in range(4):
                                first, last = (qtr == 0 and c == 0 and r == 0), (qtr == 3 and c == 3 and r == 3)
                                mm(psT[:, r:r + 1], xsq[:, c, 128 * r:128 * r + 128], ones[:, 0:1], first, last, [bconst, bxsq], [bpsT])
                    rstd_from(psB[:, :], rstdB[:], [bpsB], brB)
                    rstd_from(psT[:, 0:4], rstdT[:], [bpsT], brT)
                    psB, bpsB = nb()
                    for qtr in range(4):
                        st, bst, dsm = xst[xst_n[0] % 2]; xst_n[0] += 1
                        stv = st[:].rearrange("p a (b t) -> p (a b) t", t=128)[:, 0:4, :]
                        fw.dma("sp", dsm, stv, xTo_v[:, 4 * qtr:4 * qtr + 4, 128 * i:128 * i + 128], writes=[bst])
                        fw.op("dve", lambda e: e.tensor_copy(out=xob[:, 4 * qtr:4 * qtr + 4, :], in_=stv), reads=[bst], writes=[bxob])
                        fw.op("act", lambda e: e.activation(out=xsq[:, 0, :].rearrange("p (a t) -> p a t", t=128), in_=stv, func=AF.Square),
                              reads=[bst], writes=[bxsq])
                        for c in range(4):
                            first, last = (qtr == 0 and c == 0), (qtr == 3 and c == 3)
                            mm(psB[:, 0:128], ones[:], xsq[:, 0, 128 * c:128 * c + 128], first, False, [bconst, bxsq], [bpsB])
                        for c in range(4):
                            mm(psB[:, 256:257], xsq[:, 0, 128 * c:128 * c + 128], ones[:, 0:1], False, (qtr == 3 and c == 3), [bconst, bxsq], [bpsB])
                    rstd_from(psB[:, 0:128], rstdBo[:], [bpsB], brBo)
                    rstd_from(psB[:, 256:257], rstdTo[:], [bpsB], brTo)
                    ck(6)
                    fw.dma("sp", d_cs, cs[:, 0, :], D["cosT"][:, 512 * i:512 * i + 512], writes=[bcs])
                    fw.dma("sp", d_cs, cs[:, 1, :], D["sinT"][:, 512 * i:512 * i + 512], writes=[bcs])
                    fw.dma("sp", d_cs, cso[:, 0, :], D["cosTo"][:, 128 * i:128 * i + 128], writes=[bcso])
                    fw.dma("sp", d_cs, cso[:, 1, :], D["sinTo"][:, 128 * i:128 * i + 128], writes=[bcso])

                    xrhs = lambda c: xb[:, c, :]
                    xorhs = lambda c: xob[:, c, :]

                    for nm, t in (("KC", 0), ("VC", 1)):
                        sl = load_slab(WIN_IDX[nm])
                        for g in range(2):
                            pb, bpb = proj_K(sl, g, xrhs, 512, [bxb])
                            fw.op("dve", lambda e: e.tensor_tensor(out=kcs[t][:, g, 16:528], in0=pb[:, :], in1=rstdB[:], op=ALU.mult),
                                  reads=[bpb, brB], writes=[bkcs[t]])
                    sl = load_slab(WIN_IDX["KS"])
                    for g in range(2):
                        pb, bpb = proj_K(sl, g, xrhs, 512, [bxb])
                        fw.op("dve", lambda e: e.tensor_tensor(out=ksf[:, g, :], in0=pb[:, :], in1=rstdB[:], op=ALU.mult),
                              reads=[bpb, brB], writes=[bksf[g]])
                        fw.dma("pool", None, ks_scr[g, i], ksf[:, g, :], reads=[bksf[g]], writes=[bksc[g]])
                    sl = load_slab(WIN_IDX["KW"])
                    for g in range(2):
                        pb, bpb = proj_K(sl, g, xrhs, 512, [bxb])
                        fw.op("dve", lambda e: e.tensor_tensor(out=kwT[:, g, slot, :], in0=pb[:, :], in1=rstdB[:], op=ALU.mult),
                              reads=[bpb, brB], writes=[bkw[g][slot]])
                    sl = load_slab(WIN_IDX["VS"])
                    for r in range(4):
                        pb, bpb = proj_V(sl, lambda c: xb[:, c, 128 * r:128 * r + 128], [bxb])
                        fw.op("act", lambda e: e.activation(out=vsf[:, :, r, 0:128], in_=pb[:, 0:256].rearrange("p (g d) -> p g d", g=2),
                                                            func=AF.Copy, scale=rstdT[:, r:r + 1]), reads=[bpb, brT], writes=[bvsf])
                    for g in range(2):
                        fw.dma("pool", None, vs_scr[g, i], vsf[:, g, :, :].rearrange("p r d -> p (r d)"), reads=[bvsf], writes=[bvsc[g]])
                    sl = load_slab(WIN_IDX["VW"])
                    for r in range(4):
                        pb, bpb = proj_V(sl, lambda c: xb[:, c, 128 * r:128 * r + 128], [bxb])
                        fw.op("act", lambda e: e.activation(out=vw[:, slot, :, r, 0:128], in_=pb[:, 0:256].rearrange("p (g d) -> p g d", g=2),
                                                            func=AF.Copy, scale=rstdT[:, r:r + 1]), reads=[bpb, brT], writes=[bvw[slot]])

                    ck(7)
                    qd = 32 * (i % 4)
                    chn = i // 4
                    for t in range(2):
                        w1lo = load_slab(OFF_W1 + 2 * t, half=True)
                        w1hi = load_slab(OFF_W1 + 2 * t + 1, half=True)
                        for g in range(2):
                            pb, bpb = nb()
                            for l in range(32):
                                wt, bwt = (w1lo if l < 16 else w1hi)
                                lv = wt[:, 0:2048].rearrange("p (l h) -> p l h", h=128)[:, l % 16, :]
                                mm(pb[:, 0:32], lv, kcs[t][:, g, l:l + 497:16], l == 0, l == 31, [bwt, bkcs[t]], [bpb])
                            bia = biask if t == 0 else biasv
                            xg, x2, x3 = gtmp[:, 0, :], gtmp[:, 1, :], gtmp[:, 2, :]
                            fw.op("act", lambda e: e.activation(out=xg, in_=pb[:, 0:32], func=AF.Identity, bias=bia[:], scale=1.0),
                                  reads=[bpb, bconst], writes=[bgt])
                            fw.op("dve", lambda e: e.tensor_tensor(out=x2, in0=xg, in1=xg, op=ALU.mult), reads=[bgt], writes=[bgt])
                            fw.op("dve", lambda e: e.tensor_scalar(out=x2, in0=x2, scalar1=0.044715, scalar2=1.0, op0=ALU.mult, op1=ALU.add),
                                  reads=[bgt], writes=[bgt])
                            fw.op("dve", lambda e: e.tensor_tensor(out=x3, in0=x2, in1=xg, op=ALU.mult), reads=[bgt], writes=[bgt])
                            fw.op("act", lambda e: e.activation(out=x3, in_=x3, func=AF.Sigmoid, scale=1.5957691216057308), reads=[bgt], writes=[bgt])
                            fw.op("dve", lambda e: e.tensor_tensor(out=hidg[:, t, g, qd:qd + 32], in0=x3, in1=xg, op=ALU.mult), reads=[bgt], writes=[bhid])
                            pb, bpb = nb()
                            if t == 0:
                                mm(pb[:, 0:32], w2k[:], hidg[:, 0, g, qd:qd + 32], True, True, [bconst, bhid], [bpb])
                                fw.op("dve", lambda e: e.tensor_copy(out=kcT[:, g, 32 * i:32 * i + 32], in_=pb[:, 0:32]), reads=[bpb], writes=[bkc[g]])
                            else:
                                mm(pb[:, 0:128], hidg[:, 1, g, :], w2v[:], True, True, [bconst, bhid], [bpb])
                                fw.op("dve", lambda e: e.tensor_copy(out=vcm[qd:qd + 32, g, chn, 0:128], in_=pb[qd:qd + 32, 0:128]), reads=[bpb], writes=[bvcm[g]])
                        fw.op("dve", lambda e: e.tensor_copy(out=kcs[t][:, :, 0:16], in_=kcs[t][:, :, 512:528]), reads=[bkcs[t]], writes=[bkcs[t]])

                    ck(8)
                    for s in range(4):
                        sl = load_slab(WIN_IDX["QA%d" % s])
                        for ct in range(2):
                            pb, bpb = proj_K(sl, ct, xorhs, 128, [bxob])
                            fw.op("dve", lambda e: e.scalar_tensor_tensor(out=qaT[:, 2 * s + ct, :], in0=pb[:, 0:128], scalar=float(128 ** -0.5),
                                                                          in1=rstdBo[:], op0=ALU.mult, op1=ALU.mult), reads=[bpb, brBo], writes=[bqa])
                    pb, bpb = nb()
                    for c in range(16):
                        mm(pb[:, 0:24], xob[:, c, :], wg[:, c, :], c == 0, c == 15, [bxob, bconst], [bpb])
                    fw.op("act", lambda e: e.activation(out=gat[:], in_=pb[:, 0:24], func=AF.Sigmoid, scale=rstdTo[:, 0:1]), reads=[bpb, brTo], writes=[bgat])

                    ck(9)
                    set_rot([4, 5, 6, 7])
                    for g in range(2):
                        qg = qaT[:, 4 * g:4 * g + 4, :].rearrange("p h q -> p (h q)")
                        accs = [(banks[0], bbank[0]), (banks[1], bbank[1])]
                        lb, blb = banks[2], bbank[2]
                        nch = i // 4 + 1
                        for m in range(nch):
                            npop = 128 if m < nch - 1 else 32 * (i % 4 + 1)
                            pb, bpb = nb()
                            biases = []
                            if i == 0:
                                biases.append((shb[32:64, 160:288], bc[32:64, g, 1, :]))
                            else:
                                cha, chb = (i - 1) // 4, i // 4
                                s0 = 32 * ((i - 1) % 4)
                                if cha == chb:
                                    if m == cha:
                                        biases.append((shb[0:64, 128 - s0:256 - s0], bc[0:64, g, 0, :]))
                                else:
                                    if m == cha:
                                        biases.append((shb[0:32, 32:160], bc[0:32, g, 0, :]))
                                    if m == chb:
                                        biases.append((shb[32:64, 160:288], bc[32:64, g, 0, :]))
                            if i >= 1 and m == 0:
                                biases.append((shb[0:32, 128:256], dumb[0:32, :]))
                            mm(pb[:, :], kcT[:, g, 128 * m:128 * m + 128], qg, True, len(biases) == 0, [bkc[g], bqa], [bpb])
                            for bi, (l_, r_) in enumerate(biases):
                                mm(pb[:, :], l_, r_, False, bi == len(biases) - 1, [bconst], [bpb])
                            pt, bpt = put[put_n[0] % 3]; put_n[0] += 1
                            fw.op("act", lambda e: e.activation(out=pt[0:npop, :], in_=pb[0:npop, :], func=AF.Exp), reads=[bpb], writes=[bpt])
                            for h in range(4):
                                ab, bab = accs[h // 2]
                                mm(ab[:, 256 * (h % 2):256 * (h % 2) + 256], pt[0:npop, 128 * h:128 * h + 128], vcm[0:npop, g, m, :],
                                   m == 0 and h % 2 == 0, m == nch - 1 and h % 2 == 1, [bpt, bvcm[g]], [bab])
                                mm(lb[:, h:h + 1], pt[0:npop, 128 * h:128 * h + 128], ones[0:npop, 0:1], m == 0 and h == 0, m == nch - 1 and h == 3, [bpt, bconst], [blb])
                        ck(91)
                        rl = small[:, 0:4]
                        fw.op("dve", lambda e: e.tensor_scalar(out=rl, in0=lb[:, 0:4], scalar1=1e-30, scalar2=None, op0=ALU.max), reads=[blb], writes=[bsmall])
                        fw.op("dve", lambda e: e.reciprocal(out=rl, in_=rl), reads=[bsmall], writes=[bsmall])
                        cf = small[:, 4:8]
                        fw.op("dve", lambda e: e.tensor_tensor(out=cf, in0=rl, in1=gat[:, 12 * g:12 * g + 12:3], op=ALU.mult), reads=[bsmall, bgat], writes=[bsmall])
                        for h in range(4):
                            ab, bab = accs[h // 2]
                            oc = 256 * (h % 2)
                            fw.op("dve", lambda e: e.tensor_scalar(out=Of[:, 512 * g + 128 * h:512 * g + 128 * h + 128], in0=ab[:, oc:oc + 128],
                                                                   scalar1=small[:, 4 + h:5 + h], scalar2=None, op0=ALU.mult), reads=[bab, bsmall], writes=[bOf])
                            if h == 0:
                                fw.op("dve", lambda e: e.tensor_scalar(out=impf[:], in0=ab[:, oc + 128:oc + 256], scalar1=small[:, h:h + 1], scalar2=None,
                                                                       op0=ALU.mult), reads=[bab, bsmall], writes=[bimp])
                            else:
                                fw.op("dve", lambda e: e.scalar_tensor_tensor(out=impf[:], in0=ab[:, oc + 128:oc + 256], scalar=small[:, h:h + 1],
                                                                              in1=impf[:], op0=ALU.mult, op1=ALU.add), reads=[bab, bsmall, bimp], writes=[bimp])
                        ck(92)
                        w0 = max(0, 8 * i - 8)
                        w1 = 8 * i + 8
                        u0 = w0 - (8 * i - 8)
                        if w1 < 128:
                            fw.op("pool", lambda e: e.memset(imp1[:, w1:128], -1.0), writes=[bimp1])
                        if w0 > 0:
                            fw.op("dve", lambda e: e.tensor_copy(out=imp1[:, 0:w0], in_=impf[:, 0:w0]), reads=[bimp], writes=[bimp1])
                        fw.op("dve", lambda e: e.tensor_tensor(out=imp1[:, w0:w1], in0=impf[:, w0:w1], in1=a16[:, u0:16], op=ALU.mult),
                              reads=[bimp, bconst], writes=[bimp1])
                        fw.op("dve", lambda e: e.tensor_tensor(out=imp1[:, w0:w1], in0=imp1[:, w0:w1], in1=b16[:, u0:16], op=ALU.add),
                              reads=[bconst, bimp1], writes=[bimp1])
                        fw.op("dve", lambda e: e.memset(imp1[:, 0:1], 1e4), writes=[bimp1])
                        fw.op("dve", lambda e: e.max(out=top8[:, 0:8], in_=imp1[:]), reads=[bimp1], writes=[btop])
                        fw.op("dve", lambda e: e.match_replace(out=imp2[:], in_to_replace=top8[:, 0:8], in_values=imp1[:], imm_value=-2.0),
                              reads=[bimp1, btop], writes=[bimp2])
                        fw.op("dve", lambda e: e.max(out=top8[:, 8:16], in_=imp2[:]), reads=[bimp2], writes=[btop])
                        fw.op("dve", lambda e: e.tensor_scalar(out=small[:, 8:9], in0=top8[:, 15:16], scalar1=0.0, scalar2=None, op0=ALU.max),
                              reads=[btop], writes=[bsmall])
                        fw.op("dve", lambda e: e.tensor_scalar(out=nsel[:], in0=imp1[:], scalar1=small[:, 8:9], scalar2=NEGM, op0=ALU.is_lt, op1=ALU.mult),
                              reads=[bimp1, bsmall], writes=[bnsel])
                        ck(93)
                        pb, bpb = nb()
                        pbv = pb[:, 0:64].bitcast(BF16)
                        tr(pbv, nsel[:], [bnsel], [bpb])
                        fw.op("dve", lambda e: e.tensor_copy(out=NS[:, g, :].rearrange("p (h q) -> p h q", h=4),
                                                             in_=pbv[:, None, :].broadcast_to([128, 4, 128])), reads=[bpb], writes=[bNS[g]])
                        ck(94)
                        for br in ("slc", "win"):
                            accs = [(banks[0], bbank[0]), (banks[1], bbank[1])] if br == "slc" else [(banks[2], bbank[2]), (banks[3], bbank[3])]
                            if br == "slc":
                                tiles = list(range(4 * i + 4))
                            else:
                                tiles = [r for r in range(8) if (i > 0 or r >= 4)]
                            cur_chunk = [None, None, None]
                            for ti, kt in enumerate(tiles):
                                if br == "slc":
                                    r = kt - 4 * (i - 1)
                                    ch, rl_ = divmod(kt, 4)
                                    if ch == i:
                                        kT_ap, v_ap, bkv = ksf[:, g, 128 * rl_:128 * rl_ + 128], vsf[:, g, rl_, 0:129], [bksf[g], bvsf]
                                    else:
                                        if cur_chunk[0] != ch:
                                            kt_, vt_, bk_, dsm = ksl[ksl_n[0] % 3]; ksl_n[0] += 1
                                            fw.dma("sp", dsm, kt_[:], ks_scr[g, ch], reads=[bksc[g]], writes=[bk_])
                                            fw.dma("sp", dsm, vt_[:].rearrange("p r d -> p (r d)"), vs_scr[g, ch], reads=[bvsc[g]], writes=[bk_])
                                            cur_chunk = [ch, (kt_, vt_), bk_]
                                        kt_, vt_ = cur_chunk[1]
                                        kT_ap, v_ap, bkv = kt_[:, 128 * rl_:128 * rl_ + 128], vt_[:, rl_, 0:129], [cur_chunk[2]]
                                else:
                                    r = kt
                                    sl_ = (i - 1) % 2 if r < 4 else i % 2
                                    rl_ = r % 4
                                    kT_ap, v_ap = kwT[:, g, sl_, 128 * rl_:128 * rl_ + 128], vw[:, sl_, g, rl_, 0:129]
                                    bkv = [bkw[g][sl_], bvw[sl_]]
                                pb, bpb = nb()
                                extra = []
                                if br == "slc":
                                    a_, m_ = divmod(kt, 16)
                                    hb_, ap_ = divmod(a_, 2)
                                    extra.append((e32[64 * hb_:64 * hb_ + 64, ap_, 128 * m_:128 * m_ + 128], NS[64 * hb_:64 * hb_ + 64, g, :], [bNS[g]]))
                                    if 3 <= r <= 7:
                                        extra.append((ident[:], smb[:, r - 3, None, :].broadcast_to([128, 4, 128]), []))
                                else:
                                    extra.append((ident[:], wmb[:, r, None, :].broadcast_to([128, 4, 128]), []))
                                if 3 <= r <= 7:
                                    extra.append((ident[:], td[:, g, r - 3, :], []))
                                mm(pb[:, :], kT_ap, qg, True, False, bkv + [bqa], [bpb])
                                for bi, (l_, r_, br_) in enumerate(extra):
                                    ro = pb[:, :] if len(r_.shape) == 2 else pb[:, :].rearrange("p (h q) -> p h q", h=4)
                                    mm(ro, l_, r_, False, bi == len(extra) - 1, [bconst] + br_, [bpb])
                                pt, bpt = put[put_n[0] % 3]; put_n[0] += 1
                                fw.op("act", lambda e: e.activation(out=pt[:], in_=pb[:, :], func=AF.Exp), reads=[bpb], writes=[bpt])
                                for h in range(4):
                                    ab, bab = accs[h // 2]
                                    oc = 129 * (h % 2)
                                    mm(ab[:, oc:oc + 129], pt[:, 128 * h:128 * h + 128], v_ap, ti == 0 and h % 2 == 0, ti == len(tiles) - 1 and h % 2 == 1, [bpt] + bkv, [bab])
                            gi = 1 if br == "slc" else 2
                            lall = small[:, 16:20]
                            for h in range(4):
                                ab, bab = accs[h // 2]
                                oc = 129 * (h % 2)
                                fw.op("dve", lambda e: e.reciprocal(out=small[:, 16 + h:17 + h], in_=ab[:, oc + 128:oc + 129]), reads=[bab], writes=[bsmall])
                            fw.op("dve", lambda e: e.tensor_tensor(out=small[:, 20:24], in0=lall, in1=gat[:, 12 * g + gi:12 * g + 12:3], op=ALU.mult),
                                  reads=[bsmall, bgat], writes=[bsmall])
                            for h in range(4):
                                ab, bab = accs[h // 2]
                                oc = 129 * (h % 2)
                                dst = Of[:, 512 * g + 128 * h:512 * g + 128 * h + 128]
                                fw.op("dve", lambda e: e.scalar_tensor_tensor(out=dst, in0=ab[:, oc:oc + 128], scalar=small[:, 20 + h:21 + h], in1=dst,
                                                                              op0=ALU.mult, op1=ALU.add), reads=[bab, bsmall, bOf], writes=[bOf])
                    fw.op("act", lambda e: e.copy(out=Ob[:, 0:1024], in_=Of[:]), reads=[bOf], writes=[bOb])

                    ck(10)
                    set_rot([2, 3, 4, 5, 6, 7])
                    for h in range(4):
                        slk = load_slab(WIN_IDX["KR%d" % h])
                        p0, b0 = proj_K(slk, 0, xrhs, 512, [bxb])
                        p1, b1 = proj_K(slk, 1, xrhs, 512, [bxb])
                        rope(p0, b0, p1, b1, rstdB[:], brB, cs, bcs, 512, krT, bkr, 1.0 / 16.0)
                        for r in range(4):
                            pb, bpb = nb()
                            pbv = pb[:, 0:128].bitcast(BF16)
                            for e_ in range(2):
                                tr(pbv[:, 128 * e_:128 * e_ + 128], krT[:, e_, 128 * r:128 * r + 128], [bkr], [bpb])
                            fw.op("act", lambda e: e.activation(out=kz[:, r, :, :], in_=pbv.rearrange("p (e d) -> p e d", e=2), func=AF.Copy,
                                                                scale=zeta[:, 4 * h + r:4 * h + r + 1]), reads=[bpb, bconst], writes=[bkz])
                        slv = load_slab(WIN_IDX["VR%d" % h])
                        for r in range(4):
                            pb, bpb = proj_V(slv, lambda c: xb[:, c, 128 * r:128 * r + 128], [bxb])
                            fw.op("act", lambda e: e.activation(out=vr[:, r, :], in_=pb[:, 0:256], func=AF.Copy, scale=rstdT[:, r:r + 1]),
                                  reads=[bpb, brT], writes=[bvr])
                        slq = load_slab(WIN_IDX["QR%d" % h])
                        p0, b0 = proj_K(slq, 0, xorhs, 128, [bxob])
                        p1, b1 = proj_K(slq, 1, xorhs, 128, [bxob])
                        rope(p0, b0, p1, b1, rstdBo[:], brBo, cso, bcso, 128, qrT, bqr, 1.0)
                        fw.op("dve", lambda e: e.tensor_tensor(out=qrX[:], in0=qrT[:], in1=xi[:, h, None, :].broadcast_to([128, 2, 128]), op=ALU.mult),
                              reads=[bqr, bconst], writes=[bqrx])
                        slg = load_slab(WIN_IDX["GR%d" % h])
                        pb, bpb = proj_V(slg, lambda c: xob[:, c, :], [bxob])
                        fw.op("act", lambda e: e.activation(out=sg[:], in_=pb[:, 0:256], func=AF.Silu, scale=rstdTo[:, 0:1]), reads=[bpb, brTo], writes=[bsg])
                        pb, bpb = nb()
                        for r in range(4):
                            for e_ in range(2):
                                mm(pb[:, 128 * r:128 * r + 128], krT[:, e_, 128 * r:128 * r + 128], qrT[:, e_, :], e_ == 0, e_ == 1, [bkr, bqr], [bpb])
                        fw.op("dve", lambda e: e.tensor_tensor(out=sdT[:], in0=pb[:, :].rearrange("p (r q) -> p r q", r=4), in1=dec[:, h, :, :], op=ALU.mult),
                              reads=[bpb, bconst], writes=[bsd])
                        ab, bab = banks[h % 2], bbank[h % 2]
                        for r in range(4):
                            mm(ab[:, 0:256], sdT[:, r, :], vr[:, r, :], r == 0, False, [bsd, bvr], [bab])
                        for e_ in range(2):
                            mm(ab[:, 0:256], qrX[:, e_, :], Rb[:, h, e_, :], False, e_ == 1, [bqrx, bRb[h]], [bab])
                        fw.op("dve", lambda e: e.tensor_copy(out=gn[:], in_=ab[:, 0:256]), reads=[bab], writes=[bgn])
                        fw.op("dve", lambda e: e.tensor_reduce(out=small[:, 32:33], in_=gn[:], axis=AX.X, op=ALU.add), reads=[bgn], writes=[bsmall])
                        fw.op("act", lambda e: e.activation(out=sdT[:].rearrange("p r q -> p (r q)")[:, 0:256], in_=gn[:], func=AF.Square, accum_out=small[:, 33:34]),
                              reads=[bgn, bsmall], writes=[bsd, bsmall])
                        fw.op("dve", lambda e: e.tensor_scalar(out=small[:, 34:36], in0=small[:, 32:34], scalar1=1.0 / 256, scalar2=None, op0=ALU.mult),
                              reads=[bsmall], writes=[bsmall])
                        fw.op("dve", lambda e: e.tensor_tensor(out=small[:, 36:37], in0=small[:, 34:35], in1=small[:, 34:35], op=ALU.mult), reads=[bsmall], writes=[bsmall])
                        fw.op("dve", lambda e: e.tensor_tensor(out=small[:, 37:38], in0=small[:, 35:36], in1=small[:, 36:37], op=ALU.subtract), reads=[bsmall], writes=[bsmall])
                        fw.op("act", lambda e: e.activation(out=small[:, 38:39], in_=small[:, 37:38], func=AF.Sqrt, bias=epsT[:], scale=1.0), reads=[bsmall, bconst], writes=[bsmall])
                        fw.op("dve", lambda e: e.reciprocal(out=small[:, 39:40], in_=small[:, 38:39]), reads=[bsmall], writes=[bsmall])
                        fw.op("dve", lambda e: e.tensor_scalar(out=gn[:], in0=gn[:], scalar1=small[:, 34:35], scalar2=small[:, 39:40], op0=ALU.subtract, op1=ALU.mult),
                              reads=[bgn, bsmall], writes=[bgn])
                        fw.op("dve", lambda e: e.tensor_tensor(out=Ob[:, 1024 + 256 * h:1280 + 256 * h], in0=gn[:], in1=sg[:], op=ALU.mult), reads=[bgn, bsg], writes=[bOb])
                        for e_ in range(2):
                            pb, bpb = nb()
                            for r in range(4):
                                mm(pb[:, 0:256], kz[:, r, e_, :], vr[:, r, :], r == 0, r == 3, [bkz, bvr], [bpb])
                            fw.op("dve", lambda e: e.scalar_tensor_tensor(out=R[:, h, e_, :], in0=R[:, h, e_, :], scalar=G512[h], in1=pb[:, 0:256],
                                                                          op0=ALU.mult, op1=ALU.add), reads=[bpb], writes=[bR[h]])
                        fw.op("act", lambda e: e.copy(out=Rb[:, h, :, :], in_=R[:, h, :, :]), reads=[bR[h]], writes=[bRb[h]])

                    ck(11)
                    if DEBUG:
                        fw.op("act", lambda e: e.copy(out=dbgt[:], in_=Ob[:]), reads=[bOb], writes=[bdbg])
                        fw.dma("pool", d_out, dbg_o[128 * i:128 * i + 128, :], dbgt[:], reads=[bdbg])
                    for cgp in range(4):
                        pb, bpb = nb()
                        pbv = pb[:, 0:256].bitcast(BF16)
                        for c in range(4):
                            tr(pbv[:, 128 * c:128 * c + 128], Ob[:, 128 * (4 * cgp + c):128 * (4 * cgp + c) + 128], [bOb], [bpb])
                        fw.op("dve", lambda e: e.tensor_copy(out=oT[:, 4 * cgp:4 * cgp + 4, :], in_=pbv.rearrange("p (c t) -> p c t", c=4)), reads=[bpb], writes=[boT])
                    fw.dma("pool", d_oT, oT_scr[i], oT[:].rearrange("p c t -> p (c t)"), reads=[boT], writes=[boTs])
            ca.close()
            fw.barrier()

            with ExitStack() as pbk:
                slabs = [(sb("bslab%d" % k, [128, 4096], BF16, pbk), Buf(), (lambda *a: None)("bslab%d" % k)) for k in range(2)]
                slab_n = [0]

                def load_slab2(sid):
                    t, b, dsm = slabs[slab_n[0] % 2]
                    slab_n[0] += 1
                    fw.dma("sp", dsm, t[:], wscr[sid], reads=bwscr, writes=[b])
                    return t, b

                x1 = sb("x1", [128, 4, DM], F32, pbk); bx1 = [Buf() for _ in range(4)]
                yb = sb("yb", [128, 4, DM], F32, pbk); byb = [Buf() for _ in range(4)]
                hT = sb("hT", [128, 16, 512], BF16, pbk); bhT = Buf()
                uT = sb("uT", [128, 64, 512], BF16, pbk); buT = Buf()
                gB = sb("gB", [128, 2, DM], F32, pbk); bgB = Buf()
                hb = sb("hb", [128, DM], BF16, pbk); bhb = Buf()
                rel_ = sb("rel_", [128, 512], F32, pbk); brel = Buf()
                st = sb("bst", [128, 64], F32, pbk); bst_ = Buf()
                d_x = (lambda *a: None)("bx"); d_g = (lambda *a: None)("bg"); d_o = (lambda *a: None)("bo")
                fw.dma("sp", d_g, gB[:, 0, :], D["gpostB"], writes=[bgB])
                fw.dma("sp", d_g, gB[:, 1, :], D["gpost2B"], writes=[bgB])

                def row_rstd(src_fn, bsrc, col):
                    for q4 in range(4):
                        fw.op("act", lambda e: e.activation(out=rel_[:], in_=src_fn(q4), func=AF.Square, accum_out=st[:, 32 + q4:33 + q4]),
                              reads=[bsrc], writes=[brel, bst_])
                    fw.op("dve", lambda e: e.tensor_reduce(out=st[:, 36:37], in_=st[:, 32:36], axis=AX.X, op=ALU.add), reads=[bst_], writes=[bst_])
                    fw.op("act", lambda e: e.activation(out=st[:, 37:38], in_=st[:, 36:37], func=AF.Sqrt, bias=epsT[:], scale=1.0 / DM), reads=[bst_, bconst], writes=[bst_])
                    fw.op("dve", lambda e: e.reciprocal(out=st[:, col:col + 1], in_=st[:, 37:38]), reads=[bst_], writes=[bst_])

                for cb in range(4 if RUN_B else 0):
                    set_rot(range(8))
                    for s in range(4):
                        fw.dma("sp", d_x, hT[:, :, 128 * s:128 * s + 128], oT_scr[4 * cb + s].rearrange("p (c t) -> p c t", c=16), reads=[boTs], writes=[bhT])
                    for tt in range(4):
                        fw.dma("sp", d_x, x1[:, tt, :], xo[512 * cb + 128 * tt:512 * cb + 128 * tt + 128, :], writes=[bx1[tt]])
                    for cg in range(8):
                        sl, bsl = load_slab2(OFF_WOUT + cg)
                        slv = sl[:].rearrange("p (c n) -> p c n", c=16)
                        for tt in range(4):
                            pb, bpb = nb()
                            for c in range(16):
                                mm(pb[:, 0:256], hT[:, c, 128 * tt:128 * tt + 128], slv[:, c, :], c == 0, c == 15, [bhT, bsl], [bpb])
                            fw.op("act" if tt % 2 else "dve", (lambda e: e.copy(out=yb[:, tt, 256 * cg:256 * cg + 256], in_=pb[:, 0:256])) if tt % 2 else
                                  (lambda e: e.tensor_copy(out=yb[:, tt, 256 * cg:256 * cg + 256], in_=pb[:, 0:256])), reads=[bpb], writes=[byb[tt]])
                    for tt in range(4):
                        row_rstd(lambda q4: yb[:, tt, 512 * q4:512 * q4 + 512], byb[tt], tt)
                        fw.op("dve", lambda e: e.scalar_tensor_tensor(out=yb[:, tt, :], in0=yb[:, tt, :], scalar=st[:, tt:tt + 1], in1=gB[:, 0, :],
                                                                      op0=ALU.mult, op1=ALU.mult), reads=[byb[tt], bst_, bgB], writes=[byb[tt]])
                        fw.op("pool", lambda e: e.tensor_tensor(out=x1[:, tt, :], in0=x1[:, tt, :], in1=yb[:, tt, :], op=ALU.add), reads=[byb[tt], bx1[tt]], writes=[bx1[tt]])
                        if DEBUG:
                            r0 = 512 * cb + 128 * tt
                            fw.dma("pool", d_out, dbg_x1[r0:r0 + 128, :], x1[:, tt, :], reads=[bx1[tt]])
                        row_rstd(lambda q4: x1[:, tt, 512 * q4:512 * q4 + 512], bx1[tt], 4 + tt)
                        fw.op("act", lambda e: e.activation(out=hb[:], in_=x1[:, tt, :], func=AF.Copy, scale=st[:, 4 + tt:5 + tt]), reads=[bx1[tt], bst_], writes=[bhb])
                        for cgp in range(4):
                            pb, bpb = nb()
                            pbv = pb[:, 0:256].bitcast(BF16)
                            for c in range(4):
                                tr(pbv[:, 128 * c:128 * c + 128], hb[:, 128 * (4 * cgp + c):128 * (4 * cgp + c) + 128], [bhb], [bpb])
                            fw.op("dve", lambda e: e.tensor_copy(out=hT[:, 4 * cgp:4 * cgp + 4, 128 * tt:128 * tt + 128], in_=pbv.rearrange("p (c t) -> p c t", c=4)),
                                  reads=[bpb], writes=[bhT])
                    for s in range(32):
                        sl, bsl = load_slab2(OFF_WUP + s)
                        slv = sl[:].rearrange("p (t c n) -> p t c n", t=2, c=16)
                        for ct in range(2):
                            pb, bpb = nb()
                            for c in range(16):
                                mm(pb[:, :], slv[:, ct, c, :], hT[:, c, :], c == 0, c == 15, [bhT, bsl], [bpb])
                            fw.op("act", lambda e: e.activation(out=rel_[:], in_=pb[:, :], func=AF.Relu), reads=[bpb], writes=[brel])
                            fw.op("dve" if ct == 0 else "pool", lambda e: e.tensor_tensor(out=uT[:, 2 * s + ct, :], in0=rel_[:], in1=rel_[:], op=ALU.mult), reads=[brel], writes=[buT])
                    set_rot([4, 5, 6, 7])
                    for cg in range(8):
                        accb = [(banks[k], bbank[k]) for k in range(4)]
                        for hq in range(4):
                            sl, bsl = load_slab2(OFF_WDN + cg * 4 + hq)
                            slv = sl[:].rearrange("p (c n) -> p c n", c=16)
                            for tt in range(4):
                                ab, bab = accb[tt]
                                for c in range(16):
                                    mm(ab[:, 0:256], uT[:, 16 * hq + c, 128 * tt:128 * tt + 128], slv[:, c, :], hq == 0 and c == 0, hq == 3 and c == 15, [buT, bsl], [bab])
                        for tt in range(4):
                            ab, bab = accb[tt]
                            fw.op("act" if tt % 2 else "dve", (lambda e: e.copy(out=yb[:, tt, 256 * cg:256 * cg + 256], in_=ab[:, 0:256])) if tt % 2 else
                                  (lambda e: e.tensor_copy(out=yb[:, tt, 256 * cg:256 * cg + 256], in_=ab[:, 0:256])), reads=[bab], writes=[byb[tt]])
                    for tt in range(4):
                        row_rstd(lambda q4: yb[:, tt, 512 * q4:512 * q4 + 512], byb[tt], 8 + tt)
                        fw.op("dve", lambda e: e.scalar_tensor_tensor(out=yb[:, tt, :], in0=yb[:, tt, :], scalar=st[:, 8 + tt:9 + tt], in1=gB[:, 1, :],
                                                                      op0=ALU.mult, op1=ALU.mult), reads=[byb[tt], bst_, bgB], writes=[byb[tt]])
                        fw.op("pool", lambda e: e.tensor_tensor(out=yb[:, tt, :], in0=yb[:, tt, :], in1=x1[:, tt, :], op=ALU.add), reads=[byb[tt], bx1[tt]], writes=[byb[tt]])
                        r0 = 512 * cb + 128 * tt
                        fw.dma("pool", d_out, out[r0:r0 + 128, :], yb[:, tt, :], reads=[byb[tt]])
        except _Stop:
            ca.close()
        for ssem in fw.store_sems:
            fw.eng["sp"].wait_ge(fw.sems[ssem], fw.cnt[ssem])
        print("bass instructions:", fw.n_inst, {k: v for k, v in fw.cnt.items() if not k.startswith("d_")})
    return nc


_NC_CACHE = {}


def kernel(**inputs):
    x = np.asarray(inputs["x"], np.float32)
    f = lambda k: np.ascontiguousarray(np.asarray(inputs[k], np.float32)[0])
    cosT, sinT = _rope_tables()
    t5 = np.asarray(inputs["t5_bias"], np.float32)
    shared = {
        "w_in": f("w_in"), "w_out": f("w_out"), "w_up": f("w_up"), "w_down": f("w_down"),
        "gpre": np.ascontiguousarray(f("norm_mix_pre").reshape(16, 128).T),
        "gmlp": np.ascontiguousarray(f("norm_mlp_pre").reshape(16, 128).T),
        "gpostB": np.ascontiguousarray(np.broadcast_to(f("norm_mix_post")[None, :], (128, DM))),
        "gpost2B": np.ascontiguousarray(np.broadcast_to(f("norm_mlp_post")[None, :], (128, DM))),
        "w1k": f("cmp_w1_k"), "w1v": f("cmp_w1_v"),
        "peTk": np.ascontiguousarray(f("cmp_pe_k").T), "peTv": np.ascontiguousarray(f("cmp_pe_v").T),
        "b1k": f("cmp_b1_k").reshape(128, 1).copy(), "b1v": f("cmp_b1_v").reshape(128, 1).copy(),
        "w2k": f("cmp_w2_k"), "w2v": f("cmp_w2_v"),
        "cosT": cosT, "sinT": sinT,
    }
    in_maps = []
    own_idx = []
    for core in range(8):
        b, j = divmod(core, 4)
        idx = (np.arange(16)[:, None] * 512 + 128 * j + np.arange(128)[None, :]).reshape(-1)
        own_idx.append((b, idx))
        m = dict(shared)
        m["xT"] = np.ascontiguousarray(x[b].T)
        m["xo"] = np.ascontiguousarray(x[b][idx])
        m["xTo"] = np.ascontiguousarray(m["xo"].T)
        m["cosTo"] = np.ascontiguousarray(cosT[:, idx])
        m["sinTo"] = np.ascontiguousarray(sinT[:, idx])
        m.update(_host_consts(j, t5))
        in_maps.append(m)
    if "nc" not in _NC_CACHE:
        _NC_CACHE["nc"] = build()
    res = run_bass_kernel_spmd(_NC_CACHE["nc"], in_maps, core_ids=list(range(8)))
    outp = np.zeros((2, S, DM), np.float32)
    for core in range(8):
        b, idx = own_idx[core]
        outp[b, idx] = res.results[core]["out"]
    if DEBUG:
        kernel.dbg = [(own_idx[c], res.results[c]["dbg_o"], res.results[c]["dbg_x1"]) for c in range(8)]
    return outp
```

```python
import numpy as np
from contextlib import ExitStack
import concourse.bass as bass
import concourse.mybir as mybir
from concourse.bass_utils import run_bass_kernel_spmd

F32 = mybir.dt.float32
BF16 = mybir.dt.bfloat16
AF = mybir.ActivationFunctionType
ALU = mybir.AluOpType
AX = mybir.AxisListType

NEGM = -30000.0
S = 8192
DM = 2048
NSTEP = 16
DEBUG = False
STOP_AT = None


class _Stop(Exception):
    pass


def ck(k):
    if STOP_AT == k:
        FW.enabled = False
N_RUN_STEPS = 16
RUN_B = True


class Buf:
    __slots__ = ("name", "w", "r", "ds", "ss", "psum")

    def __init__(self, name=""):
        self.name = name
        self.w = None
        self.r = []
        self.ds = None
        self.ss = None
        self.psum = name.startswith("bank")


def _compact(lst):
    best = {}
    for k, v in lst:
        if best.get(k, 0) < v:
            best[k] = v
    return list(best.items())


class FW:
    def __init__(self, nc, es):
        self.nc = nc
        self.es = es
        self.eng = {"pe": nc.tensor, "act": nc.scalar, "dve": nc.vector, "pool": nc.gpsimd, "sp": nc.sync}
        self.sems = {}
        self.cnt = {}
        self.waited = {k: {} for k in self.eng}
        for k in self.eng:
            self.sems[k] = es.enter_context(nc.semaphore("s_" + k))
            self.cnt[k] = 0
        self.n_inst = 0
        self.store_sems = []

    def dsem(self, name):
        key = "d_" + name
        self.sems[key] = self.es.enter_context(self.nc.semaphore(key))
        self.cnt[key] = 0
        return key

    def _wait(self, e, deps):
        need = {}
        for d in deps:
            if d is None:
                continue
            k, v = d
            if k == e and e == "pe":
                continue
            if need.get(k, 0) < v:
                need[k] = v
        for k, v in need.items():
            if self.waited[e].get(k, 0) >= v:
                continue
            self.eng[e].wait_ge(self.sems[k], v)
            self.waited[e][k] = v

    def _deps(self, reads, writes, e=None):
        deps = []
        for b in reads:
            deps.append(b.w)
            if b.psum and e in ("act", "dve"):
                other = "dve" if e == "act" else "act"
                deps.extend(t for t in b.r if t[0] == other)
        for b in writes:
            deps.append(b.w)
            deps.extend(b.r)
        return deps

    def _upd(self, tok, reads, writes):
        for b in writes:
            b.w = tok
            b.r = []
        for b in reads:
            b.r.append(tok)
            if len(b.r) > 32:
                b.r = _compact(b.r)

    enabled = True

    def op(self, e, fn, reads=(), writes=()):
        if not FW.enabled:
            return
        self._wait(e, self._deps(reads, writes, e))
        ins = fn(self.eng[e])
        self.cnt[e] += 1
        ins.then_inc(self.sems[e], 1)
        self._upd((e, self.cnt[e]), reads, writes)
        self.n_inst += 1

    def barrier(self):
        if not FW.enabled:
            return
        deps = [(k, v) for k, v in self.cnt.items() if v > 0]
        for e in self.eng:
            need = {}
            for k, v in deps:
                need[k] = v
            for k, v in need.items():
                if self.waited[e].get(k, 0) >= v:
                    continue
                self.eng[e].wait_ge(self.sems[k], v)
                self.waited[e][k] = v

    def dma(self, q, sem, out, in_, reads=(), writes=(), **kw):
        if not FW.enabled:
            return
        if writes:
            assert len(writes) == 1
            b = writes[0]
            if b.ds is None:
                b.ds = self.dsem("b%d" % len(self.sems))
            sem = b.ds
        else:
            b = reads[0]
            if b.ss is None:
                b.ss = self.dsem("s%d" % len(self.sems))
                self.store_sems.append(b.ss)
            sem = b.ss
        self._wait(q, self._deps(reads, writes))
        ins = self.eng[q].dma_start(out=out, in_=in_, **kw)
        self.cnt[sem] += 16
        ins.then_inc(self.sems[sem], 16)
        self._upd((sem, self.cnt[sem]), reads, writes)
        self.n_inst += 1


def _t5_bucket(rel):
    n = np.maximum(rel, 0)
    nf = np.maximum(n, 1).astype(np.float32)
    large = 16 + (np.log(nf / np.float32(16)) / np.float32(np.log(128 / 16)) * np.float32(16)).astype(np.int32)
    large = np.minimum(large, 31)
    return np.where(n < 16, n, large)


def _host_consts(j, t5):
    qi = np.arange(128)[None, :]
    ki = np.arange(128)[:, None]
    c = {}
    raw = np.zeros((12, 128, 512), np.float32)
    r31 = np.zeros((12, 128, 512), np.float32)
    wm = np.zeros((128, 8, 128), np.float32)
    sm = np.zeros((128, 5, 128), np.float32)
    for r in range(8):
        rel = 128 * (4 + j - r) + qi - ki
        wm[:, r, :] = np.where((rel >= 0) & (rel < 512), 0.0, NEGM)
        if r >= 3:
            sm[:, r - 3, :] = np.where(rel >= 0, 0.0, NEGM)
            bk = _t5_bucket(rel)
            for g in range(2):
                for h in range(4):
                    raw[6 * g + r - 3, :, 128 * h:128 * h + 128] = t5[bk, 4 * g + h]
                    r31[6 * g + r - 3, :, 128 * h:128 * h + 128] = t5[31, 4 * g + h]
    rr = np.arange(64)[:, None]
    relc = 512 + 128 * j + qi - 16 * rr - 15
    bkc = _t5_bucket(relc)
    bcm = np.zeros((128, 2, 512), np.float32)
    for g in range(2):
        for h in range(4):
            raw[6 * g + 5, :64, 128 * h:128 * h + 128] = t5[bkc, 4 * g + h]
            r31[6 * g + 5, :64, 128 * h:128 * h + 128] = t5[31, 4 * g + h]
    for h in range(4):
        m = np.where(relc >= 0, 0.0, NEGM)
        bcm[:64, 0, 128 * h:128 * h + 128] = m
        m0 = m.copy()
        m0[32, :] = NEGM
        bcm[:64, 1, 128 * h:128 * h + 128] = m0
    c["t5raw"] = raw
    c["t531"] = r31
    c["wm"] = wm
    c["sm"] = sm
    c["bcm"] = bcm
    dum = np.zeros((128, 512), np.float32)
    dum[0, :] = NEGM
    c["dum"] = dum
    e32 = np.zeros((128, 2, 16, 128), np.float32)
    for p in range(128):
        jl = p % 64
        a, rem = divmod(jl, 32)
        m, par = rem // 2, rem % 2
        e32[p, a, m, 64 * par:64 * par + 64] = 1.0
    c["e32"] = e32.reshape(128, 4096)
    sh = np.zeros((128, 288), np.float32)
    for r in range(64):
        sh[r, r + 128] = 1.0
    c["sh"] = sh
    c["ident"] = np.eye(128, dtype=np.float32)
    msel = np.zeros((128, 4, 128), np.float32)
    for slot in range(1, 512):
        n = slot - 1
        cs = 16 * n
        for jb in range(128):
            ss = 64 * jb
            if cs <= ss + 63 and cs + 31 >= ss:
                msel[slot % 128, slot // 128, jb] = 1.0
    c["msel"] = msel
    a16 = np.ones((128, 16), np.float32)
    b16 = np.zeros((128, 16), np.float32)
    for q in range(128):
        ucur = 8 + 2 * j + (1 if q >= 64 else 0)
        for u in range(16):
            if u > ucur:
                a16[q, u] = 0.0
                b16[q, u] = -1.0
            elif u == ucur or u == ucur - 1:
                a16[q, u] = 0.0
                b16[q, u] = 1e4
    c["a16"] = a16
    c["b16"] = b16
    gam = 1.0 - 2.0 ** (-5.0 - np.arange(4, dtype=np.float64))
    dec = np.zeros((128, 4, 4, 128), np.float64)
    xi = np.zeros((128, 4, 128), np.float64)
    zeta = np.zeros((128, 4, 4), np.float64)
    for h in range(4):
        for r in range(4):
            d = 128 * (j - r) + qi - ki
            dec[:, h, r, :] = np.where(d >= 0, gam[h] ** np.maximum(d, 0), 0.0)
            zeta[:, h, r] = gam[h] ** (511 - (128 * r + np.arange(128)))
        xi[:, h, :] = gam[h] ** (128 * j + np.arange(128) + 1)[None, :]
    c["dec"] = dec.astype(np.float32)
    c["xi"] = xi.astype(np.float32)
    c["zeta"] = zeta.astype(np.float32).reshape(128, 16)
    return c


def _rope_tables():
    inv = (10000.0 ** (-np.arange(0, 256, 2, dtype=np.float32) / np.float32(256))).astype(np.float32)
    ang = np.arange(S, dtype=np.float32)[None, :] * inv[:, None]
    return np.cos(ang).astype(np.float32), np.sin(ang).astype(np.float32)


G512 = [float((1.0 - 2.0 ** (-5.0 - h)) ** 512) for h in range(4)]

WIN_SLABS = ([("QA%d" % s, "K", 256 * s) for s in range(4)] +
             [("KC", "K", 1024), ("VC", "K", 1280), ("KS", "K", 1536), ("VS", "V", 1792),
              ("KW", "K", 2048), ("VW", "V", 2304)] +
             [("QR%d" % h, "K", 2584 + 256 * h) for h in range(4)] +
             [("KR%d" % h, "K", 3608 + 256 * h) for h in range(4)] +
             [("VR%d" % h, "V", 4632 + 256 * h) for h in range(4)] +
             [("GR%d" % h, "V", 5656 + 256 * h) for h in range(4)])
WIN_IDX = {n: k for k, (n, _, _) in enumerate(WIN_SLABS)}
N_WIN = len(WIN_SLABS)
OFF_WOUT = N_WIN
OFF_WUP = OFF_WOUT + 8
OFF_WDN = OFF_WUP + 32
OFF_W1 = OFF_WDN + 32
N_SLAB = OFF_W1 + 4


def build():
    FW.enabled = True
    nc = bass.Bass("TRN2", target_bir_lowering=False)
    D = {}

    def din(name, shape, dt=F32):
        D[name] = nc.dram_tensor(name, list(shape), dt, kind="ExternalInput").ap()
        return D[name]

    xT = din("xT", [DM, S])
    xTo = din("xTo", [DM, 2048])
    xo = din("xo", [2048, DM])
    w_in = din("w_in", [DM, 6680])
    w_out = din("w_out", [DM, DM])
    w_up = din("w_up", [DM, 8192])
    w_down = din("w_down", [8192, DM])
    din("gpre", [128, 16]); din("gmlp", [128, 16])
    din("gpostB", [128, DM]); din("gpost2B", [128, DM])
    din("w1k", [4096, 128]); din("w1v", [4096, 128])
    din("peTk", [128, 32]); din("peTv", [128, 32])
    din("b1k", [128, 1]); din("b1v", [128, 1])
    din("w2k", [128, 128]); din("w2v", [128, 128])
    din("cosT", [128, S]); din("sinT", [128, S])
    din("cosTo", [128, 2048]); din("sinTo", [128, 2048])
    din("t5raw", [12, 128, 512]); din("t531", [12, 128, 512])
    din("wm", [128, 8, 128]); din("sm", [128, 5, 128]); din("bcm", [128, 2, 512]); din("dum", [128, 512])
    din("e32", [128, 4096]); din("sh", [128, 288]); din("ident", [128, 128])
    din("msel", [128, 4, 128]); din("a16", [128, 16]); din("b16", [128, 16])
    din("dec", [128, 4, 4, 128]); din("xi", [128, 4, 128]); din("zeta", [128, 16])
    out = nc.dram_tensor("out", [2048, DM], F32, kind="ExternalOutput").ap()
    if DEBUG:
        dbg_o = nc.dram_tensor("dbg_o", [2048, DM], F32, kind="ExternalOutput").ap()
        dbg_x1 = nc.dram_tensor("dbg_x1", [2048, DM], F32, kind="ExternalOutput").ap()

    wscr = nc.dram_tensor("wscr", [N_SLAB, 128, 4096], BF16, kind="Internal").ap()
    ks_scr = nc.dram_tensor("ks_scr", [2, NSTEP, 128, 512], BF16, kind="Internal").ap()
    vs_scr = nc.dram_tensor("vs_scr", [2, NSTEP, 128, 4 * 130], BF16, kind="Internal").ap()
    oT_scr = nc.dram_tensor("oT_scr", [NSTEP, 128, 16 * 128], BF16, kind="Internal").ap()

    es = ExitStack()
    with es:
        fw = FW(nc, es)
        try:

            def sb(name, shape, dt, stack=es):
                return stack.enter_context(nc.sbuf_tensor("sb_" + name, list(shape), dt))

            banks = [es.enter_context(nc.psum_tensor("bank%d" % k, [128, 512], F32)) for k in range(8)]
            bbank = [Buf("bank%d" % k) for k in range(8)]
            rot = [0]
            rotset = [[4, 5, 6, 7]]

            def set_rot(lst):
                rotset[0] = list(lst)

            def nb():
                k = rotset[0][rot[0] % len(rotset[0])]
                rot[0] += 1
                return banks[k], bbank[k]

            def mm(o, lhsT, rhs, start, stop, reads, writes):
                fw.op("pe", lambda e: e.matmul(o, lhsT=lhsT, rhs=rhs, start=start, stop=stop), reads, writes)

            def tr(o, in_, reads, writes):
                fw.op("pe", lambda e: e.transpose(o, in_, ident[:]), reads + [bconst], writes)

            d_const = (lambda *a: None)("const")
            d_out = (lambda *a: None)("out")
            d_wst = (lambda *a: None)("wstore")
            bconst = Buf("const")
            bwscr = [Buf("wscrA"), Buf("wscrB")]
            boTs = Buf("oTs")

            ident = sb("ident", [128, 128], BF16)
            ones = sb("ones", [128, 128], BF16)
            gpre = sb("gpre", [128, 16], F32)
            gmlp = sb("gmlp", [128, 16], F32)
            epsT = sb("epsT", [128, 1], F32)
            ca = ExitStack()
            e32 = sb("e32", [128, 2, 2048], BF16, ca)
            shb = sb("shb", [128, 288], BF16, ca)
            td = sb("td", [128, 2, 5, 512], BF16, ca)
            bc = sb("bc", [128, 2, 2, 512], BF16, ca)
            dumb = sb("dumb", [128, 512], BF16, ca)
            wmb = sb("wmb", [128, 8, 128], BF16, ca)
            smb = sb("smb", [128, 5, 128], BF16, ca)
            a16 = sb("a16", [128, 16], F32, ca)
            b16 = sb("b16", [128, 16], F32, ca)
            dec = sb("dec", [128, 4, 4, 128], F32, ca)
            xi = sb("xi", [128, 4, 128], F32, ca)
            zeta = sb("zeta", [128, 16], F32, ca)
            w2k = sb("w2k", [128, 128], BF16, ca)
            w2v = sb("w2v", [128, 128], BF16, ca)
            biask = sb("biask", [128, 1], F32, ca)
            biasv = sb("biasv", [128, 1], F32, ca)
            wg = sb("wg", [128, 16, 24], BF16, ca)
            vcm = sb("vcm", [128, 2, 4, 256], BF16, ca)
            kcT = sb("kcT", [128, 2, 512], BF16, ca)

            with ExitStack() as ps_:
                stA = sb("stA", [128, 16, 256], F32, ps_)
                stB = sb("stB", [128, 16, 256], F32, ps_)
                cvA = sb("cvA", [128, 4096], BF16, ps_)
                cvB = sb("cvB", [128, 4096], BF16, ps_)
                sts = [(stA, Buf("stA"), cvA, Buf("cvA"), None), (stB, Buf("stB"), cvB, Buf("cvB"), None)]
                sm_f = sb("sm_f", [128, 2048], F32, ps_)
                bsm = Buf("sm_f")
                d_sm = (lambda *a: None)("sm")

                def load_cast(dst_ap, src_ap, n, eng="dve"):
                    view = sm_f[:dst_ap.shape[0], 0:n]
                    if len(dst_ap.shape) > 2:
                        pat = {3: "p (a b) -> p a b", 4: "p (a b c) -> p a b c"}[len(dst_ap.shape)]
                        kw = dict(zip("abc", dst_ap.shape[1:]))
                        kw.pop("a")
                        view = view.rearrange(pat, **kw)
                    fw.dma("sp", d_sm, view, src_ap, writes=[bsm])
                    fw.op(eng, lambda e: e.tensor_copy(out=dst_ap, in_=view), reads=[bsm], writes=[bconst])

                load_cast(ident[:], D["ident"], 128)
                load_cast(e32[:, 0, :], D["e32"][:, 0:2048], 2048)
                load_cast(e32[:, 1, :], D["e32"][:, 2048:4096], 2048)
                load_cast(shb[:], D["sh"], 288)
                load_cast(wmb[:], D["wm"], 1024)
                load_cast(dumb[:], D["dum"], 512)
                load_cast(smb[:], D["sm"], 640)
                load_cast(a16[:], D["a16"], 16)
                load_cast(b16[:], D["b16"], 16)
                load_cast(dec[:], D["dec"], 2048)
                load_cast(xi[:], D["xi"], 512)
                load_cast(zeta[:], D["zeta"], 16)
                load_cast(gpre[:], D["gpre"], 16)
                load_cast(gmlp[:], D["gmlp"], 16)
                load_cast(w2k[:], D["w2k"], 128)
                load_cast(w2v[:], D["w2v"], 128)
                bkc = [Buf(), Buf()]
                bvcm = [Buf(), Buf()]
                for g in range(2):
                    load_cast(vcm[:, g, :, 128:256], D["msel"], 512)
                fw.op("dve", lambda e: e.memset(ones[:], 1.0), writes=[bconst])
                fw.op("dve", lambda e: e.memset(epsT[:], 1e-6), writes=[bconst])
                fw.op("dve", lambda e: e.memset(vcm[:, :, :, 0:128], 0.0), reads=[bconst], writes=bvcm)
                fw.op("dve", lambda e: e.memset(kcT[:], 0.0), writes=bkc)
                ck(1)
                t5a = sb("t5a", [128, 512], F32, ps_)
                t5b = sb("t5b", [128, 512], F32, ps_)
                bt5a, bt5b = Buf(), Buf()
                d_t5 = (lambda *a: None)("t5")
                bcm = sb("bcm", [128, 2, 512], F32, ps_)
                bbcm = Buf()
                fw.dma("sp", d_t5, bcm[:], D["bcm"], writes=[bbcm])
                for t in range(12):
                    g, k = divmod(t, 6)
                    fw.dma("sp", d_t5, t5a[:], D["t5raw"][t], writes=[bt5a])
                    fw.dma("sp", d_t5, t5b[:], D["t531"][t], writes=[bt5b])
                    if k < 5:
                        fw.op("dve", lambda e: e.tensor_tensor(out=td[:, g, k, :], in0=t5a[:], in1=t5b[:], op=ALU.subtract),
                              reads=[bt5a, bt5b], writes=[bconst])
                    else:
                        fw.op("dve", lambda e: e.tensor_tensor(out=t5a[:], in0=t5a[:], in1=t5b[:], op=ALU.subtract),
                              reads=[bt5a, bt5b], writes=[bt5a])
                        for v in range(2):
                            fw.op("dve", lambda e: e.tensor_tensor(out=bc[:, g, v, :], in0=t5a[:], in1=bcm[:, v, :], op=ALU.add),
                                  reads=[bt5a, bbcm], writes=[bconst])
                gst = sb("gst", [128, 16, 24], F32, ps_)
                bgst = Buf()
                with nc.allow_non_contiguous_dma(reason="small gate weight slice"):
                    fw.dma("sp", d_t5, gst[:], w_in.rearrange("(c p) n -> p c n", p=128)[:, :, 2560:2584], writes=[bgst])
                fw.op("dve", lambda e: e.tensor_tensor(out=wg[:], in0=gst[:], in1=gpre[:, :, None].broadcast_to([128, 16, 24]), op=ALU.mult),
                      reads=[bgst, bconst], writes=[bconst])

                ck(2)
                slab_jobs = []
                for k, (nm, kind, c0) in enumerate(WIN_SLABS):
                    slab_jobs.append((k, w_in.rearrange("(c p) n -> p c n", p=128)[:, :, c0:c0 + 256], kind, gpre))
                for cg in range(8):
                    slab_jobs.append((OFF_WOUT + cg, w_out.rearrange("(c p) n -> p c n", p=128)[:, :, 256 * cg:256 * cg + 256], "V", None))
                for s in range(32):
                    slab_jobs.append((OFF_WUP + s, w_up.rearrange("(c p) n -> p c n", p=128)[:, :, 256 * s:256 * s + 256], "K", gmlp))
                for cg in range(8):
                    for hq in range(4):
                        slab_jobs.append((OFF_WDN + cg * 4 + hq,
                                          w_down.rearrange("(c p) n -> p c n", p=128)[:, 16 * hq:16 * hq + 16, 256 * cg:256 * cg + 256], "V", None))
                for t, nm in enumerate(["w1k", "w1v"]):
                    for hf in range(2):
                        slab_jobs.append((OFF_W1 + 2 * t + hf, D[nm].rearrange("(l d) h -> d l h", d=128)[:, 16 * hf:16 * hf + 16, :], "W1", None))
                for n, (sid, src, kind, gain) in enumerate(slab_jobs):
                    st, bst, cv, bcv, dsm = sts[n % 2]
                    ce = "dve" if n % 2 == 0 else "pool"
                    if kind == "W1":
                        fw.dma("sp", dsm, st[:, :, 0:128], src, writes=[bst])
                        fw.op(ce, lambda e: e.tensor_copy(out=cv[:, 0:2048].rearrange("p (l h) -> p l h", h=128), in_=st[:, :, 0:128]),
                              reads=[bst], writes=[bcv])
                        fw.dma("act", d_wst, wscr[sid, :, 0:2048], cv[:, 0:2048], reads=[bcv], writes=[bwscr[n % 2]])
                        continue
                    fw.dma("sp", dsm, st[:], src, writes=[bst])
                    if kind == "K":
                        ov = cv[:].rearrange("p (t c n) -> p t c n", t=2, c=16)
                        iv = st[:].rearrange("p c (t n) -> p t c n", t=2)
                        gv = None if gain is None else gain[:, None, :, None].broadcast_to([128, 2, 16, 128])
                    else:
                        ov = cv[:].rearrange("p (c n) -> p c n", c=16)
                        iv = st[:]
                        gv = None if gain is None else gain[:, :, None].broadcast_to([128, 16, 256])
                    if gv is None:
                        fw.op(ce, lambda e: e.tensor_copy(out=ov, in_=iv), reads=[bst], writes=[bcv])
                    else:
                        fw.op(ce, lambda e: e.tensor_tensor(out=ov, in0=iv, in1=gv, op=ALU.mult), reads=[bst, bconst], writes=[bcv])
                    fw.dma("act", d_wst, wscr[sid], cv[:], reads=[bcv], writes=[bwscr[n % 2]])
                ck(3)
                w1st = sb("w1st", [128, 32, 128], BF16, ps_)
                pest = sb("pest", [128, 32], BF16, ps_)
                b1st = sb("b1st", [128, 1], F32, ps_)
                bw1, bpe, bb1 = Buf(), Buf(), Buf()
                d_w1 = (lambda *a: None)("w1p")
                for t, (nm, pnm, bnm, dst) in enumerate([("w1k", "peTk", "b1k", biask), ("w1v", "peTv", "b1v", biasv)]):
                    for hf in range(2):
                        fw.dma("sp", d_w1, w1st[:, 16 * hf:16 * hf + 16, :], wscr[OFF_W1 + 2 * t + hf, :, 0:2048].rearrange("p (l h) -> p l h", h=128),
                               reads=bwscr, writes=[bw1])
                    load_cast(pest[:], D[pnm], 32)
                    fw.dma("sp", d_w1, b1st[:], D[bnm], writes=[bb1])
                    pb, bpb = nb()
                    for l in range(32):
                        mm(pb[:, 0:1], w1st[:, l, :], pest[:, l:l + 1], l == 0, l == 31, [bw1, bconst], [bpb])
                    fw.op("dve", lambda e: e.tensor_tensor(out=dst[:], in0=pb[:, 0:1], in1=b1st[:], op=ALU.add), reads=[bpb, bb1], writes=[bconst])

            ck(4)
            fw.barrier()
            with ExitStack() as pa:
                slabs = [(sb("slab%d" % k, [128, 4096], BF16, pa), Buf(), (lambda *a: None)("slab%d" % k)) for k in range(3)]
                slab_n = [0]

                def load_slab(sid, half=False):
                    t, b, dsm = slabs[slab_n[0] % 3]
                    slab_n[0] += 1
                    if half:
                        fw.dma("sp", dsm, t[:, 0:2048], wscr[sid, :, 0:2048], reads=bwscr, writes=[b])
                    else:
                        fw.dma("sp", dsm, t[:], wscr[sid], reads=bwscr, writes=[b])
                    return t, b

                xst = [(sb("xst%d" % k, [128, 4, 512], F32, pa), Buf(), (lambda *a: None)("xst%d" % k)) for k in range(2)]
                xst_n = [0]
                xb = sb("xb", [128, 16, 512], BF16, pa); bxb = Buf()
                xsq = sb("xsq", [128, 4, 512], BF16, pa); bxsq = Buf()
                xob = sb("xob", [128, 16, 128], BF16, pa); bxob = Buf()
                rstdB = sb("rstdB", [128, 512], F32, pa); brB = Buf()
                rstdT = sb("rstdT", [128, 4], F32, pa); brT = Buf()
                rstdBo = sb("rstdBo", [128, 128], F32, pa); brBo = Buf()
                rstdTo = sb("rstdTo", [128, 1], F32, pa); brTo = Buf()
                cs = sb("cs", [128, 2, 512], F32, pa); bcs = Buf(); d_cs = (lambda *a: None)("cs")
                cso = sb("cso", [128, 2, 128], F32, pa); bcso = Buf()
                kcs = [sb("kcs%d" % t, [128, 2, 528], BF16, pa) for t in range(2)]
                bkcs = [Buf(), Buf()]
                ksf = sb("ksf", [128, 2, 512], BF16, pa); bksf = [Buf(), Buf()]
                vsf = sb("vsf", [128, 2, 4, 130], BF16, pa); bvsf = Buf()
                kwT = sb("kwT", [128, 2, 2, 512], BF16, pa); bkw = [[Buf(), Buf()], [Buf(), Buf()]]
                vw = sb("vw", [128, 2, 2, 4, 130], BF16, pa); bvw = [Buf(), Buf()]
                ksl = [(sb("ksl%d" % k, [128, 512], BF16, pa), sb("vsl%d" % k, [128, 4, 130], BF16, pa), Buf(), (lambda *a: None)("ksl%d" % k)) for k in range(3)]
                ksl_n = [0]
                d_kst = [[(lambda *a: None)("kst%d%d" % (g_, p_)) for p_ in range(2)] for g_ in range(2)]
                bksc = [Buf(), Buf()]
                bvsc = [Buf(), Buf()]
                qaT = sb("qaT", [128, 8, 128], BF16, pa); bqa = Buf()
                gat = sb("gat", [128, 24], F32, pa); bgat = Buf()
                hidg = sb("hidg", [128, 2, 2, 128], BF16, pa); bhid = Buf()
                gtmp = sb("gtmp", [128, 3, 32], F32, pa); bgt = Buf()
                put = [(sb("put%d" % k, [128, 512], BF16, pa), Buf()) for k in range(3)]
                put_n = [0]
                Of = sb("Of", [128, 1024], F32, pa); bOf = Buf()
                Ob = sb("Ob", [128, 2048], BF16, pa); bOb = Buf()
                oT = sb("oT", [128, 16, 128], BF16, pa); boT = Buf()
                d_oT = (lambda *a: None)("oT")
                impf = sb("impf", [128, 128], F32, pa); bimp = Buf()
                imp1 = sb("imp1", [128, 128], F32, pa); bimp1 = Buf()
                imp2 = sb("imp2", [128, 128], F32, pa); bimp2 = Buf()
                top8 = sb("top8", [128, 16], F32, pa); btop = Buf()
                nsel = sb("nsel", [128, 128], BF16, pa); bnsel = Buf()
                NS = sb("NS", [128, 2, 512], BF16, pa); bNS = [Buf(), Buf()]
                small = sb("small", [128, 64], F32, pa); bsmall = Buf()
                R = sb("R", [128, 4, 2, 256], F32, pa); bR = [Buf() for _ in range(4)]
                Rb = sb("Rb", [128, 4, 2, 256], BF16, pa); bRb = [Buf() for _ in range(4)]
                krT = sb("krT", [128, 2, 512], BF16, pa); bkr = Buf()
                rtmp = sb("rtmp", [128, 4, 512], F32, pa); brt = Buf()
                kz = sb("kz", [128, 4, 2, 128], BF16, pa); bkz = Buf()
                vr = sb("vr", [128, 4, 256], BF16, pa); bvr = Buf()
                qrT = sb("qrT", [128, 2, 128], BF16, pa); bqr = Buf()
                qrX = sb("qrX", [128, 2, 128], BF16, pa); bqrx = Buf()
                sdT = sb("sdT", [128, 4, 128], BF16, pa); bsd = Buf()
                sg = sb("sg", [128, 256], F32, pa); bsg = Buf()
                gn = sb("gn", [128, 256], F32, pa); bgn = Buf()
                dbgt = sb("dbgt", [128, 2048], F32, pa) if DEBUG else None
                bdbg = Buf()

                fw.op("dve", lambda e: e.memset(R[:], 0.0), writes=bR)
                fw.op("dve", lambda e: e.memset(Rb[:], 0.0), writes=bRb)
                fw.op("dve", lambda e: e.memset(vsf[:, :, :, 128:130], 1.0), writes=[bvsf])
                fw.op("dve", lambda e: e.memset(vw[:, :, :, :, 128:130], 1.0), writes=bvw)
                for t in range(2):
                    fw.op("dve", lambda e: e.memset(kcs[t][:], 0.0), writes=[bkcs[t]])
                fw.op("dve", lambda e: e.memset(hidg[:], 0.0), writes=[bhid])

                ck(5)
                xT_v = xT.rearrange("(c p) t -> p c t", p=128)
                xTo_v = xTo.rearrange("(c p) t -> p c t", p=128)

                def rstd_from(ps_ap, dst_ap, breads, bw):
                    fw.op("act", lambda e: e.activation(out=dst_ap, in_=ps_ap, func=AF.Sqrt, bias=epsT[:dst_ap.shape[0], :], scale=1.0 / DM),
                          reads=breads + [bconst], writes=[bw])
                    fw.op("dve", lambda e: e.reciprocal(out=dst_ap, in_=dst_ap), reads=[bw], writes=[bw])

                def proj_K(slab, ct, rhs_fn, n, breads):
                    t, b = slab
                    pb, bpb = nb()
                    tv = t[:].rearrange("p (t c n) -> p t c n", t=2, c=16)
                    for c in range(16):
                        mm(pb[:, 0:n], tv[:, ct, c, :], rhs_fn(c), c == 0, c == 15, [b] + breads, [bpb])
                    return pb, bpb

                def proj_V(slab, lhs_fn, breads, ncols=256):
                    t, b = slab
                    pb, bpb = nb()
                    tv = t[:].rearrange("p (c n) -> p c n", c=16)
                    for c in range(16):
                        mm(pb[:, 0:ncols], lhs_fn(c), tv[:, c, 0:ncols], c == 0, c == 15, [b] + breads, [bpb])
                    return pb, bpb

                def rope(pb0, bp0, pb1, bp1, rB, brBx, tab, btab, n, dst, bdst, pre):
                    a1, a2, t1, t2 = rtmp[:, 0, 0:n], rtmp[:, 1, 0:n], rtmp[:, 2, 0:n], rtmp[:, 3, 0:n]
                    fw.op("dve", lambda e: e.scalar_tensor_tensor(out=a1, in0=pb0[:, 0:n], scalar=pre, in1=rB, op0=ALU.mult, op1=ALU.mult),
                          reads=[bp0, brBx], writes=[brt])
                    fw.op("dve", lambda e: e.scalar_tensor_tensor(out=a2, in0=pb1[:, 0:n], scalar=pre, in1=rB, op0=ALU.mult, op1=ALU.mult),
                          reads=[bp1, brBx], writes=[brt])
                    fw.op("dve", lambda e: e.tensor_tensor(out=t1, in0=a1, in1=tab[:, 0, 0:n], op=ALU.mult), reads=[brt, btab], writes=[brt])
                    fw.op("dve", lambda e: e.tensor_tensor(out=t2, in0=a2, in1=tab[:, 1, 0:n], op=ALU.mult), reads=[brt, btab], writes=[brt])
                    fw.op("dve", lambda e: e.tensor_tensor(out=dst[:, 0, 0:n], in0=t1, in1=t2, op=ALU.subtract), reads=[brt], writes=[bdst])
                    fw.op("dve", lambda e: e.tensor_tensor(out=t1, in0=a2, in1=tab[:, 0, 0:n], op=ALU.mult), reads=[brt, btab], writes=[brt])
                    fw.op("dve", lambda e: e.tensor_tensor(out=t2, in0=a1, in1=tab[:, 1, 0:n], op=ALU.mult), reads=[brt, btab], writes=[brt])
                    fw.op("dve", lambda e: e.tensor_tensor(out=dst[:, 1, 0:n], in0=t1, in1=t2, op=ALU.add), reads=[brt], writes=[bdst])

                for i in range(N_RUN_STEPS):
                    slot = i % 2
                    set_rot(range(8))
                    psB, bpsB = nb()
                    psT, bpsT = nb()
                    for qtr in range(4):
                        st, bst, dsm = xst[xst_n[0] % 2]; xst_n[0] += 1
                        fw.dma("sp", dsm, st[:], xT_v[:, 4 * qtr:4 * qtr + 4, 512 * i:512 * i + 512], writes=[bst])
                        fw.op("dve", lambda e: e.tensor_copy(out=xb[:, 4 * qtr:4 * qtr + 4, :], in_=st[:]), reads=[bst], writes=[bxb])
                        fw.op("act", lambda e: e.activation(out=xsq[:], in_=st[:], func=AF.Square), reads=[bst], writes=[bxsq])
                        for c in range(4):
                            first, last = (qtr == 0 and c == 0), (qtr == 3 and c == 3)
                            mm(psB[:, :], ones[:], xsq[:, c, :], first, last, [bconst, bxsq], [bpsB])
                        for r in range(4):
                            for c in range(4):
                                first, last = (qtr == 0 and c == 0 and r == 0), (qtr == 3 and c == 3 and r == 3)
                                mm(psT[:, r:r + 1], xsq[:, c, 128 * r:128 * r + 128], ones[:, 0:1], first, last, [bconst, bxsq], [bpsT])
                    rstd_from(psB[:, :], rstdB[:], [bpsB], brB)
                    rstd_from(psT[:, 0:4], rstdT[:], [bpsT], brT)
                    psB, bpsB = nb()
                    for qtr in range(4):
                        st, bst, dsm = xst[xst_n[0] % 2]; xst_n[0] += 1
                        stv = st[:].rearrange("p a (b t) -> p (a b) t", t=128)[:, 0:4, :]
                        fw.dma("sp", dsm, stv, xTo_v[:, 4 * qtr:4 * qtr + 4, 128 * i:128 * i + 128], writes=[bst])
                        fw.op("dve", lambda e: e.tensor_copy(out=xob[:, 4 * qtr:4 * qtr + 4, :], in_=stv), reads=[bst], writes=[bxob])
                        fw.op("act", lambda e: e.activation(out=xsq[:, 0, :].rearrange("p (a t) -> p a t", t=128), in_=stv, func=AF.Square),
                              reads=[bst], writes=[bxsq])
                        for c in range(4):
                            first, last = (qtr == 0 and c == 0), (qtr == 3 and c == 3)
                            mm(psB[:, 0:128], ones[:], xsq[:, 0, 128 * c:128 * c + 128], first, False, [bconst, bxsq], [bpsB])
                        for c in range(4):
                            mm(psB[:, 256:257], xsq[:, 0, 128 * c:128 * c + 128], ones[:, 0:1], False, (qtr == 3 and c == 3), [bconst, bxsq], [bpsB])
                    rstd_from(psB[:, 0:128], rstdBo[:], [bpsB], brBo)
                    rstd_from(psB[:, 256:257], rstdTo[:], [bpsB], brTo)
                    ck(6)
                    fw.dma("sp", d_cs, cs[:, 0, :], D["cosT"][:, 512 * i:512 * i + 512], writes=[bcs])
                    fw.dma("sp", d_cs, cs[:, 1, :], D["sinT"][:, 512 * i:512 * i + 512], writes=[bcs])
                    fw.dma("sp", d_cs, cso[:, 0, :], D["cosTo"][:, 128 * i:128 * i + 128], writes=[bcso])
                    fw.dma("sp", d_cs, cso[:, 1, :], D["sinTo"][:, 128 * i:128 * i + 128], writes=[bcso])

                    xrhs = lambda c: xb[:, c, :]
                    xorhs = lambda c: xob[:, c, :]

                    for nm, t in (("KC", 0), ("VC", 1)):
                        sl = load_slab(WIN_IDX[nm])
                        for g in range(2):
                            pb, bpb = proj_K(sl, g, xrhs, 512, [bxb])
                            fw.op("dve", lambda e: e.tensor_tensor(out=kcs[t][:, g, 16:528], in0=pb[:, :], in1=rstdB[:], op=ALU.mult),
                                  reads=[bpb, brB], writes=[bkcs[t]])
                    sl = load_slab(WIN_IDX["KS"])
                    for g in range(2):
                        pb, bpb = proj_K(sl, g, xrhs, 512, [bxb])
                        fw.op("dve", lambda e: e.tensor_tensor(out=ksf[:, g, :], in0=pb[:, :], in1=rstdB[:], op=ALU.mult),
                              reads=[bpb, brB], writes=[bksf[g]])
                        fw.dma("pool", None, ks_scr[g, i], ksf[:, g, :], reads=[bksf[g]], writes=[bksc[g]])
                    sl = load_slab(WIN_IDX["KW"])
                    for g in range(2):
                        pb, bpb = proj_K(sl, g, xrhs, 512, [bxb])
                        fw.op("dve", lambda e: e.tensor_tensor(out=kwT[:, g, slot, :], in0=pb[:, :], in1=rstdB[:], op=ALU.mult),
                              reads=[bpb, brB], writes=[bkw[g][slot]])
                    sl = load_slab(WIN_IDX["VS"])
                    for r in range(4):
                        pb, bpb = proj_V(sl, lambda c: xb[:, c, 128 * r:128 * r + 128], [bxb])
                        fw.op("act", lambda e: e.activation(out=vsf[:, :, r, 0:128], in_=pb[:, 0:256].rearrange("p (g d) -> p g d", g=2),
                                                            func=AF.Copy, scale=rstdT[:, r:r + 1]), reads=[bpb, brT], writes=[bvsf])
                    for g in range(2):
                        fw.dma("pool", None, vs_scr[g, i], vsf[:, g, :, :].rearrange("p r d -> p (r d)"), reads=[bvsf], writes=[bvsc[g]])
                    sl = load_slab(WIN_IDX["VW"])
                    for r in range(4):
                        pb, bpb = proj_V(sl, lambda c: xb[:, c, 128 * r:128 * r + 128], [bxb])
                        fw.op("act", lambda e: e.activation(out=vw[:, slot, :, r, 0:128], in_=pb[:, 0:256].rearrange("p (g d) -> p g d", g=2),
                                                            func=AF.Copy, scale=rstdT[:, r:r + 1]), reads=[bpb, brT], writes=[bvw[slot]])

                    ck(7)
                    qd = 32 * (i % 4)
                    chn = i // 4
                    for t in range(2):
                        w1lo = load_slab(OFF_W1 + 2 * t, half=True)
                        w1hi = load_slab(OFF_W1 + 2 * t + 1, half=True)
                        for g in range(2):
                            pb, bpb = nb()
                            for l in range(32):
                                wt, bwt = (w1lo if l < 16 else w1hi)
                                lv = wt[:, 0:2048].rearrange("p (l h) -> p l h", h=128)[:, l % 16, :]
                                mm(pb[:, 0:32], lv, kcs[t][:, g, l:l + 497:16], l == 0, l == 31, [bwt, bkcs[t]], [bpb])
                            bia = biask if t == 0 else biasv
                            xg, x2, x3 = gtmp[:, 0, :], gtmp[:, 1, :], gtmp[:, 2, :]
                            fw.op("act", lambda e: e.activation(out=xg, in_=pb[:, 0:32], func=AF.Identity, bias=bia[:], scale=1.0),
                                  reads=[bpb, bconst], writes=[bgt])
                            fw.op("dve", lambda e: e.tensor_tensor(out=x2, in0=xg, in1=xg, op=ALU.mult), reads=[bgt], writes=[bgt])
                            fw.op("dve", lambda e: e.tensor_scalar(out=x2, in0=x2, scalar1=0.044715, scalar2=1.0, op0=ALU.mult, op1=ALU.add),
                                  reads=[bgt], writes=[bgt])
                            fw.op("dve", lambda e: e.tensor_tensor(out=x3, in0=x2, in1=xg, op=ALU.mult), reads=[bgt], writes=[bgt])
                            fw.op("act", lambda e: e.activation(out=x3, in_=x3, func=AF.Sigmoid, scale=1.5957691216057308), reads=[bgt], writes=[bgt])
                            fw.op("dve", lambda e: e.tensor_tensor(out=hidg[:, t, g, qd:qd + 32], in0=x3, in1=xg, op=ALU.mult), reads=[bgt], writes=[bhid])
                            pb, bpb = nb()
                            if t == 0:
                                mm(pb[:, 0:32], w2k[:], hidg[:, 0, g, qd:qd + 32], True, True, [bconst, bhid], [bpb])
                                fw.op("dve", lambda e: e.tensor_copy(out=kcT[:, g, 32 * i:32 * i + 32], in_=pb[:, 0:32]), reads=[bpb], writes=[bkc[g]])
                            else:
                                mm(pb[:, 0:128], hidg[:, 1, g, :], w2v[:], True, True, [bconst, bhid], [bpb])
                                fw.op("dve", lambda e: e.tensor_copy(out=vcm[qd:qd + 32, g, chn, 0:128], in_=pb[qd:qd + 32, 0:128]), reads=[bpb], writes=[bvcm[g]])
                        fw.op("dve", lambda e: e.tensor_copy(out=kcs[t][:, :, 0:16], in_=kcs[t][:, :, 512:528]), reads=[bkcs[t]], writes=[bkcs[t]])

                    ck(8)
                    for s in range(4):
                        sl = load_slab(WIN_IDX["QA%d" % s])
                        for ct in range(2):
                            pb, bpb = proj_K(sl, ct, xorhs, 128, [bxob])
                            fw.op("dve", lambda e: e.scalar_tensor_tensor(out=qaT[:, 2 * s + ct, :], in0=pb[:, 0:128], scalar=float(128 ** -0.5),
                                                                          in1=rstdBo[:], op0=ALU.mult, op1=ALU.mult), reads=[bpb, brBo], writes=[bqa])
                    pb, bpb = nb()
                    for c in range(16):
                        mm(pb[:, 0:24], xob[:, c, :], wg[:, c, :], c == 0, c == 15, [bxob, bconst], [bpb])
                    fw.op("act", lambda e: e.activation(out=gat[:], in_=pb[:, 0:24], func=AF.Sigmoid, scale=rstdTo[:, 0:1]), reads=[bpb, brTo], writes=[bgat])

                    ck(9)
                    set_rot([4, 5, 6, 7])
                    for g in range(2):
                        qg = qaT[:, 4 * g:4 * g + 4, :].rearrange("p h q -> p (h q)")
                        accs = [(banks[0], bbank[0]), (banks[1], bbank[1])]
                        lb, blb = banks[2], bbank[2]
                        nch = i // 4 + 1
                        for m in range(nch):
                            npop = 128 if m < nch - 1 else 32 * (i % 4 + 1)
                            pb, bpb = nb()
                            biases = []
                            if i == 0:
                                biases.append((shb[32:64, 160:288], bc[32:64, g, 1, :]))
                            else:
                                cha, chb = (i - 1) // 4, i // 4
                                s0 = 32 * ((i - 1) % 4)
                                if cha == chb:
                                    if m == cha:
                                        biases.append((shb[0:64, 128 - s0:256 - s0], bc[0:64, g, 0, :]))
                                else:
                                    if m == cha:
                                        biases.append((shb[0:32, 32:160], bc[0:32, g, 0, :]))
                                    if m == chb:
                                        biases.append((shb[32:64, 160:288], bc[32:64, g, 0, :]))
                            if i >= 1 and m == 0:
                                biases.append((shb[0:32, 128:256], dumb[0:32, :]))
                            mm(pb[:, :], kcT[:, g, 128 * m:128 * m + 128], qg, True, len(biases) == 0, [bkc[g], bqa], [bpb])
                            for bi, (l_, r_) in enumerate(biases):
                                mm(pb[:, :], l_, r_, False, bi == len(biases) - 1, [bconst], [bpb])
                            pt, bpt = put[put_n[0] % 3]; put_n[0] += 1
                            fw.op("act", lambda e: e.activation(out=pt[0:npop, :], in_=pb[0:npop, :], func=AF.Exp), reads=[bpb], writes=[bpt])
                            for h in range(4):
                                ab, bab = accs[h // 2]
                                mm(ab[:, 256 * (h % 2):256 * (h % 2) + 256], pt[0:npop, 128 * h:128 * h + 128], vcm[0:npop, g, m, :],
                                   m == 0 and h % 2 == 0, m == nch - 1 and h % 2 == 1, [bpt, bvcm[g]], [bab])
                                mm(lb[:, h:h + 1], pt[0:npop, 128 * h:128 * h + 128], ones[0:npop, 0:1], m == 0 and h == 0, m == nch - 1 and h == 3, [bpt, bconst], [blb])
                        ck(91)
                        rl = small[:, 0:4]
                        fw.op("dve", lambda e: e.tensor_scalar(out=rl, in0=lb[:, 0:4], scalar1=1e-30, scalar2=None, op0=ALU.max), reads=[blb], writes=[bsmall])
                        fw.op("dve", lambda e: e.reciprocal(out=rl, in_=rl), reads=[bsmall], writes=[bsmall])
                        cf = small[:, 4:8]
                        fw.op("dve", lambda e: e.tensor_tensor(out=cf, in0=rl, in1=gat[:, 12 * g:12 * g + 12:3], op=ALU.mult), reads=[bsmall, bgat], writes=[bsmall])
                        for h in range(4):
                            ab, bab = accs[h // 2]
                            oc = 256 * (h % 2)
                            fw.op("dve", lambda e: e.tensor_scalar(out=Of[:, 512 * g + 128 * h:512 * g + 128 * h + 128], in0=ab[:, oc:oc + 128],
                                                                   scalar1=small[:, 4 + h:5 + h], scalar2=None, op0=ALU.mult), reads=[bab, bsmall], writes=[bOf])
                            if h == 0:
                                fw.op("dve", lambda e: e.tensor_scalar(out=impf[:], in0=ab[:, oc + 128:oc + 256], scalar1=small[:, h:h + 1], scalar2=None,
                                                                       op0=ALU.mult), reads=[bab, bsmall], writes=[bimp])
                            else:
                                fw.op("dve", lambda e: e.scalar_tensor_tensor(out=impf[:], in0=ab[:, oc + 128:oc + 256], scalar=small[:, h:h + 1],
                                                                              in1=impf[:], op0=ALU.mult, op1=ALU.add), reads=[bab, bsmall, bimp], writes=[bimp])
                        ck(92)
                        w0 = max(0, 8 * i - 8)
                        w1 = 8 * i + 8
                        u0 = w0 - (8 * i - 8)
                        if w1 < 128:
                            fw.op("pool", lambda e: e.memset(imp1[:, w1:128], -1.0), writes=[bimp1])
                        if w0 > 0:
                            fw.op("dve", lambda e: e.tensor_copy(out=imp1[:, 0:w0], in_=impf[:, 0:w0]), reads=[bimp], writes=[bimp1])
                        fw.op("dve", lambda e: e.tensor_tensor(out=imp1[:, w0:w1], in0=impf[:, w0:w1], in1=a16[:, u0:16], op=ALU.mult),
                              reads=[bimp, bconst], writes=[bimp1])
                        fw.op("dve", lambda e: e.tensor_tensor(out=imp1[:, w0:w1], in0=imp1[:, w0:w1], in1=b16[:, u0:16], op=ALU.add),
                              reads=[bconst, bimp1], writes=[bimp1])
                        fw.op("dve", lambda e: e.memset(imp1[:, 0:1], 1e4), writes=[bimp1])
                        fw.op("dve", lambda e: e.max(out=top8[:, 0:8], in_=imp1[:]), reads=[bimp1], writes=[btop])
                        fw.op("dve", lambda e: e.match_replace(out=imp2[:], in_to_replace=top8[:, 0:8], in_values=imp1[:], imm_value=-2.0),
                              reads=[bimp1, btop], writes=[bimp2])
                        fw.op("dve", lambda e: e.max(out=top8[:, 8:16], in_=imp2[:]), reads=[bimp2], writes=[btop])
                        fw.op("dve", lambda e: e.tensor_scalar(out=small[:, 8:9], in0=top8[:, 15:16], scalar1=0.0, scalar2=None, op0=ALU.max),
                              reads=[btop], writes=[bsmall])
                        fw.op("dve", lambda e: e.tensor_scalar(out=nsel[:], in0=imp1[:], scalar1=small[:, 8:9], scalar2=NEGM, op0=ALU.is_lt, op1=ALU.mult),
                              reads=[bimp1, bsmall], writes=[bnsel])
                        ck(93)
                        pb, bpb = nb()
                        pbv = pb[:, 0:64].bitcast(BF16)
                        tr(pbv, nsel[:], [bnsel], [bpb])
                        fw.op("dve", lambda e: e.tensor_copy(out=NS[:, g, :].rearrange("p (h q) -> p h q", h=4),
                                                             in_=pbv[:, None, :].broadcast_to([128, 4, 128])), reads=[bpb], writes=[bNS[g]])
                        ck(94)
                        for br in ("slc", "win"):
                            accs = [(banks[0], bbank[0]), (banks[1], bbank[1])] if br == "slc" else [(banks[2], bbank[2]), (banks[3], bbank[3])]
                            if br == "slc":
                                tiles = list(range(4 * i + 4))
                            else:
                                tiles = [r for r in range(8) if (i > 0 or r >= 4)]
                            cur_chunk = [None, None, None]
                            ntl = len(tiles)

                            def emit_pv(pt_, bpt_, v_ap_, bkv_, ti_):
                                for h in range(4):
                                    ab, bab = accs[h // 2]
                                    oc = 129 * (h % 2)
                                    mm(ab[:, oc:oc + 129], pt_[:, 128 * h:128 * h + 128], v_ap_, ti_ == 0 and h % 2 == 0, ti_ == ntl - 1 and h % 2 == 1,
                                       [bpt_] + bkv_, [bab])

                            pend = None
                            for ti, kt in enumerate(tiles):
                                if br == "slc":
                                    r = kt - 4 * (i - 1)
                                    ch, rl_ = divmod(kt, 4)
                                    if ch == i:
                                        kT_ap, v_ap, bkv = ksf[:, g, 128 * rl_:128 * rl_ + 128], vsf[:, g, rl_, 0:129], [bksf[g], bvsf]
                                    else:
                                        if cur_chunk[0] != ch:
                                            kt_, vt_, bk_, dsm = ksl[ksl_n[0] % 3]; ksl_n[0] += 1
                                            fw.dma("sp", dsm, kt_[:], ks_scr[g, ch], reads=[bksc[g]], writes=[bk_])
                                            fw.dma("sp", dsm, vt_[:].rearrange("p r d -> p (r d)"), vs_scr[g, ch], reads=[bvsc[g]], writes=[bk_])
                                            cur_chunk = [ch, (kt_, vt_), bk_]
                                        kt_, vt_ = cur_chunk[1]
                                        kT_ap, v_ap, bkv = kt_[:, 128 * rl_:128 * rl_ + 128], vt_[:, rl_, 0:129], [cur_chunk[2]]
                                else:
                                    r = kt
                                    sl_ = (i - 1) % 2 if r < 4 else i % 2
                                    rl_ = r % 4
                                    kT_ap, v_ap = kwT[:, g, sl_, 128 * rl_:128 * rl_ + 128], vw[:, sl_, g, rl_, 0:129]
                                    bkv = [bkw[g][sl_], bvw[sl_]]
                                pb, bpb = nb()
                                extra = []
                                if br == "slc":
                                    a_, m_ = divmod(kt, 16)
                                    hb_, ap_ = divmod(a_, 2)
                                    extra.append((e32[64 * hb_:64 * hb_ + 64, ap_, 128 * m_:128 * m_ + 128], NS[64 * hb_:64 * hb_ + 64, g, :], [bNS[g]]))
                                    if 3 <= r <= 7:
                                        extra.append((ident[:], smb[:, r - 3, None, :].broadcast_to([128, 4, 128]), []))
                                else:
                                    extra.append((ident[:], wmb[:, r, None, :].broadcast_to([128, 4, 128]), []))
                                if 3 <= r <= 7:
                                    extra.append((ident[:], td[:, g, r - 3, :], []))
                                mm(pb[:, :], kT_ap, qg, True, False, bkv + [bqa], [bpb])
                                for bi, (l_, r_, br_) in enumerate(extra):
                                    ro = pb[:, :] if len(r_.shape) == 2 else pb[:, :].rearrange("p (h q) -> p h q", h=4)
                                    mm(ro, l_, r_, False, bi == len(extra) - 1, [bconst] + br_, [bpb])
                                pt, bpt = put[put_n[0] % 3]; put_n[0] += 1
                                fw.op("act", lambda e: e.activation(out=pt[:], in_=pb[:, :], func=AF.Exp), reads=[bpb], writes=[bpt])
                                if pend is not None:
                                    emit_pv(*pend)
                                pend = (pt, bpt, v_ap, bkv, ti)
                            if pend is not None:
                                emit_pv(*pend)
                                pend = None
                            gi = 1 if br == "slc" else 2
                            lall = small[:, 16:20]
                            for h in range(4):
                                ab, bab = accs[h // 2]
                                oc = 129 * (h % 2)
                                fw.op("dve", lambda e: e.reciprocal(out=small[:, 16 + h:17 + h], in_=ab[:, oc + 128:oc + 129]), reads=[bab], writes=[bsmall])
                            fw.op("dve", lambda e: e.tensor_tensor(out=small[:, 20:24], in0=lall, in1=gat[:, 12 * g + gi:12 * g + 12:3], op=ALU.mult),
                                  reads=[bsmall, bgat], writes=[bsmall])
                            for h in range(4):
                                ab, bab = accs[h // 2]
                                oc = 129 * (h % 2)
                                dst = Of[:, 512 * g + 128 * h:512 * g + 128 * h + 128]
                                fw.op("dve", lambda e: e.scalar_tensor_tensor(out=dst, in0=ab[:, oc:oc + 128], scalar=small[:, 20 + h:21 + h], in1=dst,
                                                                              op0=ALU.mult, op1=ALU.add), reads=[bab, bsmall, bOf], writes=[bOf])
                    fw.op("act", lambda e: e.copy(out=Ob[:, 0:1024], in_=Of[:]), reads=[bOf], writes=[bOb])

                    ck(10)
                    set_rot([2, 3, 4, 5, 6, 7])
                    for h in range(4):
                        slk = load_slab(WIN_IDX["KR%d" % h])
                        p0, b0 = proj_K(slk, 0, xrhs, 512, [bxb])
                        p1, b1 = proj_K(slk, 1, xrhs, 512, [bxb])
                        rope(p0, b0, p1, b1, rstdB[:], brB, cs, bcs, 512, krT, bkr, 1.0 / 16.0)
                        for r in range(4):
                            pb, bpb = nb()
                            pbv = pb[:, 0:128].bitcast(BF16)
                            for e_ in range(2):
                                tr(pbv[:, 128 * e_:128 * e_ + 128], krT[:, e_, 128 * r:128 * r + 128], [bkr], [bpb])
                            fw.op("act", lambda e: e.activation(out=kz[:, r, :, :], in_=pbv.rearrange("p (e d) -> p e d", e=2), func=AF.Copy,
                                                                scale=zeta[:, 4 * h + r:4 * h + r + 1]), reads=[bpb, bconst], writes=[bkz])
                        slv = load_slab(WIN_IDX["VR%d" % h])
                        for r in range(4):
                            pb, bpb = proj_V(slv, lambda c: xb[:, c, 128 * r:128 * r + 128], [bxb])
                            fw.op("act", lambda e: e.activation(out=vr[:, r, :], in_=pb[:, 0:256], func=AF.Copy, scale=rstdT[:, r:r + 1]),
                                  reads=[bpb, brT], writes=[bvr])
                        slq = load_slab(WIN_IDX["QR%d" % h])
                        p0, b0 = proj_K(slq, 0, xorhs, 128, [bxob])
                        p1, b1 = proj_K(slq, 1, xorhs, 128, [bxob])
                        rope(p0, b0, p1, b1, rstdBo[:], brBo, cso, bcso, 128, qrT, bqr, 1.0)
                        fw.op("dve", lambda e: e.tensor_tensor(out=qrX[:], in0=qrT[:], in1=xi[:, h, None, :].broadcast_to([128, 2, 128]), op=ALU.mult),
                              reads=[bqr, bconst], writes=[bqrx])
                        slg = load_slab(WIN_IDX["GR%d" % h])
                        pb, bpb = proj_V(slg, lambda c: xob[:, c, :], [bxob])
                        fw.op("act", lambda e: e.activation(out=sg[:], in_=pb[:, 0:256], func=AF.Silu, scale=rstdTo[:, 0:1]), reads=[bpb, brTo], writes=[bsg])
                        pb, bpb = nb()
                        for r in range(4):
                            for e_ in range(2):
                                mm(pb[:, 128 * r:128 * r + 128], krT[:, e_, 128 * r:128 * r + 128], qrT[:, e_, :], e_ == 0, e_ == 1, [bkr, bqr], [bpb])
                        fw.op("dve", lambda e: e.tensor_tensor(out=sdT[:], in0=pb[:, :].rearrange("p (r q) -> p r q", r=4), in1=dec[:, h, :, :], op=ALU.mult),
                              reads=[bpb, bconst], writes=[bsd])
                        ab, bab = banks[h % 2], bbank[h % 2]
                        for r in range(4):
                            mm(ab[:, 0:256], sdT[:, r, :], vr[:, r, :], r == 0, False, [bsd, bvr], [bab])
                        for e_ in range(2):
                            mm(ab[:, 0:256], qrX[:, e_, :], Rb[:, h, e_, :], False, e_ == 1, [bqrx, bRb[h]], [bab])
                        fw.op("dve", lambda e: e.tensor_copy(out=gn[:], in_=ab[:, 0:256]), reads=[bab], writes=[bgn])
                        fw.op("dve", lambda e: e.tensor_reduce(out=small[:, 32:33], in_=gn[:], axis=AX.X, op=ALU.add), reads=[bgn], writes=[bsmall])
                        fw.op("act", lambda e: e.activation(out=sdT[:].rearrange("p r q -> p (r q)")[:, 0:256], in_=gn[:], func=AF.Square, accum_out=small[:, 33:34]),
                              reads=[bgn, bsmall], writes=[bsd, bsmall])
                        fw.op("dve", lambda e: e.tensor_scalar(out=small[:, 34:36], in0=small[:, 32:34], scalar1=1.0 / 256, scalar2=None, op0=ALU.mult),
                              reads=[bsmall], writes=[bsmall])
                        fw.op("dve", lambda e: e.tensor_tensor(out=small[:, 36:37], in0=small[:, 34:35], in1=small[:, 34:35], op=ALU.mult), reads=[bsmall], writes=[bsmall])
                        fw.op("dve", lambda e: e.tensor_tensor(out=small[:, 37:38], in0=small[:, 35:36], in1=small[:, 36:37], op=ALU.subtract), reads=[bsmall], writes=[bsmall])
                        fw.op("act", lambda e: e.activation(out=small[:, 38:39], in_=small[:, 37:38], func=AF.Sqrt, bias=epsT[:], scale=1.0), reads=[bsmall, bconst], writes=[bsmall])
                        fw.op("dve", lambda e: e.reciprocal(out=small[:, 39:40], in_=small[:, 38:39]), reads=[bsmall], writes=[bsmall])
                        fw.op("dve", lambda e: e.tensor_scalar(out=gn[:], in0=gn[:], scalar1=small[:, 34:35], scalar2=small[:, 39:40], op0=ALU.subtract, op1=ALU.mult),
                              reads=[bgn, bsmall], writes=[bgn])
                        fw.op("dve", lambda e: e.tensor_tensor(out=Ob[:, 1024 + 256 * h:1280 + 256 * h], in0=gn[:], in1=sg[:], op=ALU.mult), reads=[bgn, bsg], writes=[bOb])
                        for e_ in range(2):
                            pb, bpb = nb()
                            for r in range(4):
                                mm(pb[:, 0:256], kz[:, r, e_, :], vr[:, r, :], r == 0, r == 3, [bkz, bvr], [bpb])
                            fw.op("dve", lambda e: e.scalar_tensor_tensor(out=R[:, h, e_, :], in0=R[:, h, e_, :], scalar=G512[h], in1=pb[:, 0:256],
                                                                          op0=ALU.mult, op1=ALU.add), reads=[bpb], writes=[bR[h]])
                        fw.op("act", lambda e: e.copy(out=Rb[:, h, :, :], in_=R[:, h, :, :]), reads=[bR[h]], writes=[bRb[h]])

                    ck(11)
                    if DEBUG:
                        fw.op("act", lambda e: e.copy(out=dbgt[:], in_=Ob[:]), reads=[bOb], writes=[bdbg])
                        fw.dma("pool", d_out, dbg_o[128 * i:128 * i + 128, :], dbgt[:], reads=[bdbg])
                    for cgp in range(4):
                        pb, bpb = nb()
                        pbv = pb[:, 0:256].bitcast(BF16)
                        for c in range(4):
                            tr(pbv[:, 128 * c:128 * c + 128], Ob[:, 128 * (4 * cgp + c):128 * (4 * cgp + c) + 128], [bOb], [bpb])
                        fw.op("dve", lambda e: e.tensor_copy(out=oT[:, 4 * cgp:4 * cgp + 4, :], in_=pbv.rearrange("p (c t) -> p c t", c=4)), reads=[bpb], writes=[boT])
                    fw.dma("pool", d_oT, oT_scr[i], oT[:].rearrange("p c t -> p (c t)"), reads=[boT], writes=[boTs])
            ca.close()
            fw.barrier()

            with ExitStack() as pbk:
                slabs = [(sb("bslab%d" % k, [128, 4096], BF16, pbk), Buf(), (lambda *a: None)("bslab%d" % k)) for k in range(2)]
                slab_n = [0]

                def load_slab2(sid):
                    t, b, dsm = slabs[slab_n[0] % 2]
                    slab_n[0] += 1
                    fw.dma("sp", dsm, t[:], wscr[sid], reads=bwscr, writes=[b])
                    return t, b

                x1 = sb("x1", [128, 4, DM], F32, pbk); bx1 = [Buf() for _ in range(4)]
                yb = sb("yb", [128, 4, DM], F32, pbk); byb = [Buf() for _ in range(4)]
                hT = sb("hT", [128, 16, 512], BF16, pbk); bhT = Buf()
                uT = sb("uT", [128, 64, 512], BF16, pbk); buT = Buf()
                gB = sb("gB", [128, 2, DM], F32, pbk); bgB = Buf()
                hb = sb("hb", [128, DM], BF16, pbk); bhb = Buf()
                rel_ = sb("rel_", [128, 512], F32, pbk); brel = Buf()
                st = sb("bst", [128, 64], F32, pbk); bst_ = Buf()
                d_x = (lambda *a: None)("bx"); d_g = (lambda *a: None)("bg"); d_o = (lambda *a: None)("bo")
                fw.dma("sp", d_g, gB[:, 0, :], D["gpostB"], writes=[bgB])
                fw.dma("sp", d_g, gB[:, 1, :], D["gpost2B"], writes=[bgB])

                def row_rstd(src_fn, bsrc, col):
                    for q4 in range(4):
                        fw.op("act", lambda e: e.activation(out=rel_[:], in_=src_fn(q4), func=AF.Square, accum_out=st[:, 32 + q4:33 + q4]),
                              reads=[bsrc], writes=[brel, bst_])
                    fw.op("dve", lambda e: e.tensor_reduce(out=st[:, 36:37], in_=st[:, 32:36], axis=AX.X, op=ALU.add), reads=[bst_], writes=[bst_])
                    fw.op("act", lambda e: e.activation(out=st[:, 37:38], in_=st[:, 36:37], func=AF.Sqrt, bias=epsT[:], scale=1.0 / DM), reads=[bst_, bconst], writes=[bst_])
                    fw.op("dve", lambda e: e.reciprocal(out=st[:, col:col + 1], in_=st[:, 37:38]), reads=[bst_], writes=[bst_])

                for cb in range(4 if RUN_B else 0):
                    set_rot(range(8))
                    for s in range(4):
                        fw.dma("sp", d_x, hT[:, :, 128 * s:128 * s + 128], oT_scr[4 * cb + s].rearrange("p (c t) -> p c t", c=16), reads=[boTs], writes=[bhT])
                    for tt in range(4):
                        fw.dma("sp", d_x, x1[:, tt, :], xo[512 * cb + 128 * tt:512 * cb + 128 * tt + 128, :], writes=[bx1[tt]])
                    for cg in range(8):
                        sl, bsl = load_slab2(OFF_WOUT + cg)
                        slv = sl[:].rearrange("p (c n) -> p c n", c=16)
                        for tt in range(4):
                            pb, bpb = nb()
                            for c in range(16):
                                mm(pb[:, 0:256], hT[:, c, 128 * tt:128 * tt + 128], slv[:, c, :], c == 0, c == 15, [bhT, bsl], [bpb])
                            fw.op("act" if tt % 2 else "dve", (lambda e: e.copy(out=yb[:, tt, 256 * cg:256 * cg + 256], in_=pb[:, 0:256])) if tt % 2 else
                                  (lambda e: e.tensor_copy(out=yb[:, tt, 256 * cg:256 * cg + 256], in_=pb[:, 0:256])), reads=[bpb], writes=[byb[tt]])
                    for tt in range(4):
                        row_rstd(lambda q4: yb[:, tt, 512 * q4:512 * q4 + 512], byb[tt], tt)
                        fw.op("dve", lambda e: e.scalar_tensor_tensor(out=yb[:, tt, :], in0=yb[:, tt, :], scalar=st[:, tt:tt + 1], in1=gB[:, 0, :],
                                                                      op0=ALU.mult, op1=ALU.mult), reads=[byb[tt], bst_, bgB], writes=[byb[tt]])
                        fw.op("pool", lambda e: e.tensor_tensor(out=x1[:, tt, :], in0=x1[:, tt, :], in1=yb[:, tt, :], op=ALU.add), reads=[byb[tt], bx1[tt]], writes=[bx1[tt]])
                        if DEBUG:
                            r0 = 512 * cb + 128 * tt
                            fw.dma("pool", d_out, dbg_x1[r0:r0 + 128, :], x1[:, tt, :], reads=[bx1[tt]])
                        row_rstd(lambda q4: x1[:, tt, 512 * q4:512 * q4 + 512], bx1[tt], 4 + tt)
                        fw.op("act", lambda e: e.activation(out=hb[:], in_=x1[:, tt, :], func=AF.Copy, scale=st[:, 4 + tt:5 + tt]), reads=[bx1[tt], bst_], writes=[bhb])
                        for cgp in range(4):
                            pb, bpb = nb()
                            pbv = pb[:, 0:256].bitcast(BF16)
                            for c in range(4):
                                tr(pbv[:, 128 * c:128 * c + 128], hb[:, 128 * (4 * cgp + c):128 * (4 * cgp + c) + 128], [bhb], [bpb])
                            fw.op("dve", lambda e: e.tensor_copy(out=hT[:, 4 * cgp:4 * cgp + 4, 128 * tt:128 * tt + 128], in_=pbv.rearrange("p (c t) -> p c t", c=4)),
                                  reads=[bpb], writes=[bhT])
                    for s in range(32):
                        sl, bsl = load_slab2(OFF_WUP + s)
                        slv = sl[:].rearrange("p (t c n) -> p t c n", t=2, c=16)
                        for ct in range(2):
                            pb, bpb = nb()
                            for c in range(16):
                                mm(pb[:, :], slv[:, ct, c, :], hT[:, c, :], c == 0, c == 15, [bhT, bsl], [bpb])
                            fw.op("act", lambda e: e.activation(out=rel_[:], in_=pb[:, :], func=AF.Relu), reads=[bpb], writes=[brel])
                            fw.op("dve" if ct == 0 else "pool", lambda e: e.tensor_tensor(out=uT[:, 2 * s + ct, :], in0=rel_[:], in1=rel_[:], op=ALU.mult), reads=[brel], writes=[buT])
                    set_rot([4, 5, 6, 7])
                    for cg in range(8):
                        accb = [(banks[k], bbank[k]) for k in range(4)]
                        for hq in range(4):
                            sl, bsl = load_slab2(OFF_WDN + cg * 4 + hq)
                            slv = sl[:].rearrange("p (c n) -> p c n", c=16)
                            for tt in range(4):
                                ab, bab = accb[tt]
                                for c in range(16):
                                    mm(ab[:, 0:256], uT[:, 16 * hq + c, 128 * tt:128 * tt + 128], slv[:, c, :], hq == 0 and c == 0, hq == 3 and c == 15, [buT, bsl], [bab])
                        for tt in range(4):
                            ab, bab = accb[tt]
                            fw.op("act" if tt % 2 else "dve", (lambda e: e.copy(out=yb[:, tt, 256 * cg:256 * cg + 256], in_=ab[:, 0:256])) if tt % 2 else
                                  (lambda e: e.tensor_copy(out=yb[:, tt, 256 * cg:256 * cg + 256], in_=ab[:, 0:256])), reads=[bab], writes=[byb[tt]])
                    for tt in range(4):
                        row_rstd(lambda q4: yb[:, tt, 512 * q4:512 * q4 + 512], byb[tt], 8 + tt)
                        fw.op("dve", lambda e: e.scalar_tensor_tensor(out=yb[:, tt, :], in0=yb[:, tt, :], scalar=st[:, 8 + tt:9 + tt], in1=gB[:, 1, :],
                                                                      op0=ALU.mult, op1=ALU.mult), reads=[byb[tt], bst_, bgB], writes=[byb[tt]])
                        fw.op("pool", lambda e: e.tensor_tensor(out=yb[:, tt, :], in0=yb[:, tt, :], in1=x1[:, tt, :], op=ALU.add), reads=[byb[tt], bx1[tt]], writes=[byb[tt]])
                        r0 = 512 * cb + 128 * tt
                        fw.dma("pool", d_out, out[r0:r0 + 128, :], yb[:, tt, :], reads=[byb[tt]])
        except _Stop:
            ca.close()
        for ssem in fw.store_sems:
            fw.eng["sp"].wait_ge(fw.sems[ssem], fw.cnt[ssem])
        print("bass instructions:", fw.n_inst, {k: v for k, v in fw.cnt.items() if not k.startswith("d_")})
    return nc


_NC_CACHE = {}


def kernel(**inputs):
    x = np.asarray(inputs["x"], np.float32)
    f = lambda k: np.ascontiguousarray(np.asarray(inputs[k], np.float32)[0])
    cosT, sinT = _rope_tables()
    t5 = np.asarray(inputs["t5_bias"], np.float32)
    shared = {
        "w_in": f("w_in"), "w_out": f("w_out"), "w_up": f("w_up"), "w_down": f("w_down"),
        "gpre": np.ascontiguousarray(f("norm_mix_pre").reshape(16, 128).T),
        "gmlp": np.ascontiguousarray(f("norm_mlp_pre").reshape(16, 128).T),
        "gpostB": np.ascontiguousarray(np.broadcast_to(f("norm_mix_post")[None, :], (128, DM))),
        "gpost2B": np.ascontiguousarray(np.broadcast_to(f("norm_mlp_post")[None, :], (128, DM))),
        "w1k": f("cmp_w1_k"), "w1v": f("cmp_w1_v"),
        "peTk": np.ascontiguousarray(f("cmp_pe_k").T), "peTv": np.ascontiguousarray(f("cmp_pe_v").T),
        "b1k": f("cmp_b1_k").reshape(128, 1).copy(), "b1v": f("cmp_b1_v").reshape(128, 1).copy(),
        "w2k": f("cmp_w2_k"), "w2v": f("cmp_w2_v"),
        "cosT": cosT, "sinT": sinT,
    }
    in_maps = []
    own_idx = []
    for core in range(8):
        b, j = divmod(core, 4)
        idx = (np.arange(16)[:, None] * 512 + 128 * j + np.arange(128)[None, :]).reshape(-1)
        own_idx.append((b, idx))
        m = dict(shared)
        m["xT"] = np.ascontiguousarray(x[b].T)
        m["xo"] = np.ascontiguousarray(x[b][idx])
        m["xTo"] = np.ascontiguousarray(m["xo"].T)
        m["cosTo"] = np.ascontiguousarray(cosT[:, idx])
        m["sinTo"] = np.ascontiguousarray(sinT[:, idx])
        m.update(_host_consts(j, t5))
        in_maps.append(m)
    if "nc" not in _NC_CACHE:
        _NC_CACHE["nc"] = build()
    res = run_bass_kernel_spmd(_NC_CACHE["nc"], in_maps, core_ids=list(range(8)))
    outp = np.zeros((2, S, DM), np.float32)
    for core in range(8):
        b, idx = own_idx[core]
        outp[b, idx] = res.results[core]["out"]
    if DEBUG:
        kernel.dbg = [(own_idx[c], res.results[c]["dbg_o"], res.results[c]["dbg_x1"]) for c in range(8)]
    return outp
```
